# Optimizing a Trainium2 kernel written in Bass

```python
import math, functools
import jax, jax.numpy as jnp
from jax import lax
import numpy as np

D_MODEL = 1024
BATCH = 4
SEQ = 4096
DEPTH = 1
DEC_BATCH = 32
DEC_SEQ = 4
PAST_LEN = 8192
PAGE_SIZE = 128

D_CONV = D_MODEL
CONV_W = 3
N_HEADS = 16
HEAD_DIM = 64
N_KV = 4
GROUP = N_HEADS // N_KV
CMP_BLOCK = 32
CMP_STRIDE = 16
SEL_BLOCK = 64
N_SEL = 16
WINDOW = 512
D_PHI = 2 * HEAD_DIM
Q_BLOCK = 128
D_FF = ((8 * D_MODEL // 3 + 255) // 256) * 256
KV_W = N_KV * HEAD_DIM
IN_SIZES = (D_CONV, D_CONV, D_CONV, N_HEADS * HEAD_DIM, KV_W, KV_W, KV_W, KV_W, KV_W, KV_W, 3 * N_HEADS, 2 * D_MODEL)
IN_COLS = 3 * D_CONV + N_HEADS * HEAD_DIM + 6 * KV_W + 3 * N_HEADS + 2 * D_MODEL
EPS = 1e-6
NEG = -1e30
FORCE_BONUS = 1e3

kernel_name = 'hybrid_shortconv_nsa_alibi_adaln_step'


def rmsnorm(x, g):
    xf = x.astype(jnp.float32)
    inv = lax.rsqrt(jnp.mean(xf * xf, axis=-1, keepdims=True) + EPS)
    return (xf * inv).astype(x.dtype) * g


def alibi_slopes():
    s = 2.0 ** (-8.0 * jnp.arange(1, N_HEADS + 1, dtype=jnp.float32) / N_HEADS)
    return s.reshape(N_KV, GROUP)


def split_columns(z):
    parts, start = [], 0
    for size in IN_SIZES:
        parts.append(z[..., start:start + size])
        start += size
    return parts


def short_conv(u, u_prev, w, bias):
    t = u.shape[1]
    up = jnp.concatenate([u_prev, u], axis=1)
    y = bias + sum(w[k] * up[:, k:k + t] for k in range(CONV_W))
    return y, up[:, t:]


def compress(rows, pe, w1, w2):
    b, t, g, d = rows.shape
    nc = t // CMP_STRIDE
    halves = jnp.pad(rows, ((0, 0), (0, CMP_STRIDE), (0, 0), (0, 0))).reshape(b, nc + 1, CMP_STRIDE, g, d)
    blocks = jnp.concatenate([halves[:, :-1], halves[:, 1:]], axis=2) + pe[:, None, :]
    flat = blocks.transpose(0, 1, 3, 2, 4).reshape(b, nc, g, CMP_BLOCK * d)
    return jax.nn.gelu(flat @ w1) @ w2


def compress_kv(rows, pe, w1, w2):
    return (compress(rows[:, :, 0], pe[0], w1[0], w2[0]),
            compress(rows[:, :, 1], pe[1], w1[1], w2[1]))


def attend_block(q, gates, qpos, kc, vc, ks_t, vs_t, kw, vw, kwpos, slopes):
    b, nq = q.shape[:2]
    nc = kc.shape[1]
    ns = ks_t.shape[2]
    tq = qpos.astype(jnp.float32)
    sl = slopes[:, :, None, None]
    ci = jnp.arange(nc, dtype=jnp.int32)
    c_valid = (ci * CMP_STRIDE + CMP_BLOCK - 1)[None, :] <= qpos[:, None]
    c_dist = tq[:, None] - (ci * CMP_STRIDE).astype(jnp.float32)[None, :] - 0.5 * (CMP_BLOCK - 1)
    s_c = jnp.einsum('bqgrd,bcgd->bgrqc', q, kc).astype(jnp.float32) - sl * c_dist
    p_c = jnp.where(c_valid, jax.nn.softmax(jnp.where(c_valid, s_c, NEG), axis=-1), 0.0)
    o_c = jnp.einsum('bgrqc,bcgd->bqgrd', p_c.astype(vc.dtype), vc)
    r4 = p_c.sum(axis=2).reshape(b, N_KV, nq, ns, SEL_BLOCK // CMP_STRIDE)
    imp = r4.sum(-1) + jnp.pad(r4[..., :-1, -1], ((0, 0), (0, 0), (0, 0), (1, 0)))
    si = jnp.arange(ns, dtype=jnp.int32)
    cur = (qpos // SEL_BLOCK)[:, None]
    s_valid = (si * SEL_BLOCK)[None, :] <= qpos[:, None]
    forced = (si[None, :] == 0) | (si[None, :] == cur) | (si[None, :] == cur - 1)
    score = jnp.where(s_valid, imp + jnp.where(forced, FORCE_BONUS, 0.0), NEG)
    _, idx = lax.top_k(score, min(N_SEL, ns))
    bi = jnp.arange(b)[:, None, None, None]
    gi = jnp.arange(N_KV)[None, :, None, None]
    k_sel = ks_t[bi, gi, idx]
    v_sel = vs_t[bi, gi, idx]
    spos = idx[..., None] * SEL_BLOCK + jnp.arange(SEL_BLOCK, dtype=jnp.int32)
    tok_valid = spos <= qpos[None, None, :, None, None]
    s_dist = tq[None, None, :, None, None] - spos.astype(jnp.float32)
    s_s = jnp.einsum('bqgrd,bgqnkd->bgrqnk', q, k_sel).astype(jnp.float32) - slopes[None, :, :, None, None, None] * s_dist[:, :, None]
    p_s = jax.nn.softmax(jnp.where(tok_valid[:, :, None], s_s, NEG), axis=(-2, -1))
    o_s = jnp.einsum('bgrqnk,bgqnkd->bqgrd', p_s.astype(v_sel.dtype), v_sel)
    w_valid = (kwpos[None, :] <= qpos[:, None]) & (qpos[:, None] - kwpos[None, :] < WINDOW) & (kwpos[None, :] >= 0)
    w_dist = tq[:, None] - kwpos.astype(jnp.float32)[None, :]
    s_w = jnp.einsum('bqgrd,bkgd->bgrqk', q, kw).astype(jnp.float32) - sl * w_dist
    p_w = jax.nn.softmax(jnp.where(w_valid, s_w, NEG), axis=-1)
    o_w = jnp.einsum('bgrqk,bkgd->bqgrd', p_w.astype(vw.dtype), vw)
    o = gates[..., 0:1] * o_c + gates[..., 1:2] * o_s + gates[..., 2:3] * o_w
    return o.reshape(b, nq, N_HEADS * HEAD_DIM)


def nsa_prompt(q, gates, kv_cmp, kv_sel, kv_win, pe, w1, w2, slopes):
    b, t = q.shape[:2]
    kc, vc = compress_kv(kv_cmp, pe, w1, w2)
    sel_t = kv_sel.reshape(b, t // SEL_BLOCK, SEL_BLOCK, 2, N_KV, HEAD_DIM).transpose(3, 0, 4, 1, 2, 5)
    kw_pad = jnp.pad(kv_win, ((0, 0), (WINDOW, 0), (0, 0), (0, 0), (0, 0)))

    def one(i):
        q0 = i * Q_BLOCK
        qb = lax.dynamic_slice_in_dim(q, q0, Q_BLOCK, axis=1)
        gb = lax.dynamic_slice_in_dim(gates, q0, Q_BLOCK, axis=1)
        wb = lax.dynamic_slice_in_dim(kw_pad, q0, WINDOW + Q_BLOCK, axis=1)
        qpos = q0 + jnp.arange(Q_BLOCK, dtype=jnp.int32)
        kwpos = q0 - WINDOW + jnp.arange(WINDOW + Q_BLOCK, dtype=jnp.int32)
        return attend_block(qb, gb, qpos, kc, vc, sel_t[0], sel_t[1], wb[:, :, 0], wb[:, :, 1], kwpos, slopes)

    o = lax.map(one, jnp.arange(t // Q_BLOCK, dtype=jnp.int32))
    o = o.transpose(1, 0, 2, 3).reshape(b, t, N_HEADS * HEAD_DIM)
    keep = min(WINDOW, t)
    return o, (kv_cmp, kv_sel, kv_win[:, t - keep:])


def nsa_sample(q, gates, kv_cmp, kv_sel, kv_win, cache_cmp, cache_sel, cache_win, page_table, pe, w1, w2, slopes):
    b, s = q.shape[:2]
    past = page_table.shape[1] * cache_cmp.shape[1]
    t_all = past + s
    t_pad = -(-t_all // SEL_BLOCK) * SEL_BLOCK

    def full_rows(cache, new):
        rows = cache[page_table].reshape(b, past, 2, N_KV, HEAD_DIM)
        rows = jnp.concatenate([rows, new], axis=1)
        return jnp.pad(rows, ((0, 0), (0, t_pad - t_all), (0, 0), (0, 0), (0, 0)))

    kc, vc = compress_kv(full_rows(cache_cmp, kv_cmp), pe, w1, w2)
    sel_t = full_rows(cache_sel, kv_sel).reshape(b, t_pad // SEL_BLOCK, SEL_BLOCK, 2, N_KV, HEAD_DIM).transpose(3, 0, 4, 1, 2, 5)
    wbuf = cache_win.shape[1]
    win = jnp.concatenate([cache_win, kv_win], axis=1)
    qpos = past + jnp.arange(s, dtype=jnp.int32)
    kwpos = past - wbuf + jnp.arange(wbuf + s, dtype=jnp.int32)
    o = attend_block(q, gates, qpos, kc, vc, sel_t[0], sel_t[1], win[:, :, 0], win[:, :, 1], kwpos, slopes)
    return o, (kv_cmp, kv_sel, win[:, s:])


def layer(x, c, conv_prev, attend, w):
    b, t = x.shape[:2]
    mod = jax.nn.silu(c) @ w['w_ada'] + w['b_ada']
    sh1, sc1, g1, sh2, sc2, g2 = jnp.split(mod[:, None, :], 6, axis=-1)
    h = rmsnorm(x, w['norm1']) * (1 + sc1) + sh1
    bg, cg, xin, q, kc, vc, ks, vs, kw, vw, nsa_g, merge_g = split_columns(h @ w['w_in'])
    v, conv_new = short_conv(cg * xin, conv_prev, w['w_conv'], w['b_conv'])
    y_a = (bg * v) @ w['w_out_conv']
    qh = q.reshape(b, t, N_KV, GROUP, HEAD_DIM) * (HEAD_DIM ** -0.5)
    kvh = lambda k_, v_: jnp.stack([k_.reshape(b, t, N_KV, HEAD_DIM), v_.reshape(b, t, N_KV, HEAD_DIM)], axis=2)
    gates = jax.nn.sigmoid(nsa_g.reshape(b, t, N_KV, GROUP, 3))
    o_b, nsa_state = attend(qh, gates, kvh(kc, vc), kvh(ks, vs), kvh(kw, vw))
    y_b = o_b @ w['w_o_nsa']
    ga, gb = jnp.split(jax.nn.sigmoid(merge_g), 2, axis=-1)
    x = x + g1 * ((ga * y_a + gb * y_b) @ w['w_out'])
    h2 = rmsnorm(x, w['norm2']) * (1 + sc2) + sh2
    x = x + g2 * ((jax.nn.silu(h2 @ w['w_gate']) * (h2 @ w['w_up'])) @ w['w_down'])
    return x, nsa_state, conv_new


def setup_inputs(seed: int = 0) -> dict:
    key = jax.random.key(seed)
    k = jax.random.split(key, 28)
    n_pages = PAST_LEN // PAGE_SIZE
    n_used = DEC_BATCH * n_pages
    n_phys = n_used + (n_used + 3) // 4
    win_buf = min(WINDOW, PAST_LEN)
    kv_row = (2, N_KV, HEAD_DIM)

    def nrm(kk, shape, scale=1.0):
        return jax.random.normal(kk, shape, jnp.float32) * scale

    return {
        'x_prompt': nrm(k[0], (BATCH, SEQ, D_MODEL)),
        'x_sample': nrm(k[1], (DEC_BATCH, DEC_SEQ, D_MODEL)),
        'c_prompt': nrm(k[2], (BATCH, D_MODEL)),
        'c_sample': nrm(k[3], (DEC_BATCH, D_MODEL)),
        'cache_cmp': nrm(k[4], (DEPTH, n_phys, PAGE_SIZE) + kv_row),
        'cache_sel': nrm(k[5], (DEPTH, n_phys, PAGE_SIZE) + kv_row),
        'cache_win': nrm(k[6], (DEPTH, DEC_BATCH, win_buf) + kv_row),
        'state_conv': nrm(k[7], (DEPTH, DEC_BATCH, CONV_W - 1, D_CONV)),
        'page_table': jax.random.permutation(k[8], n_phys)[:n_used].reshape(DEC_BATCH, n_pages).astype(jnp.int32),
        'w_ada': nrm(k[9], (DEPTH, D_MODEL, 6 * D_MODEL), 0.5 * D_MODEL ** -0.5),
        'b_ada': nrm(k[10], (DEPTH, 6 * D_MODEL), 0.01),
        'norm1': 1.0 + nrm(k[11], (DEPTH, D_MODEL), 0.02),
        'w_in': nrm(k[12], (DEPTH, D_MODEL, IN_COLS), D_MODEL ** -0.5),
        'w_conv': nrm(k[13], (DEPTH, CONV_W, D_CONV), CONV_W ** -0.5),
        'b_conv': nrm(k[14], (DEPTH, D_CONV), 0.01),
        'w_out_conv': nrm(k[15], (DEPTH, D_CONV, D_MODEL), D_CONV ** -0.5),
        'pe_cmp': nrm(k[16], (DEPTH, 2, CMP_BLOCK, HEAD_DIM), 0.02),
        'w_phi1': nrm(k[17], (DEPTH, 2, CMP_BLOCK * HEAD_DIM, D_PHI), (CMP_BLOCK * HEAD_DIM) ** -0.5),
        'w_phi2': nrm(k[18], (DEPTH, 2, D_PHI, HEAD_DIM), D_PHI ** -0.5),
        'w_o_nsa': nrm(k[19], (DEPTH, N_HEADS * HEAD_DIM, D_MODEL), (N_HEADS * HEAD_DIM) ** -0.5),
        'w_out': nrm(k[20], (DEPTH, D_MODEL, D_MODEL), D_MODEL ** -0.5),
        'norm2': 1.0 + nrm(k[21], (DEPTH, D_MODEL), 0.02),
        'w_gate': nrm(k[22], (DEPTH, D_MODEL, D_FF), D_MODEL ** -0.5),
        'w_up': nrm(k[23], (DEPTH, D_MODEL, D_FF), D_MODEL ** -0.5),
        'w_down': nrm(k[24], (DEPTH, D_FF, D_MODEL), D_FF ** -0.5),
        'norm_f': 1.0 + nrm(k[25], (D_MODEL,), 0.02),
    }


def reference(x_prompt, x_sample, c_prompt, c_sample, cache_cmp, cache_sel, cache_win, state_conv, page_table,
              w_ada, b_ada, norm1, w_in, w_conv, b_conv, w_out_conv, pe_cmp, w_phi1, w_phi2, w_o_nsa, w_out,
              norm2, w_gate, w_up, w_down, norm_f):
    slopes = alibi_slopes()
    xp, xs = x_prompt, x_sample
    conv0 = jnp.zeros((xp.shape[0], CONV_W - 1, D_CONV), xp.dtype)
    st_p, st_s = [], []
    for l in range(DEPTH):
        w = dict(w_ada=w_ada[l], b_ada=b_ada[l], norm1=norm1[l], w_in=w_in[l], w_conv=w_conv[l], b_conv=b_conv[l],
                 w_out_conv=w_out_conv[l], w_o_nsa=w_o_nsa[l], w_out=w_out[l], norm2=norm2[l],
                 w_gate=w_gate[l], w_up=w_up[l], w_down=w_down[l])
        att_p = functools.partial(nsa_prompt, pe=pe_cmp[l], w1=w_phi1[l], w2=w_phi2[l], slopes=slopes)
        xp, nsa_p, conv_p = layer(xp, c_prompt, conv0, att_p, w)
        att_s = functools.partial(nsa_sample, cache_cmp=cache_cmp[l], cache_sel=cache_sel[l], cache_win=cache_win[l],
                                  page_table=page_table, pe=pe_cmp[l], w1=w_phi1[l], w2=w_phi2[l], slopes=slopes)
        xs, nsa_s, conv_s = layer(xs, c_sample, state_conv[l], att_s, w)
        st_p.append(nsa_p + (conv_p,))
        st_s.append(nsa_s + (conv_s,))
    y_prompt = rmsnorm(xp, norm_f)
    y_sample = rmsnorm(xs, norm_f)
    new_cmp_prompt = jnp.stack([s_[0] for s_ in st_p])
    new_sel_prompt = jnp.stack([s_[1] for s_ in st_p])
    new_win_prompt = jnp.stack([s_[2] for s_ in st_p])
    new_conv_prompt = jnp.stack([s_[3] for s_ in st_p])
    new_cmp_sample = jnp.stack([s_[0] for s_ in st_s])
    new_sel_sample = jnp.stack([s_[1] for s_ in st_s])
    new_win_sample = jnp.stack([s_[2] for s_ in st_s])
    new_conv_sample = jnp.stack([s_[3] for s_ in st_s])
    return (y_prompt, y_sample, new_cmp_prompt, new_sel_prompt, new_win_prompt, new_conv_prompt,
            new_cmp_sample, new_sel_sample, new_win_sample, new_conv_sample)
```

```python
import numpy as np
import ml_dtypes
from contextlib import ExitStack
import concourse.bass as bass
import concourse.mybir as mybir
from concourse.bass_utils import run_bass_kernel_spmd

F32 = mybir.dt.float32
BF16 = mybir.dt.bfloat16
I32 = mybir.dt.int32
AF = mybir.ActivationFunctionType
ALU = mybir.AluOpType
AX = mybir.AxisListType
NPBF = ml_dtypes.bfloat16

D = 1024
T = 4096
NH = 16
HD = 64
DFF = 2816
NFC = DFF // 128
BIG = 30000.0
EPS = 1e-6
SLOPES = [2.0 ** (-(h + 1) / 2.0) for h in range(16)]
SUBS = ((0, 3), (1, 2))
PAST = 8192
NPG = 64
NPHYS = 2560

CFG = dict(units=8, sample=True, attn=True)


class Eng:
    def __init__(self, name):
        self.name = name
        self.q = []
        self.cnt = 0
        self.sid = None
        self.waited = {}
        self.dsids = []
        self.di = 0


class Res:
    __slots__ = ("w", "r")

    def __init__(self):
        self.w = {}
        self.r = {}


class Sched:
    def __init__(self, nc, es):
        self.nc = nc
        self.sems = []
        self.dry = False
        self.engs = {}
        for n in ("pe", "act", "dve", "pool", "sp"):
            e = Eng(n)
            e.sid = self.newsem(es, "c_" + n)
            self.engs[n] = e
        for n, k in (("sp", 12), ("pool", 12), ("act", 4)):
            self.engs[n].dsids = [self.newsem(es, "d_%s%d" % (n, i)) for i in range(k)]

    def newsem(self, es, name):
        s = es.enter_context(self.nc.semaphore(name))
        self.sems.append(s)
        return len(self.sems) - 1

    def _wait(self, eng, need):
        for sid, v in need.items():
            if eng.waited.get(sid, 0) < v:
                eng.waited[sid] = v
                sem = self.sems[sid]
                eng.q.append(lambda e, sem=sem, v=v: e.wait_ge(sem, v))

    def op(self, en, fn, reads=(), writes=(), signal=True, dma=False):
        if self.dry:
            return
        eng = self.engs[en]
        need = {}

        def add(tok):
            if need.get(tok[0], 0) < tok[1]:
                need[tok[0]] = tok[1]

        me = en + (":dma" if dma else "")
        for r in reads:
            for tok in r.w.values():
                add(tok)
        for w in writes:
            for tok in w.w.values():
                if en != "pe" or dma or tok[2] != me:
                    add(tok)
            for tok in w.r.values():
                if en != "pe" or dma or tok[2] != me:
                    add(tok)
        if dma:
            k = eng.di % len(eng.dsids)
            sid = eng.dsids[k]
            val = 16 * (eng.di // len(eng.dsids) + 1)
            eng.di += 1
            if val > 16:
                add((sid, val - 16, me))
            tok = (sid, val, me)
            self._wait(eng, need)
            sem = self.sems[sid]
            eng.q.append(lambda e, fn=fn, sem=sem: fn(e).then_inc(sem, 16))
        else:
            self._wait(eng, need)
            if signal:
                eng.cnt += 1
                tok = (eng.sid, eng.cnt, me)
                sem = self.sems[eng.sid]
                eng.q.append(lambda e, fn=fn, sem=sem: fn(e).then_inc(sem, 1))
            else:
                tok = (eng.sid, eng.cnt + 1, me)
                eng.q.append(lambda e, fn=fn: fn(e))
        for r in reads:
            r.r[tok[0]] = tok
        for w in writes:
            w.w = {tok[0]: tok}
            w.r = {}
        return tok

    def barrier(self):
        if self.dry:
            return
        need = {}
        for e in self.engs.values():
            if e.cnt:
                need[e.sid] = e.cnt
            for k, sid in enumerate(e.dsids):
                n = (e.di - k + len(e.dsids) - 1) // len(e.dsids) if e.di > k else 0
                if n:
                    need[sid] = 16 * n
        for e in self.engs.values():
            self._wait(e, dict(need))

    def finish(self):
        self.barrier()
        nc = self.nc
        q = self.engs
        with nc.Block() as block:
            @block.tensor
            def _(e):
                for f in q["pe"].q:
                    f(e)

            @block.scalar
            def _(e):
                for f in q["act"].q:
                    f(e)

            @block.vector
            def _(e):
                for f in q["dve"].q:
                    f(e)

            @block.gpsimd
            def _(e):
                for f in q["pool"].q:
                    f(e)

            @block.sync
            def _(e):
                for f in q["sp"].q:
                    f(e)


class Ring:
    def __init__(self, aps):
        self.aps = aps
        self.res = [Res() for _ in aps]
        self.i = 0

    def next(self):
        k = self.i % len(self.aps)
        self.i += 1
        return self.aps[k], self.res[k]


class Prog:
    def __init__(self):
        self.nc = bass.Bass("TRN2", target_bir_lowering=False)
        self.es = ExitStack()
        self.S = Sched(self.nc, self.es)
        self.slab_reqs = []
        self.slab_i = 0
        self.slab_issued = 0
        self.uid = 0

    def din(self, name, shape, dt=F32):
        return self.nc.dram_tensor(name, list(shape), dt, kind="ExternalInput").ap()

    def dout(self, name, shape, dt=F32):
        return self.nc.dram_tensor(name, list(shape), dt, kind="ExternalOutput").ap()

    def sb(self, es, name, shape, dt):
        self.uid += 1
        return es.enter_context(self.nc.sbuf_tensor("%s_%d" % (name, self.uid), list(shape), dt))

    def ps(self, es, name, shape, dt):
        self.uid += 1
        return es.enter_context(self.nc.psum_tensor("%s_%d" % (name, self.uid), list(shape), dt))

    def op(self, *a, **k):
        return self.S.op(*a, **k)

    def dma(self, q, out, in_, reads=(), writes=(), **kw):
        self.S.op(q, lambda e, out=out, in_=in_, kw=kw: e.dma_start(out=out, in_=in_, **kw), reads, writes, dma=True)

    def mm(self, out, lhsT, rhs, start, stop, reads=(), writes=(), signal=None, sgc=False):
        if signal is None:
            signal = stop
        self.S.op("pe", lambda e, out=out, lhsT=lhsT, rhs=rhs, start=start, stop=stop, sgc=sgc:
                  e.matmul(out, lhsT=lhsT, rhs=rhs, start=start, stop=stop, skip_group_check=sgc), reads, writes, signal=signal)

    def tr(self, out, in_, ident, reads=(), writes=(), signal=True):
        self.S.op("pe", lambda e, out=out, in_=in_, ident=ident: e.transpose(out, in_, ident), reads, writes, signal=signal)

    def act(self, out, in_, func, reads=(), writes=(), eng="act", **kw):
        self.S.op(eng, lambda e, out=out, in_=in_, func=func, kw=kw: e.activation(out=out, in_=in_, func=func, **kw), reads, writes)

    def tt(self, eng, out, in0, in1, op, reads=(), writes=()):
        self.S.op(eng, lambda e, out=out, in0=in0, in1=in1, op=op: e.tensor_tensor(out=out, in0=in0, in1=in1, op=op), reads, writes)

    def ts(self, eng, out, in0, s1, op0, s2=None, op1=None, reads=(), writes=(), accum=None):
        def f(e, out=out, in0=in0, s1=s1, op0=op0, s2=s2, op1=op1, accum=accum):
            kw = {}
            if op1 is not None:
                kw["op1"] = op1
            if accum is not None:
                kw["accum_out"] = accum
            return e.tensor_scalar(out=out, in0=in0, scalar1=s1, scalar2=s2, op0=op0, **kw)
        self.S.op(eng, f, reads, writes)

    def stt(self, out, in0, scalar, in1, op0, op1, reads=(), writes=()):
        self.S.op("dve", lambda e, out=out, in0=in0, scalar=scalar, in1=in1, op0=op0, op1=op1:
                  e.scalar_tensor_tensor(out=out, in0=in0, scalar=scalar, in1=in1, op0=op0, op1=op1), reads, writes)

    def cp(self, eng, out, in_, reads=(), writes=()):
        if eng == "act":
            self.S.op("act", lambda e, out=out, in_=in_: e.copy(out=out, in_=in_), reads, writes)
        else:
            self.S.op(eng, lambda e, out=out, in_=in_: e.tensor_copy(out=out, in_=in_), reads, writes)

    def memset(self, eng, ap, val, writes=()):
        self.S.op(eng, lambda e, ap=ap, val=val: e.memset(ap, val), (), writes)

    def slab(self, src, nk, ncols):
        if self.S.dry:
            self.slab_reqs.append((src, nk, ncols))
            return None, None
        k = self.slab_i
        self.slab_i += 1
        nb = len(self.slabs)
        while self.slab_issued < min(len(self.slab_reqs), k + nb):
            i = self.slab_issued
            s_, nk_, nc_ = self.slab_reqs[i]
            buf = self.slabs[i % nb]
            dst = buf[:, 0:nk_ * nc_].rearrange("p (k c) -> p k c", c=nc_)
            q = "sp"
            self.dma(q, dst, s_.rearrange("(k p) c -> p k c", p=128), reads=(), writes=(self.slab_res[i % nb],))
            self.slab_issued += 1
        buf = self.slabs[k % nb]
        return buf[:, 0:nk * ncols].rearrange("p (k c) -> p k c", c=ncols), self.slab_res[k % nb]


FM1_0 = 0
Q_0 = 3072
KD_0 = 4096
KV_0 = 6144
G_0 = 7680
NCX = 7744
ND2 = 4096


def build_program(cfg):
    P = Prog()
    nc = P.nc
    es = P.es
    S = P.S
    NU = cfg["units"]

    xall = P.din("xall", [T, D])
    xown = P.din("xown", [8, 260, D])
    ctok = P.din("ctok", [2, 128, D])
    w_inx = P.din("w_inx", [D, NCX])
    w_d2 = P.din("w_d2", [D, ND2])
    w_ada = P.din("w_ada", [D, 6 * D])
    b_ada = P.din("b_ada", [1, 6 * D])
    norms = P.din("norms", [3, D])
    w_out = P.din("w_out", [D, D])
    w_gu = P.din("w_gu", [D, 2 * DFF])
    w_down = P.din("w_down", [DFF, D])
    w_phi1 = P.din("w_phi1", [2 * 2048, 128])
    w_phi2 = P.din("w_phi2", [2, 128, 64])
    pe_cmp = P.din("pe_cmp", [2, 32, 64])
    wcv = P.din("wcv", [128, 8, 4])
    t_ident = P.din("t_ident", [128, 128])
    t_E = P.din("t_E", [64, T], BF16)
    t_f32 = P.din("t_f32", [128, TF_N])
    t_bf = P.din("t_bf", [128, TB_N], BF16)
    yown = P.dout("yown", [8, 256, D])
    okv = [P.dout("okv%d" % i, [T, 512]) for i in range(2)]
    owin = P.dout("owin", [512, 512])
    oconv = P.dout("oconv", [2, D])
    if cfg.get("sample", True):
        xs_d = P.din("xs", [16, D])
        sconv_d = P.din("sconv", [4, 2, D])
        cwin_d = P.din("cwin", [4, 512, 512])
        ptab_d = P.din("ptab", [4, NPG], I32)
        ccmp_d = P.din("ccmp", [NPHYS * 128, 512])
        csel_d = P.din("csel", [NPHYS * 128, 512])
        ts_f32 = P.din("ts_f32", [128, TSF_N])
        ts_bf = P.din("ts_bf", [128, TSB_N], BF16)
        ys_d = P.dout("ys", [16, D])
        oskv = [P.dout("oskv%d" % i, [16, 512]) for i in range(2)]
        owin_s = P.dout("owin_s", [4, 512, 512])
        oconv_s = P.dout("oconv_s", [4, 2, D])
        if cfg.get("dbg"):
            dbg_d = P.dout("dbg", [4, 3, 4, D])
            dbg_p = P.dout("dbg_p", [64, 512])
            dbg_v = P.dout("dbg_v", [128, 4 * 4 * 64])
            dbg_pt = P.dout("dbg_pt", [128, 256])
    def scr(name, shape):
        return nc.dram_tensor(name, list(shape), BF16).ap()
    s_inx = scr("s_inx", [D, NCX])
    s_d2 = scr("s_d2", [D, ND2])
    s_ada = scr("s_ada", [D, 6 * D])
    s_out = scr("s_out", [D, D])
    s_gu = scr("s_gu", [D, 2 * DFF])
    s_down = scr("s_down", [DFF, D])
    s_phi1 = scr("s_phi1", [2 * 2048, 128])

    def sb(name, shape, dt):
        return P.sb(es, name, shape, dt)
    ident = sb("ident", [128, 128], BF16)
    TF = sb("TF", [128, TF_N], F32)
    TB = sb("TB", [128, TB_N], BF16)
    MOD = sb("MOD", [128, 6 * D], F32)
    KE = VS = KW = VW = KcT = VcT = Vc = CONVO = None
    R2 = sb("R2", [128, 8, 562], BF16)
    W2 = sb("W2", [128, 2, 64], BF16)
    PET = sb("PET", [128, 2, 16], BF16)
    PEB = sb("PEB", [128, 2], F32)
    WCV = sb("WCV", [128, 8, 4], F32)
    NSL = 2
    P.slabs = [sb("slab%d" % i, [128, 4096], BF16) for i in range(NSL)]
    P.slab_res = [Res() for _ in range(NSL)]
    xk = Ring([sb("xk%d" % i, [128, D], F32) for i in range(2)])
    hb = Ring([sb("hb%d" % i, [128, D], BF16) for i in range(2)])
    sm = Ring([sb("sm%d" % i, [128, 8], F32) for i in range(6)])
    PD = Ring([P.ps(es, "PD%d" % i, [128, 512], F32)[:] for i in range(3)])
    PTRt = P.ps(es, "PTR", [128, 1024], BF16)
    PTR = Ring([PTRt[:]])
    PSt = [P.ps(es, "PS%d" % i, [128, 512], F32) for i in range(2)]
    PS = Ring([PSt[0][:, 0:256], PSt[1][:, 0:256]])
    POt = [P.ps(es, "PO%d" % i, [128, 512], F32) for i in range(2)]
    PO = Ring([POt[0][:, 0:130], POt[1][:, 0:130]])
    r_ident, r_TF, r_TB, r_MOD, r_NFR = Res(), Res(), Res(), Res(), Res()
    r_KE = [Res() for _ in range(4)]
    r_VS, r_KW, r_VW, r_Kc, r_VcT, r_Vc, r_R2, r_W2, r_PEB, r_WCV, r_CONVO = (Res() for _ in range(11))

    def tf(name):
        o, n = TF_OFF[name]
        return TF[:, o:o + n]

    def tb(name):
        o, n = TB_OFF[name]
        return TB[:, o:o + n]
    QPOS = tf("QPOS").rearrange("p (m a) -> p m a", a=2)
    CI31 = tf("CI31")
    CPOS = tf("CPOS")
    QSB = tf("QSB").rearrange("p (m a h) -> p m a h", a=2, h=16)
    FBT = tb("FBT").rearrange("p (u j) -> p u j", j=64)
    POSQ = tf("POSQ")
    ALB = tf("ALB").rearrange("p (h r) -> p h r", r=36)
    HF = tf("HF").rearrange("p (m k) -> p m k", k=4)
    DT = tb("DT").rearrange("p (i q) -> p i q", q=128)
    WT = tb("WT").rearrange("p (i q) -> p i q", q=128)

    MSEC = dict(SH1=0, M1=1, G1=2, SH2=3, M2=4, G2=5)

    def msec(name):
        k = MSEC[name]
        return MOD[:, k * D:(k + 1) * D]

    def precast(src, dst, rows, cols):
        for r0 in range(0, rows, 128):
            for c0 in range(0, cols, 4096):
                cw = min(4096, cols - c0)
                k = P.slab_i
                P.slab_i += 1
                buf, rr = P.slabs[k % NSL], P.slab_res[k % NSL]
                P.dma("pool", buf[:, 0:cw], src[r0:r0 + 128, c0:c0 + cw], writes=(rr,), max_dma_last_dim=4096)
                P.dma("sp", dst[r0:r0 + 128, c0:c0 + cw], buf[:, 0:cw], reads=(rr,))

    def rstd_of(xt, xr, n):
        junk, jr = hb.next()
        s1, s1r = sm.next()
        P.act(junk[0:n, :], xt, AF.Square, reads=(xr,), writes=(jr, s1r), accum_out=s1[0:n, 0:1])
        P.act(s1[0:n, 1:2], s1[0:n, 0:1], AF.Ln, reads=(s1r,), writes=(s1r,), scale=1.0 / D, bias=TF[0:n, TF_OFF["EPS"][0]:TF_OFF["EPS"][0] + 1])
        P.act(s1[0:n, 2:3], s1[0:n, 1:2], AF.Exp, reads=(s1r,), writes=(s1r,), scale=-0.5)
        return s1[0:n, 2:3], s1r

    def norm_mod(xt, xr, n, Mn, SHn, dst, col):
        rs, rsr = rstd_of(xt, xr, n)
        tmp, tr_ = xk.next()
        P.stt(tmp[0:n, :], xt, rs, msec(Mn)[0:n, :], ALU.mult, ALU.mult, reads=(xr, rsr, r_MOD), writes=(tr_,))
        h, hr = hb.next()
        P.tt("pool", h[0:n, :], tmp[0:n, :], msec(SHn)[0:n, :], ALU.add, reads=(tr_, r_MOD), writes=(hr,))
        dstap, dstres = dst
        pt, ptr_ = PTR.next()
        for kc in range(8):
            P.tr(pt[:, kc * 128:kc * 128 + n], h[0:n, kc * 128:(kc + 1) * 128], ident[0:n, 0:n],
                 reads=(hr, r_ident), writes=(ptr_,), signal=(kc == 7))
        ptv = pt.rearrange("p (j t) -> p j t", t=128)
        P.nm_i = getattr(P, "nm_i", 0) + 1
        P.cp("act" if P.nm_i % 2 == 0 else "dve", dstap[:, 0:8, col:col + n], ptv[:, 0:8, 0:n], reads=(ptr_,), writes=(dstres,))

    def proj(out_ps, out_res, lhs_list, rhs_list, reads):
        n = len(lhs_list)
        for k in range(n):
            P.mm(out_ps, lhs_list[k], rhs_list[k], start=(k == 0), stop=(k == n - 1), reads=reads, writes=(out_res,))

    def sigmoid_from(out, src, shape_cols, reads, writes, tmpring):
        P.act(out, src, AF.Exp, reads=reads, writes=writes, scale=-1.0)
        P.ts("dve", out, out, 1.0, ALU.add, reads=writes, writes=writes)
        P.op("dve", lambda e, out=out: e.reciprocal(out=out, in_=out), reads=writes, writes=writes)

    P.dma("pool", ident[:], t_ident, writes=(r_ident,))
    P.dma("sp", TF[:], t_f32, writes=(r_TF,))
    P.dma("sp", TB[:], t_bf, writes=(r_TB,))
    P.dma("sp", WCV[:], wcv, writes=(r_WCV,))
    P.dma("pool", W2[:], w_phi2.rearrange("k p d -> p k d"), writes=(r_W2,))
    with nc.allow_non_contiguous_dma(reason="tiny pe table"):
        P.dma("pool", PET[:], pe_cmp.rearrange("k (j s) d -> (s d) k j", s=2), writes=(r_TB,), allow_slow_non_contiguous=True)
    P.memset("pool", R2[:], 0.0, writes=(r_R2,))
    if not S.dry:
        precast(w_phi1, s_phi1, 4096, 128)
        precast(w_ada, s_ada, D, 6 * D)
        precast(w_inx, s_inx, D, NCX)
        precast(w_d2, s_d2, D, ND2)
        precast(w_out, s_out, D, D)
        precast(w_gu, s_gu, D, 2 * DFF)
        precast(w_down, s_down, DFF, D)
        S.barrier()
        P.slab_i = 0
        P.slab_res = [Res() for _ in range(NSL)]

    def compute_mod(kind):
        with ExitStack() as ph:
            cT = P.sb(ph, "cT", [128, 8, 128], BF16)
            bt = Ring([P.sb(ph, "bt%d" % i, [128, 512], F32) for i in range(2)])
            r_cT = Res()
            ct, cr = xk.next()
            P.dma("pool", ct[:], ctok[kind], writes=(cr,))
            e1, e1r = xk.next()
            sigmoid_from(e1[:], ct[:], None, (cr,), (e1r,), None)
            sl, slr = hb.next()
            P.tt("dve", sl[:], ct[:], e1[:], ALU.mult, reads=(cr, e1r), writes=(slr,))
            mstop = cfg.get("mstop", 9)
            if mstop < 2:
                S.barrier()
                return
            pt, ptr_ = PTR.next()
            for kc in range(8):
                P.tr(pt[:, kc * 128:(kc + 1) * 128], sl[:, kc * 128:(kc + 1) * 128], ident[:], reads=(slr, r_ident), writes=(ptr_,), signal=(kc == 7))
            P.cp("act", cT[:], pt.rearrange("p (j t) -> p j t", t=128), reads=(ptr_,), writes=(r_cT,))
            if mstop < 3:
                S.barrier()
                return
            for n in range(12 if mstop >= 4 else 1):
                w, wr = P.slab(s_ada[:, n * 512:(n + 1) * 512], 8, 512)
                b_, br = bt.next()
                P.dma("pool", b_[:], b_ada[0:1, n * 512:(n + 1) * 512].to_broadcast((128, 512)), writes=(br,))
                if S.dry:
                    continue
                pd, pdr = PD.next()
                proj(pd, pdr, [cT[:, k, :] for k in range(8)], [w[:, k, :] for k in range(8)], (r_cT, wr))
                P.tt("dve", MOD[:, n * 512:(n + 1) * 512], pd, b_[:], ALU.add, reads=(pdr, br), writes=(r_MOD,))
            if mstop < 5:
                S.barrier()
                return
            for sec, nrow in ((1, 0), (4, 1)):
                nr, nrr = xk.next()
                P.dma("pool", nr[:], norms[nrow:nrow + 1, :].to_broadcast((128, D)), writes=(nrr,))
                P.stt(MOD[:, sec * D:(sec + 1) * D], MOD[:, sec * D:(sec + 1) * D], 1.0, nr[:], ALU.add, ALU.mult,
                      reads=(r_MOD, nrr), writes=(r_MOD,))
            S.barrier()

    def compress_block(sb_, KcTd, rKc, VcTd, rVc, XG, TG, GG, r_XG, r_TG, r_GG):
        pc, pcr = PD.next()
        for kv in range(2):
            w1, w1r = P.slab(s_phi1[kv * 2048:(kv + 1) * 2048, :], 16, 128)
            if S.dry:
                continue
            for g in range(4):
                gi = kv * 4 + g
                for j in range(16):
                    P.mm(pc[:, gi * 34:(gi + 1) * 34], w1[:, j, :], R2[:, gi, 2 * j:2 * j + 16 * 33 + 1:16],
                         start=(j == 0), stop=(j == 15), reads=(w1r, r_R2), writes=(pcr,))
            if not hasattr(P, "peb_done"):
                pb_, pbr = PD.next()
                for j in range(16):
                    P.mm(pb_[:, 0:1], w1[:, j, :], PET[:, kv, j:j + 1], start=(j == 0), stop=(j == 15), reads=(w1r, r_TB), writes=(pbr,))
                P.cp("dve", PEB[:, kv:kv + 1], pb_[:, 0:1], reads=(pbr,), writes=(r_PEB,))
        if S.dry:
            return
        P.peb_done = True
        for kv in range(2):
            P.ts("dve", XG[:, kv * 136:(kv + 1) * 136], pc[:, kv * 136:(kv + 1) * 136], PEB[:, kv:kv + 1], ALU.add,
                 reads=(pcr, r_PEB), writes=(r_XG,))
        P.tt("dve", TG[:], XG[:], XG[:], ALU.mult, reads=(r_XG,), writes=(r_TG,))
        P.ts("dve", TG[:], TG[:], 0.044715, ALU.mult, 1.0, ALU.add, reads=(r_TG,), writes=(r_TG,))
        P.tt("dve", TG[:], TG[:], XG[:], ALU.mult, reads=(r_TG, r_XG), writes=(r_TG,))
        P.act(TG[:], TG[:], AF.Exp, reads=(r_TG,), writes=(r_TG,), scale=-1.5957691216057308)
        P.ts("dve", TG[:], TG[:], 1.0, ALU.add, reads=(r_TG,), writes=(r_TG,))
        P.op("dve", lambda e: e.reciprocal(out=TG[:], in_=TG[:]), reads=(r_TG,), writes=(r_TG,))
        P.tt("dve", GG[:], XG[:], TG[:], ALU.mult, reads=(r_TG, r_XG), writes=(r_GG,))
        p2, p2r = PD.next()
        for gi in range(8):
            P.mm(p2[0:64, gi * 34:(gi + 1) * 34], W2[:, gi // 4, :], GG[:, gi * 34:(gi + 1) * 34], start=True, stop=True,
                 reads=(r_W2, r_GG), writes=(p2r,), signal=(gi == 7))
        P.cp("act", KcTd[0:64, :, 32 * sb_ + 1:32 * sb_ + 35], p2[0:64, 0:136].rearrange("p (g i) -> p g i", i=34), reads=(p2r,), writes=(rKc,))
        P.cp("act", VcTd[0:64, :, 32 * sb_ + 1:32 * sb_ + 35], p2[0:64, 136:272].rearrange("p (g i) -> p g i", i=34), reads=(p2r,), writes=(rVc,))
        P.cp("pool", R2[:, :, 0:16], R2[:, :, 512:528], reads=(r_R2,), writes=(r_R2,))

    def kv_pass(sb_):
        slot = sb_ % 2
        with ExitStack() as ph:
            hT = P.sb(ph, "hTkv", [128, 8, 512], BF16)
            r_hT = Res()
            stage = Ring([P.sb(ph, "stg%d" % i, [128, 512], F32) for i in range(2)])
            XG = P.sb(ph, "XG", [128, 272], F32)
            TG = P.sb(ph, "TG", [128, 272], F32)
            GG = P.sb(ph, "GG", [128, 272], BF16)
            r_XG, r_TG, r_GG = Res(), Res(), Res()
            for tt_ in range(4):
                xt, xr = xk.next()
                P.dma("pool", xt[:], xall[sb_ * 512 + tt_ * 128: sb_ * 512 + tt_ * 128 + 128, :], writes=(xr,))
                norm_mod(xt[:], xr, 128, "M1", "SH1", (hT, r_hT), tt_ * 128)
            kstop = cfg.get("kstop", 9)
            for i in range(8 if kstop >= 2 else 0):
                w, wr = P.slab(s_inx[:, KD_0 + 256 * i: KD_0 + 256 * i + 256], 8, 256)
                if S.dry:
                    continue
                for j in range(2):
                    ti = 2 * i + j
                    kind, g = ti // 4, ti % 4
                    pd, pdr = PD.next()
                    proj(pd, pdr, [w[:, k, j * 128:(j + 1) * 128] for k in range(8)], [hT[:, k, :] for k in range(8)], (r_hT, wr))
                    if kind == 0:
                        P.cp("act", KE[0:64, g, sb_ * 512:(sb_ + 1) * 512], pd[0:64, :], reads=(pdr,), writes=(r_KE[g],))
                    elif kind == 1:
                        P.cp("act", KW[0:64, g, slot * 512:(slot + 1) * 512], pd[0:64, :], reads=(pdr,), writes=(r_KW,))
                    else:
                        gi = (kind - 2) * 4 + g
                        P.cp("dve", R2[0:64, gi, 16:528], pd[0:64, :], reads=(pdr,), writes=(r_R2,))
                        P.cp("act", R2[64:128, gi, 15:527], pd[64:128, :], reads=(pdr,), writes=(r_R2,))
            compress_block(sb_, KcT, r_Kc, VcT, r_VcT, XG, TG, GG, r_XG, r_TG, r_GG)
            if not S.dry:
                for cc in range((32 * sb_ + 31) // 128 + 1):
                    for g in range(4 if kstop >= 3.4 else 0):
                        pt, ptr_ = PTR.next()
                        P.tr(pt[:, 0:64], VcT[0:64, g, 128 * cc + 2:128 * cc + 130], ident[0:64, 0:64], reads=(r_VcT, r_ident), writes=(ptr_,))
                        P.cp("dve", Vc[:, cc, g, :], pt[:, 0:64], reads=(ptr_,), writes=(r_Vc,))
            for i in range(3 if kstop >= 5 else 0):
                w, wr = P.slab(s_inx[:, KV_0 + 512 * i: KV_0 + 512 * i + 512], 8, 512)
                if S.dry:
                    continue
                for tt_ in range(4):
                    pd, pdr = PD.next()
                    proj(pd, pdr, [hT[:, k, tt_ * 128:(tt_ + 1) * 128] for k in range(8)], [w[:, k, :] for k in range(8)], (r_hT, wr))
                    st, sr = stage.next()
                    P.cp("act" if tt_ % 2 == 0 else "dve", st[:], pd, reads=(pdr,), writes=(sr,))
                    t0 = sb_ * 512 + tt_ * 128
                    if i < 2:
                        P.dma("pool", okv[i][t0:t0 + 128, :], st[:], reads=(sr,))
                    elif sb_ == 7:
                        P.dma("pool", owin[tt_ * 128:(tt_ + 1) * 128, :], st[:], reads=(sr,))
                    if i == 1:
                        P.cp("pool", VS[:, sb_ * 4 + tt_, :, 0:64], st[:, 256:512].rearrange("p (g d) -> p g d", d=64), reads=(sr,), writes=(r_VS,))
                    elif i == 2:
                        P.cp("pool", VW[:, slot * 4 + tt_, :, 0:64], st[:, 256:512].rearrange("p (g d) -> p g d", d=64), reads=(sr,), writes=(r_VW,))
            S.barrier()

    def attention(m, QB, r_QB, GATES, r_G, OB, r_OB, ph):
        N = 32 * m + 32
        NS = N // 4
        ncc = (N + 127) // 128
        CVB = P.sb(ph, "CVB", [128, 256], BF16)
        SC = Ring([P.sb(ph, "SC%d" % i, [128, 256], F32) for i in range(2)])
        EC = Ring([P.sb(ph, "EC%d" % i, [128, 256], F32) for i in range(2)])
        PB = Ring([P.sb(ph, "PB%d" % i, [128, 256], BF16) for i in range(2)])
        PcT = Ring([P.sb(ph, "PcT%d" % i, [128, 2, 128], BF16) for i in range(2)])
        IMP = Ring([P.sb(ph, "IMP%d" % i, [128, 256], F32) for i in range(2)])
        I64 = P.sb(ph, "I64", [128, 64], F32)
        SCO = P.sb(ph, "SCO", [128, 64], F32)
        WRK = P.sb(ph, "WRK", [128, 64], F32)
        M8 = P.sb(ph, "M8", [128, 16], F32)
        SELB = Ring([P.sb(ph, "SELB%d" % i, [128, 128], BF16) for i in range(2)])
        PT = Ring([P.sb(ph, "PT%d" % i, [128, 256], BF16) for i in range(4)])
        OH = Ring([P.sb(ph, "OH%d" % i, [128, 2, 64], F32) for i in range(2)])
        RS = Ring([P.sb(ph, "RS%d" % i, [128, 4], F32) for i in range(4)])
        r_CVB, r_ZER, r_I64, r_SCO, r_WRK, r_M8 = (Res() for _ in range(6))
        for sbuf_, sres in zip(SELB.aps, SELB.res):
            P.memset("pool", sbuf_[:, 0:64], 0.0, writes=(sres,))
            P.memset("pool", sbuf_[:, 64:128], -BIG, writes=(sres,))
        for a in range(2):
            P.ts("dve", CVB[:, 0:N], CI31[:, 0:N], QPOS[:, m, a:a + 1], ALU.is_gt, -BIG, ALU.mult, reads=(r_TF,), writes=(r_CVB,))
            for g in range(4):
                imp, impr = IMP.next()
                for r in range(4):
                    h = 4 * g + r
                    pc, pcr = PS.next()
                    P.mm(pc[:, 0:N], QB[0:64, h, a * 128:(a + 1) * 128], KcT[0:64, g, 2:2 + N], start=True, stop=False, reads=(r_QB[h], r_Kc), writes=(pcr,))
                    P.mm(pc[:, 0:N], ident[:], CVB[:, 0:N], start=False, stop=True, reads=(r_ident, r_CVB), writes=(pcr,))
                    sc, scr_ = SC.next()
                    P.stt(sc[:, 0:N], CPOS[:, 0:N], SLOPES[h], pc[:, 0:N], ALU.mult, ALU.add, reads=(pcr, r_TF), writes=(scr_,))
                    ec, ecr = EC.next()
                    rs, rsr = RS.next()
                    P.act(ec[:, 0:N], sc[:, 0:N], AF.Exp, reads=(scr_, r_TF), writes=(ecr, rsr), bias=QSB[:, m, a, h:h + 1], accum_out=rs[:, 0:1])
                    P.ts("dve", rs[:, 1:2], rs[:, 0:1], 1e-30, ALU.add, reads=(rsr,), writes=(rsr,))
                    P.op("dve", lambda e, rs=rs: e.reciprocal(out=rs[:, 2:3], in_=rs[:, 1:2]), reads=(rsr,), writes=(rsr,))
                    if r == 0:
                        P.ts("dve", imp[:, 0:N], ec[:, 0:N], rs[:, 2:3], ALU.mult, reads=(ecr, rsr), writes=(impr,))
                    else:
                        P.stt(imp[:, 0:N], ec[:, 0:N], rs[:, 2:3], imp[:, 0:N], ALU.mult, ALU.add, reads=(ecr, rsr, impr), writes=(impr,))
                    pb, pbr = PB.next()
                    P.ts("dve", pb[:, 0:N], ec[:, 0:N], rs[:, 2:3], ALU.mult, reads=(ecr, rsr), writes=(pbr,))
                    pct, pctr = PcT.next()
                    for cc in range(ncc):
                        w_ = min(128, N - 128 * cc)
                        pt, ptr_ = PTR.next()
                        P.tr(pt[0:w_, 0:128], pb[:, 128 * cc:128 * cc + w_], ident[:], reads=(pbr, r_ident), writes=(ptr_,))
                        P.cp("act", pct[0:w_, cc, :], pt[0:w_, 0:128], reads=(ptr_,), writes=(pctr,))
                    po_, por = PO.next()
                    po = po_[:, 0:64]
                    for cc in range(ncc):
                        w_ = min(128, N - 128 * cc)
                        P.mm(po, pct[0:w_, cc, :], Vc[0:w_, cc, g, :], start=(cc == 0), stop=(cc == ncc - 1), reads=(pctr, r_Vc), writes=(por,))
                    P.ts("dve", OB[:, a, 64 * h:64 * h + 64], po, GATES[:, a, 3 * h:3 * h + 1], ALU.mult, reads=(por, r_G), writes=(r_OB,))
                P.op("dve", lambda e, imp=imp: e.tensor_reduce(out=I64[:, 0:NS], in_=imp[:, 0:N].rearrange("p (j k) -> p j k", k=4), axis=AX.X, op=ALU.add),
                     reads=(impr,), writes=(r_I64,))
                if NS > 1:
                    P.tt("dve", I64[:, 1:NS], I64[:, 1:NS], imp[:, 3:N - 4:4], ALU.add, reads=(impr, r_I64), writes=(r_I64,))
                P.tt("dve", SCO[:, 0:NS], I64[:, 0:NS], FBT[:, 2 * m + a, 0:NS], ALU.add, reads=(r_I64, r_TB), writes=(r_SCO,))
                sel, selr = SELB.next()
                if NS >= 16:
                    P.op("dve", lambda e: e.max(out=M8[:, 0:8], in_=SCO[:, 0:NS]), reads=(r_SCO,), writes=(r_M8,))
                    P.op("dve", lambda e: e.match_replace(out=WRK[:, 0:NS], in_to_replace=M8[:, 0:8], in_values=SCO[:, 0:NS], imm_value=-3.0e38),
                         reads=(r_SCO, r_M8), writes=(r_WRK,))
                    P.op("dve", lambda e: e.max(out=M8[:, 8:16], in_=WRK[:, 0:NS]), reads=(r_WRK,), writes=(r_M8,))
                    P.ts("dve", M8[:, 15:16], M8[:, 15:16], -1e29, ALU.max, reads=(r_M8,), writes=(r_M8,))
                    P.ts("dve", sel[:, 64:64 + NS], SCO[:, 0:NS], M8[:, 15:16], ALU.is_lt, -BIG, ALU.mult, reads=(r_SCO, r_M8), writes=(selr,))
                else:
                    P.ts("dve", sel[:, 64:64 + NS], SCO[:, 0:NS], -1e29, ALU.is_lt, -BIG, ALU.mult, reads=(r_SCO,), writes=(selr,))
                pt, ptr_ = PTR.next()
                P.tr(pt[:, 0:128], sel[:], ident[:], reads=(selr, r_ident), writes=(ptr_,))
                for r in range(4):
                    h = 4 * g + r
                    P.stt(QB[64:128, h, a * 128:(a + 1) * 128], POSQ[64:128, a * 128:(a + 1) * 128], -SLOPES[h], pt[64:128, 0:128],
                          ALU.mult, ALU.add, reads=(ptr_, r_TF), writes=(r_QB[h],))
        tiles = []
        for h in range(16):
            g = h // 4
            for br in (1, 2):
                lst = []
                if br == 1:
                    for j in range(4 * m + 4):
                        r = j - 4 * m
                        c0, c1 = (0, 256) if r <= 1 else (128, 256)
                        masks = {0: [(0, DT[:, 0, :])], 1: [(0, DT[:, 1, :])], 2: [(1, DT[:, 2, :])], 3: [(1, DT[:, 3, :])]}.get(r, [])
                        pv = [a for a in (0, 1) if (a == 1 or r <= 1)]
                        lst.append(dict(k=KE[:, g, j * 128:(j + 1) * 128], kres=r_KE[g], c0=c0, c1=c1, masks=masks,
                                        bias=ALB[:, h, 32 + r:33 + r], v=VS[:, j, g, 0:65], vres=r_VS, pv=pv))
                else:
                    for rp in range(8):
                        j = 4 * m - 4 + rp
                        if j < 0:
                            continue
                        sl = (j // 4) % 2
                        c0, c1 = (0, 128) if rp <= 1 else ((0, 256) if rp <= 5 else (128, 256))
                        masks = []
                        if rp <= 5:
                            masks.append((0, WT[:, rp, :]))
                        if rp >= 2:
                            masks.append((1, WT[:, 6 + rp - 2, :]))
                        pv = [a for a in (0, 1) if (a == 0 and rp <= 5) or (a == 1 and rp >= 2)]
                        lst.append(dict(k=KW[:, g, sl * 512 + (j % 4) * 128: sl * 512 + (j % 4) * 128 + 128], kres=r_KW, c0=c0, c1=c1, masks=masks,
                                        bias=ALB[:, h, 32 + rp - 4:33 + rp - 4], v=VW[:, sl * 4 + j % 4, g, 0:65], vres=r_VW, pv=pv))
                for a in (0, 1):
                    idx = [i for i, t_ in enumerate(lst) if a in t_["pv"]]
                    for i, t_ in enumerate(lst):
                        t_.setdefault("first", {})[a] = (i == idx[0])
                        t_.setdefault("last", {})[a] = (i == idx[-1])
                for i, t_ in enumerate(lst):
                    t_["h"] = h
                    t_["br"] = br
                    t_["end"] = (i == len(lst) - 1)
                    t_["begin"] = (i == 0)
                tiles.extend(lst)
        nt = len(tiles)
        state = {}

        def stageA(t_):
            ps, psr = PS.next()
            t_["ps"], t_["psr"] = ps, psr
            c0, c1 = t_["c0"], t_["c1"]
            nm = len(t_["masks"])
            P.mm(ps[:, c0:c1], t_["k"], QB[:, t_["h"], c0:c1], start=True, stop=(nm == 0), reads=(t_["kres"], r_QB[t_["h"]]), writes=(psr,))
            for i, (a, tab) in enumerate(t_["masks"]):
                P.mm(ps[:, a * 128:(a + 1) * 128], ident[:], tab, start=False, stop=(i == nm - 1), reads=(r_ident, r_TB), writes=(psr,))

        def stageB(t_):
            pt, ptr_ = PT.next()
            t_["pt"], t_["ptr"] = pt, ptr_
            c0, c1 = t_["c0"], t_["c1"]
            P.act(pt[:, c0:c1], t_["ps"][:, c0:c1], AF.Exp, reads=(t_["psr"], r_TF), writes=(ptr_,), bias=t_["bias"])

        def stageC(t_):
            if t_["begin"]:
                state["po"] = PO.next()
            po, por = state["po"]
            pov = po.rearrange("p (a d) -> p a d", d=65)
            for a in t_["pv"]:
                P.mm(pov[:, a, :], t_["pt"][:, a * 128:(a + 1) * 128], t_["v"], start=(t_["begin"] and a == t_["pv"][0]), stop=t_["last"][a],
                     reads=(t_["ptr"], t_["vres"]), writes=(por,), signal=True, sgc=True)
            if t_["end"]:
                h, br = t_["h"], t_["br"]
                rs, rsr = RS.next()
                P.op("dve", lambda e, rs=rs, pov=pov: e.reciprocal(out=rs[:, 0:2], in_=pov[:, :, 64]), reads=(por,), writes=(rsr,))
                P.tt("dve", rs[:, 2:4], rs[:, 0:2], GATES[:, :, 3 * h + br], ALU.mult, reads=(rsr, r_G), writes=(rsr,))
                if br == 1:
                    state["oh"] = OH.next()
                oh, ohr = state["oh"]
                for a in (0, 1):
                    if br == 1:
                        P.ts("dve", oh[:, a, :], pov[:, a, 0:64], rs[:, 2 + a:3 + a], ALU.mult, reads=(por, rsr), writes=(ohr,))
                    else:
                        P.stt(oh[:, a, :], pov[:, a, 0:64], rs[:, 2 + a:3 + a], oh[:, a, :], ALU.mult, ALU.add, reads=(por, rsr, ohr), writes=(ohr,))
                if br == 2:
                    P.tt("pool", OB[:, :, 64 * h:64 * h + 64], oh[:], OB[:, :, 64 * h:64 * h + 64], ALU.add, reads=(ohr, r_OB), writes=(r_OB,))

        for t in range(nt + 2):
            if t < nt:
                stageA(tiles[t])
            if 1 <= t <= nt:
                stageB(tiles[t - 1])
            if t >= 2:
                stageC(tiles[t - 2])

    def own_unit(m):
        with ExitStack() as ph:
            hT = P.sb(ph, "hTo", [128, 8, 260], BF16)
            REG = P.sb(ph, "REG", [128, 6144], BF16)
            YA = REG[:, 0:2048].rearrange("p (c t) -> p c t", t=256)
            oT = REG[:, 2048:4096].rearrange("p (c t) -> p c t", t=256)
            zT = REG[:, 4096:6144].rearrange("p (c t) -> p c t", t=256)
            aT = REG[:, 0:5632].rearrange("p (c t) -> p c t", t=256)
            XO = P.sb(ph, "XO", [128, 2, D], F32)
            QB = P.sb(ph, "QB", [128, 16, 256], BF16)
            OB = P.sb(ph, "OB", [128, 2, D], BF16)
            GATES = P.sb(ph, "GATES", [128, 2, 48], F32)
            CG = P.sb(ph, "CG", [128, 260], F32)
            UP = P.sb(ph, "UP", [128, 2, 130], F32)
            VV = P.sb(ph, "VV", [128, 2, 128], F32)
            SG = P.sb(ph, "SG", [128, 512], F32)
            TM = Ring([P.sb(ph, "TM%d" % i, [128, 512], F32) for i in range(2)])
            r_hT, r_YA, r_oT, r_zT, r_aT, r_G, r_OB, r_CG, r_UP, r_VV, r_SG, r_Z1, r_Z2 = (Res() for _ in range(13))
            r_XO = [Res(), Res()]
            r_QB = [Res() for _ in range(16)]
            TMN = (REG[:, 0:2048].bitcast(F32), r_aT)
            for a in range(2):
                P.dma("pool", XO[:, a, :], xown[m, a * 128:(a + 1) * 128, :], writes=(r_XO[a],))
                norm_mod(XO[:, a, :], r_XO[a], 128, "M1", "SH1", (hT, r_hT), a * 128)
            xh, xhr = xk.next()
            P.dma("pool", xh[0:4, :], xown[m, 256:260, :], writes=(xhr,))
            norm_mod(xh[0:4, :], xhr, 4, "M1", "SH1", (hT, r_hT), 256)
            for c in range(8):
                w, wr = P.slab(s_inx[:, FM1_0 + 384 * c: FM1_0 + 384 * c + 384], 8, 384)
                if S.dry:
                    continue
                pa, par = PD.next()
                pb_, pbr = PD.next()
                rh = [hT[:, k, 0:256] for k in range(8)]
                rhh = [hT[:, k, 256:260] for k in range(8)]
                proj(pa[:, 0:256], par, [w[:, k, 0:128] for k in range(8)], rh, (r_hT, wr))
                proj(pa[:, 256:512], par, [w[:, k, 128:256] for k in range(8)], rh, (r_hT, wr))
                proj(pb_[:, 0:256], pbr, [w[:, k, 256:384] for k in range(8)], rh, (r_hT, wr))
                proj(pb_[:, 256:260], pbr, [w[:, k, 128:256] for k in range(8)], rhh, (r_hT, wr))
                proj(pb_[:, 260:264], pbr, [w[:, k, 256:384] for k in range(8)], rhh, (r_hT, wr))
                P.cp("act", CG[:, 0:256], pa[:, 256:512], reads=(par,), writes=(r_CG,))
                P.cp("act", CG[:, 256:260], pb_[:, 256:260], reads=(pbr,), writes=(r_CG,))
                P.tt("dve", UP[:, :, 2:130], pb_[:, 0:256].rearrange("p (a t) -> p a t", t=128), CG[:, 0:256].rearrange("p (a t) -> p a t", t=128),
                     ALU.mult, reads=(pbr, r_CG), writes=(r_UP,))
                P.tt("dve", UP[:, :, 0:2], pb_[:, 260:264].rearrange("p (a t) -> p a t", t=2), CG[:, 256:260].rearrange("p (a t) -> p a t", t=2),
                     ALU.mult, reads=(pbr, r_CG), writes=(r_UP,))
                P.tt("dve", UP[:, :, 0:2], UP[:, :, 0:2], HF[:, m, :].rearrange("p (a t) -> p a t", t=2), ALU.mult, reads=(r_UP, r_TF), writes=(r_UP,))
                P.ts("dve", VV[:], UP[:, :, 0:128], WCV[:, c, 0:1], ALU.mult, WCV[:, c, 3:4], ALU.add, reads=(r_UP, r_WCV), writes=(r_VV,))
                P.stt(VV[:], UP[:, :, 1:129], WCV[:, c, 1:2], VV[:], ALU.mult, ALU.add, reads=(r_UP, r_WCV, r_VV), writes=(r_VV,))
                P.stt(VV[:], UP[:, :, 2:130], WCV[:, c, 2:3], VV[:], ALU.mult, ALU.add, reads=(r_UP, r_WCV, r_VV), writes=(r_VV,))
                P.tt("dve", YA[:, c, :].rearrange("p (a t) -> p a t", t=128), VV[:], pa[:, 0:256].rearrange("p (a t) -> p a t", t=128), ALU.mult,
                     reads=(r_VV, par), writes=(r_YA,))
                if m == 7:
                    P.cp("dve", CONVO[:, c, :], UP[:, 1, 128:130], reads=(r_UP,), writes=(r_CONVO,))
            w, wr = P.slab(s_inx[:, G_0:G_0 + 64], 8, 64)
            if not S.dry:
                for a in range(2):
                    pd, pdr = PD.next()
                    proj(pd[:, 0:48], pdr, [hT[:, k, a * 128:(a + 1) * 128] for k in range(8)], [w[:, k, 0:48] for k in range(8)], (r_hT, wr))
                    sigmoid_from(GATES[:, a, :], pd[:, 0:48], None, (pdr,), (r_G,), None)
            for i in range(4):
                w, wr = P.slab(s_inx[:, Q_0 + 256 * i: Q_0 + 256 * i + 256], 8, 256)
                if S.dry:
                    continue
                for j in range(2):
                    mt = 2 * i + j
                    pd, pdr = PD.next()
                    proj(pd[:, 0:256], pdr, [w[:, k, j * 128:(j + 1) * 128] for k in range(8)], [hT[:, k, 0:256] for k in range(8)], (r_hT, wr))
                    P.op("act", lambda e, pd=pd, mt=mt: e.mul(out=QB[0:64, 2 * mt, :], in_=pd[0:64, 0:256], mul=0.125), reads=(pdr,), writes=(r_QB[2 * mt],))
                    P.op("act", lambda e, pd=pd, mt=mt: e.mul(out=QB[0:64, 2 * mt + 1, :], in_=pd[64:128, 0:256], mul=0.125), reads=(pdr,), writes=(r_QB[2 * mt + 1],))
            if not S.dry:
                if cfg["attn"]:
                    attention(m, QB, r_QB, GATES, r_G, OB, r_OB, ph)
                else:
                    P.memset("pool", OB[:], 0.0, writes=(r_OB,))
                for a in range(2):
                    pt, ptr_ = PTR.next()
                    for kc in range(8):
                        P.tr(pt[:, kc * 128:(kc + 1) * 128], OB[:, a, kc * 128:(kc + 1) * 128], ident[:], reads=(r_OB, r_ident), writes=(ptr_,), signal=(kc == 7))
                    ptv = pt.rearrange("p (j t) -> p j t", t=128)
                    P.cp("act" if a == 0 else "dve", oT[:, 0:8, a * 128:(a + 1) * 128], ptv[:, 0:8, :], reads=(ptr_,), writes=(r_oT,))
            for c in range(8):
                w, wr = P.slab(s_d2[:, 512 * c:512 * c + 512], 8, 512)
                if S.dry:
                    continue
                pa, par = PD.next()
                pb_, pbr = PD.next()
                rh = [hT[:, k, 0:256] for k in range(8)]
                proj(pa[:, 0:256], par, [w[:, k, 0:128] for k in range(8)], rh, (r_hT, wr))
                proj(pa[:, 256:512], par, [w[:, k, 128:256] for k in range(8)], rh, (r_hT, wr))
                proj(pb_[:, 0:256], pbr, [w[:, k, 256:384] for k in range(8)], [YA[:, k, :] for k in range(8)], (r_YA, wr))
                proj(pb_[:, 256:512], pbr, [w[:, k, 384:512] for k in range(8)], [oT[:, k, :] for k in range(8)], (r_oT, wr))
                sigmoid_from(SG[:], pa, None, (par,), (r_SG,), None)
                P.tt("dve", SG[:], SG[:], pb_, ALU.mult, reads=(r_SG, pbr), writes=(r_SG,))
                P.tt("pool", zT[:, c, :], SG[:, 0:256], SG[:, 256:512], ALU.add, reads=(r_SG,), writes=(r_zT,))
            for i in range(2):
                w, wr = P.slab(s_out[:, 512 * i:512 * i + 512], 8, 512)
                if S.dry:
                    continue
                for a in range(2):
                    pd, pdr = PD.next()
                    proj(pd, pdr, [zT[:, k, a * 128:(a + 1) * 128] for k in range(8)], [w[:, k, :] for k in range(8)], (r_zT, wr))
                    tm, tmr = TM.next()
                    P.tt("dve", tm[:], pd, msec("G1")[:, 512 * i:512 * i + 512], ALU.mult, reads=(pdr, r_MOD), writes=(tmr,))
                    P.tt("pool", XO[:, a, 512 * i:512 * i + 512], tm[:], XO[:, a, 512 * i:512 * i + 512], ALU.add, reads=(tmr, r_XO[a]), writes=(r_XO[a],))
            if not S.dry:
                for a in range(2):
                    norm_mod(XO[:, a, :], r_XO[a], 128, "M2", "SH2", (hT, r_hT), a * 128)
                S.barrier()
            for fc in range(NFC):
                w, wr = P.slab(s_gu[:, 256 * fc:256 * fc + 256], 8, 256)
                if S.dry:
                    continue
                pd, pdr = PD.next()
                rh = [hT[:, k, 0:256] for k in range(8)]
                proj(pd[:, 0:256], pdr, [w[:, k, 0:128] for k in range(8)], rh, (r_hT, wr))
                proj(pd[:, 256:512], pdr, [w[:, k, 128:256] for k in range(8)], rh, (r_hT, wr))
                tm, tmr = TM.next()
                sigmoid_from(tm[:, 0:256], pd[:, 0:256], None, (pdr,), (tmr,), None)
                P.tt("dve", tm[:, 0:256], tm[:, 0:256], pd[:, 0:256], ALU.mult, reads=(tmr, pdr), writes=(tmr,))
                P.tt("dve", aT[:, fc, :], tm[:, 0:256], pd[:, 256:512], ALU.mult, reads=(tmr, pdr), writes=(r_aT,))
            for i in range(2):
                pds = None
                for s_ in range(3):
                    nk = 8 if s_ < 2 else 6
                    w, wr = P.slab(s_down[1024 * s_:1024 * s_ + 128 * nk, 512 * i:512 * i + 512], nk, 512)
                    if S.dry:
                        continue
                    if pds is None:
                        pds = [PD.next(), PD.next()]
                    for a in range(2):
                        pd, pdr = pds[a]
                        for k in range(nk):
                            fc = 8 * s_ + k
                            P.mm(pd, aT[:, fc, a * 128:(a + 1) * 128], w[:, k, :], start=(fc == 0), stop=(fc == NFC - 1), reads=(r_aT, wr), writes=(pdr,),
                                 signal=(k == nk - 1))
                if S.dry:
                    continue
                for a in range(2):
                    pd, pdr = pds[a]
                    tm, tmr = TM.next()
                    P.tt("dve", tm[:], pd, msec("G2")[:, 512 * i:512 * i + 512], ALU.mult, reads=(pdr, r_MOD), writes=(tmr,))
                    P.tt("pool", XO[:, a, 512 * i:512 * i + 512], tm[:], XO[:, a, 512 * i:512 * i + 512], ALU.add, reads=(tmr, r_XO[a]), writes=(r_XO[a],))
            if not S.dry:
                nf, nfr = TMN
                P.dma("pool", nf[:], norms[2:3, :].to_broadcast((128, D)), writes=(nfr,))
                for a in range(2):
                    rs, rsr = rstd_of(XO[:, a, :], r_XO[a], 128)
                    yt, ytr = xk.next()
                    P.stt(yt[:], XO[:, a, :], rs, nf[:], ALU.mult, ALU.mult, reads=(r_XO[a], rsr, nfr), writes=(ytr,))
                    P.dma("pool", yown[m, a * 128:(a + 1) * 128, :], yt[:], reads=(ytr,))
                S.barrier()

    def sample_phase():
        compute_mod(1)
        with ExitStack() as ph:
            def A(name, shape, dt):
                return P.sb(ph, name, shape, dt)
            TSF = A("TSF", [128, TSF_N], F32)
            TSB = A("TSB", [128, TSB_N], BF16)
            r_TS = Res()
            P.dma("sp", TSF[:], ts_f32, writes=(r_TS,))
            P.dma("sp", TSB[:], ts_bf, writes=(r_TS,))

            def tsf(name, rows=128):
                o, n = TSF_OFF[name]
                return TSF[0:rows, o:o + n]
            SLC = tsf("SLC", 64)
            NQB = tsf("NQB", 64)
            CPS = tsf("CPS", 64)
            FBS = tsf("FBS", 64)
            SELG = tsf("SELG", 64)
            L2 = tsf("L2", 2)
            B2S = tsf("B2S", 2).rearrange("p (c r) -> p c r", r=64)
            B2W = tsf("B2W", 2).rearrange("p (c r) -> p c r", r=64)
            NEWB = tsf("NEWB", 4)
            WM0 = TSB[:, TSB_OFF["WM0"][0]:TSB_OFF["WM0"][0] + 64]
            EFULL = TSB[:, TSB_OFF["EFULL"][0]:TSB_OFF["EFULL"][0] + PAST]
            PTB = A("PTB", [128, 4 * NPG], I32)
            IDX = A("IDX", [128, 4 * NPG], I32)
            r_IDX = Res()
            P.dma("pool", PTB[:], ptab_d.rearrange("b j -> (b j)").rearrange("(o n) -> o n", o=1).to_broadcast((128, 4 * NPG)), writes=(r_IDX,))
            P.ts("dve", IDX[:], PTB[:], 128.0, ALU.mult, tsf("PIO")[:, 0:1], ALU.add, reads=(r_IDX, r_TS), writes=(r_IDX,))
            XS = A("XS", [128, D], F32)
            hT = A("hTs", [128, 8, 16], BF16)
            YA = A("YAs", [128, 8, 16], BF16)
            oT = A("oTs", [128, 8, 16], BF16)
            zT = A("zTs", [128, 8, 16], BF16)
            aT = A("aTs", [128, NFC, 16], BF16)
            QBs = A("QBs", [64, 16, 16], BF16)
            KNS = A("KNS", [64, 4, 16], BF16)
            KNW = A("KNW", [64, 4, 16], BF16)
            VN = A("VN", [4, 4, 2, 4, 66], BF16)
            SCV = A("SCV", [128, 8, 4, 2], F32)
            CONVS = A("CONVS", [128, 8, 4, 2], F32)
            UPs = A("UPs", [128, 4, 6], F32)
            VVs = A("VVs", [128, 4, 4], F32)
            CGs = A("CGs", [128, 16], F32)
            GATS = A("GATS", [4, 4, 48], F32)
            STG = A("STG", [128, 3, 512], F32)
            SG = A("SGs", [128, 64], F32)
            TM = Ring([A("TMs%d" % i, [128, 512], F32) for i in range(2)])
            XG = A("XGs", [128, 272], F32)
            TG = A("TGs", [128, 272], F32)
            GG = A("GGs", [128, 272], BF16)
            PG = Ring([A("PG%d" % i, [128, 4, 512], BF16) for i in range(2)])
            KT = Ring([A("KT%d" % i, [64, 4, 512], BF16) for i in range(2)])
            VP = Ring([A("VP%d" % i, [128, 4, 4, 66], BF16) for i in range(2)])
            KcTs = A("KcTs", [64, 4, 520], BF16)
            VcTs = A("VcTs", [64, 4, 520], BF16)
            Vcs = A("Vcs", [128, 4, 4, 64], BF16)
            QZ = A("QZ", [64, 4, 64], BF16)
            SCs = A("SCs", [64, 512], F32)
            ECs = A("ECs", [64, 512], F32)
            P32 = A("P32", [64, 512], F32)
            PBs = A("PBs", [64, 512], BF16)
            I64s = A("I64s", [64, 128], F32)
            SCOs = A("SCOs", [64, 128], F32)
            WRKs = A("WRKs", [64, 128], F32)
            M8s = A("M8s", [64, 16], F32)
            SELs = A("SELs", [64, 128], BF16)
            MBT = A("MBT", [128, 64], BF16)
            PcTs = A("PcTs", [128, 4, 64], BF16)
            PTs = Ring([A("PTs%d" % i, [128, 256], BF16) for i in range(2)])
            SN = A("SN", [4, 64], F32)
            PTn = A("PTn", [4, 64], BF16)
            OSUM = A("OSUM", [4, 16, 64], F32)
            OTMP = A("OTMP", [4, 6, 64], F32)
            OBs = A("OBs", [4, D], BF16)
            RSs = Ring([A("RSs%d" % i, [64, 16], F32) for i in range(4)])
            (r_XS, r_hT, r_YA, r_oT, r_zT, r_aT, r_QBs, r_KN, r_VN, r_SCV, r_CONVS, r_UP, r_VV, r_CG, r_GAT, r_STG, r_SG, r_XG, r_TG, r_GG,
             r_KcTs, r_VcTs, r_Vcs, r_QZ, r_SCs, r_ECs, r_P32, r_PBs, r_I64, r_SCO, r_WRK, r_M8, r_SEL, r_MBT, r_PcT, r_SN, r_PTn, r_OSUM,
             r_OTMP, r_OBs) = (Res() for _ in range(40))
            for vtile, rr in zip(VP.aps, VP.res):
                P.memset("pool", vtile[:, :, :, 64:66], 1.0, writes=(rr,))
            P.memset("pool", VN[:, :, :, :, 64:66], 1.0, writes=(r_VN,))
            P.memset("pool", QZ[:], 0.0, writes=(r_QZ,))
            P.memset("pool", oT[:], 0.0, writes=(r_oT,))
            P.memset("pool", P32[:], 0.0, writes=(r_P32,))
            P.dma("pool", XS[0:16, :], xs_d, writes=(r_XS,))
            norm_mod(XS[0:16, :], r_XS, 16, "M1", "SH1", (hT, r_hT), 0)
            for c in range(8):
                P.dma("pool", SCV[:, c, :, :], sconv_d[:, :, c * 128:(c + 1) * 128].rearrange("b k p -> p b k"), writes=(r_SCV,), allow_slow_non_contiguous=True)
            rh = [hT[:, k, :] for k in range(8)]
            for c in range(8):
                w, wr = P.slab(s_inx[:, FM1_0 + 384 * c: FM1_0 + 384 * c + 384], 8, 384)
                if S.dry:
                    continue
                pa, par = PD.next()
                proj(pa[:, 0:16], par, [w[:, k, 0:128] for k in range(8)], rh, (r_hT, wr))
                proj(pa[:, 16:32], par, [w[:, k, 128:256] for k in range(8)], rh, (r_hT, wr))
                proj(pa[:, 32:48], par, [w[:, k, 256:384] for k in range(8)], rh, (r_hT, wr))
                P.cp("dve", CGs[:], pa[:, 16:32], reads=(par,), writes=(r_CG,))
                P.tt("dve", UPs[:, :, 2:6], pa[:, 32:48].rearrange("p (b q) -> p b q", q=4), CGs[:].rearrange("p (b q) -> p b q", q=4), ALU.mult,
                     reads=(par, r_CG), writes=(r_UP,))
                P.cp("dve", UPs[:, :, 0:2], SCV[:, c, :, :], reads=(r_SCV,), writes=(r_UP,))
                P.ts("dve", VVs[:], UPs[:, :, 0:4], WCV[:, c, 0:1], ALU.mult, WCV[:, c, 3:4], ALU.add, reads=(r_UP, r_WCV), writes=(r_VV,))
                P.stt(VVs[:], UPs[:, :, 1:5], WCV[:, c, 1:2], VVs[:], ALU.mult, ALU.add, reads=(r_UP, r_WCV, r_VV), writes=(r_VV,))
                P.stt(VVs[:], UPs[:, :, 2:6], WCV[:, c, 2:3], VVs[:], ALU.mult, ALU.add, reads=(r_UP, r_WCV, r_VV), writes=(r_VV,))
                P.tt("dve", YA[:, c, :].rearrange("p (b q) -> p b q", q=4), VVs[:], pa[:, 0:16].rearrange("p (b q) -> p b q", q=4), ALU.mult,
                     reads=(r_VV, par), writes=(r_YA,))
                P.cp("dve", CONVS[:, c, :, :], UPs[:, :, 4:6], reads=(r_UP,), writes=(r_CONVS,))
            if not S.dry:
                for c in range(8):
                    P.dma("pool", oconv_s[:, :, c * 128:(c + 1) * 128].rearrange("b k p -> p b k"), CONVS[:, c, :, :], reads=(r_CONVS,), allow_slow_non_contiguous=True)
            w, wr = P.slab(s_inx[:, G_0:G_0 + 64], 8, 64)
            if not S.dry:
                pd, pdr = PD.next()
                for bi in range(4):
                    proj(pd[0:4, 48 * bi:48 * bi + 48], pdr, [hT[:, k, 4 * bi:4 * bi + 4] for k in range(8)], [w[:, k, 0:48] for k in range(8)], (r_hT, wr))
                sigmoid_from(GATS[:].rearrange("p b j -> p (b j)"), pd[0:4, 0:192], None, (pdr,), (r_GAT,), None)
            for i in range(4):
                w, wr = P.slab(s_inx[:, Q_0 + 256 * i: Q_0 + 256 * i + 256], 8, 256)
                if S.dry:
                    continue
                for j in range(2):
                    mt = 2 * i + j
                    pd, pdr = PD.next()
                    proj(pd[:, 0:16], pdr, [w[:, k, j * 128:(j + 1) * 128] for k in range(8)], rh, (r_hT, wr))
                    P.op("act", lambda e, pd=pd, mt=mt: e.mul(out=QBs[0:64, 2 * mt, :], in_=pd[0:64, 0:16], mul=0.125), reads=(pdr,), writes=(r_QBs,))
                    P.op("act", lambda e, pd=pd, mt=mt: e.mul(out=QBs[0:64, 2 * mt + 1, :], in_=pd[64:128, 0:16], mul=0.125), reads=(pdr,), writes=(r_QBs,))
            for i in range(4):
                w, wr = P.slab(s_inx[:, KD_0 + 256 * i: KD_0 + 256 * i + 256], 8, 256)
                if S.dry:
                    continue
                for j in range(2):
                    ti = 2 * i + j
                    kind, g = ti // 4, ti % 4
                    pd, pdr = PD.next()
                    proj(pd[:, 0:16], pdr, [w[:, k, j * 128:(j + 1) * 128] for k in range(8)], rh, (r_hT, wr))
                    P.cp("act", (KNS if kind == 0 else KNW)[0:64, g, :], pd[0:64, 0:16], reads=(pdr,), writes=(r_KN,))
            for i in range(3):
                w, wr = P.slab(s_inx[:, KV_0 + 512 * i: KV_0 + 512 * i + 512], 8, 512)
                if S.dry:
                    continue
                pd, pdr = PD.next()
                proj(pd[0:16, :], pdr, [hT[:, k, :] for k in range(8)], [w[:, k, :] for k in range(8)], (r_hT, wr))
                P.cp("act", STG[0:16, i, :], pd[0:16, :], reads=(pdr,), writes=(r_STG,))
                if i < 2:
                    P.dma("pool", oskv[i], STG[0:16, i, :], reads=(r_STG,))
                else:
                    for bi in range(4):
                        P.dma("pool", owin_s[bi, 508:512, :], STG[4 * bi:4 * bi + 4, i, :], reads=(r_STG,))
                if i >= 1:
                    for bi in range(4):
                        pv, pvr = PD.next()
                        proj(pv[0:4, 0:256], pvr, [hT[:, k, 4 * bi:4 * bi + 4] for k in range(8)], [w[:, k, 256:512] for k in range(8)], (r_hT, wr))
                        P.cp("dve", VN[0:4, bi, i - 1, :, 0:64], pv[0:4, 0:256].rearrange("p (g d) -> p g d", d=64), reads=(pvr,), writes=(r_VN,))
            if not S.dry:
                for bi in range(4):
                    P.dma("sp", owin_s[bi, 0:508, :], cwin_d[bi, 4:512, :])
            OAB = [(PO.aps[0], PO.res[0]), (PO.aps[1], PO.res[1]), (PD.aps[2], PD.res[2])]
            PDs = Ring([PD.aps[0], PD.aps[1]])
            PDs.res = [PD.res[0], PD.res[1]]
            POt_full = [POt[0][:], POt[1][:], PD.aps[2]]

            def oa(h):
                b = h // 6
                sl_ = h - 6 * b
                return POt_full[b][0:4, sl_ * 65:sl_ * 65 + 65], OAB[b][1], b

            regstate = {}

            def gather4(cache, bi, pg, dst, dres):
                for s_ in range(4):
                    j = 4 * pg + s_

                    def fn(e, j=j, s_=s_, dst=dst, cache=cache, bi=bi):
                        return e.indirect_dma_start(out=dst[:, s_, :], out_offset=None, in_=cache,
                                                    in_offset=bass.IndirectOffsetOnAxis(ap=IDX[:, bi * NPG + j:bi * NPG + j + 1], axis=0))
                    P.op("pool", fn, (r_IDX,), (dres,), dma=True)

            def trans_block(src, sres, col0, eng):
                pt, ptr_ = PTR.next()
                for s_ in range(4):
                    P.tr(pt[:, s_ * 128:(s_ + 1) * 128], src[:, s_, col0:col0 + 128], ident[:], reads=(sres, r_ident), writes=(ptr_,), signal=(s_ == 3))
                return pt[:, 0:512], ptr_

            def attn_chunks(bi, ktile, kres, vtile, vres, nchunks, mask_fn, bias_fn, first_flags):
                ps, psr = PS.next()
                for cs in range(nchunks):
                    for g in range(4):
                        P.mm(ps[:, cs * 64 + 16 * g:cs * 64 + 16 * g + 16], ktile[0:64, g, cs * 128:(cs + 1) * 128], QBs[0:64, 4 * g:4 * g + 4, 4 * bi:4 * bi + 4],
                             start=(cs == 0 and g == 0), stop=False, reads=(kres, r_QBs), writes=(psr,), signal=False, sgc=True)
                    mk = mask_fn(cs)
                    if mk is not None:
                        P.mm(ps[:, cs * 64:cs * 64 + 64], mk[0], mk[1], start=False, stop=False, reads=(mk[2],), writes=(psr,), signal=False, sgc=True)
                    P.mm(ps[:, cs * 64:cs * 64 + 64], L2, bias_fn(cs), start=False, stop=True, reads=(r_TS,), writes=(psr,), signal=True, sgc=True)
                pt_, ptr2 = PTs.next()
                P.act(pt_[:, 0:64 * nchunks], ps[:, 0:64 * nchunks], AF.Exp, reads=(psr,), writes=(ptr2,))
                for cs in range(nchunks):
                    for h in range(16):
                        o_, ores, b = oa(h)
                        P.mm(o_, pt_[:, cs * 64 + 4 * h:cs * 64 + 4 * h + 4], vtile[:, cs, h // 4, 0:65], start=first_flags[b], stop=False,
                             reads=(ptr2, vres), writes=(ores,), signal=(h % 6 == 5 or h == 15), sgc=True)
                        first_flags[b] = False

            def new_chunk(bi, KN, vsel, first_flags):
                pn, pnr = PS.next()
                for g in range(4):
                    P.mm(pn[0:4, 16 * g:16 * g + 16], KN[0:64, g, 4 * bi:4 * bi + 4], QBs[0:64, 4 * g:4 * g + 4, 4 * bi:4 * bi + 4],
                         start=(g == 0), stop=(g == 3), reads=(r_KN, r_QBs), writes=(pnr,), signal=(g == 3), sgc=True)
                P.tt("dve", SN[:], pn[0:4, 0:64], NEWB, ALU.add, reads=(pnr, r_TS), writes=(r_SN,))
                P.act(PTn[:], SN[:], AF.Exp, reads=(r_SN,), writes=(r_PTn,))
                for h in range(16):
                    o_, ores, b = oa(h)
                    P.mm(o_, PTn[0:4, 4 * h:4 * h + 4], VN[0:4, bi, vsel, h // 4, 0:65], start=first_flags[b], stop=True,
                         reads=(r_PTn, r_VN), writes=(ores,), signal=(h % 6 == 5 or h == 15), sgc=True)
                    first_flags[b] = False

            def combine(bi, br, normalized, first):
                for b in range(3):
                    h0 = 6 * b
                    nh = min(6, 16 - h0)
                    ov = POt_full[b][0:4, 0:nh * 65].rearrange("p (h d) -> p h d", d=65)
                    ores = OAB[b][1]
                    rs, rsr = RSs.next()
                    gsl = GATS[0:4, bi, 3 * h0 + br:3 * (h0 + nh - 1) + br + 1:3]
                    if normalized:
                        P.cp("dve", rs[0:4, 8:8 + nh], gsl, reads=(r_GAT,), writes=(rsr,))
                    else:
                        P.op("dve", lambda e, rs=rs, ov=ov, nh=nh: e.reciprocal(out=rs[0:4, 0:nh], in_=ov[:, :, 64]), reads=(ores,), writes=(rsr,))
                        P.tt("dve", rs[0:4, 8:8 + nh], rs[0:4, 0:nh], gsl, ALU.mult, reads=(rsr, r_GAT), writes=(rsr,))
                    wv = rs[0:4, 8:8 + nh]
                    wb = bass.AP(wv.tensor, wv.offset, [list(x) for x in wv.ap] + [[0, 64]])
                    if first:
                        P.tt("dve", OSUM[0:4, h0:h0 + nh, :], ov[:, :, 0:64], wb, ALU.mult, reads=(ores, rsr), writes=(r_OSUM,))
                    else:
                        P.tt("dve", OTMP[0:4, 0:nh, :], ov[:, :, 0:64], wb, ALU.mult, reads=(ores, rsr), writes=(r_OTMP,))
                        P.tt("dve", OSUM[0:4, h0:h0 + nh, :], OSUM[0:4, h0:h0 + nh, :], OTMP[0:4, 0:nh, :], ALU.add, reads=(r_OTMP, r_OSUM), writes=(r_OSUM,))

            for bi in range(cfg.get("sbatches", 4)):
                P.memset("pool", R2[:], 0.0, writes=(r_R2,))
                for pg in range(16):
                    pgt, pgr = PG.next()
                    gather4(ccmp_d, bi, pg, pgt, pgr)
                    for blk in range(4):
                        pt, ptr_ = trans_block(pgt, pgr, 128 * blk, None)
                        gA = (blk // 2) * 4 + 2 * (blk % 2)
                        eng = "act" if blk % 2 == 0 else "dve"
                        P.cp(eng, R2[0:64, gA, 16:528], pt[0:64, :], reads=(ptr_,), writes=(r_R2,))
                        P.cp(eng, R2[64:128, gA, 15:527], pt[0:64, :], reads=(ptr_,), writes=(r_R2,))
                        P.cp(eng, R2[64:128, gA + 1, 15:527], pt[64:128, :], reads=(ptr_,), writes=(r_R2,))
                        P.cp(eng, R2[0:64, gA + 1, 16:528], pt[64:128, :], reads=(ptr_,), writes=(r_R2,))
                    compress_block(pg, KcTs, r_KcTs, VcTs, r_VcTs, XG, TG, GG, r_XG, r_TG, r_GG)
                for cc in range(4):
                    for g in range(4):
                        pt, ptr_ = PTR.next()
                        P.tr(pt[:, 0:64], VcTs[0:64, g, 128 * cc + 2:128 * cc + 130], ident[0:64, 0:64], reads=(r_VcTs, r_ident), writes=(ptr_,))
                        P.cp("dve", Vcs[:, cc, g, :], pt[:, 0:64], reads=(ptr_,), writes=(r_Vcs,))
                for g in range(4):
                    P.cp("dve", QZ[0:64, g, 16 * g:16 * g + 16].rearrange("p (r q) -> p r q", q=4), QBs[0:64, 4 * g:4 * g + 4, 4 * bi:4 * bi + 4],
                         reads=(r_QBs,), writes=(r_QZ,))
                pc, pcr = PDs.next()
                for g in range(4):
                    P.mm(pc[0:64, 0:512], QZ[0:64, g, :], KcTs[0:64, g, 2:514], start=(g == 0), stop=(g == 3), reads=(r_QZ, r_KcTs), writes=(pcr,))
                P.stt(SCs[:, 0:511], CPS[:, 0:511], SLC[:, 0:1], pc[0:64, 0:511], ALU.mult, ALU.add, reads=(pcr, r_TS), writes=(r_SCs,))
                rs, rsr = RSs.next()
                P.act(ECs[:, 0:511], SCs[:, 0:511], AF.Exp, reads=(r_SCs, r_TS), writes=(r_ECs, rsr), bias=NQB[:, 0:1], accum_out=rs[:, 0:1])
                P.ts("dve", rs[:, 1:2], rs[:, 0:1], 1e-30, ALU.add, reads=(rsr,), writes=(rsr,))
                P.op("dve", lambda e, rs=rs: e.reciprocal(out=rs[:, 2:3], in_=rs[:, 1:2]), reads=(rsr,), writes=(rsr,))
                P.ts("dve", P32[:, 0:511], ECs[:, 0:511], rs[:, 2:3], ALU.mult, reads=(r_ECs, rsr), writes=(r_P32,))
                P.cp("dve", PBs[:], P32[:], reads=(r_P32,), writes=(r_PBs,))
                pi, pir = PDs.next()
                P.mm(pi[0:64, 0:512], SELG, P32[:], start=True, stop=True, reads=(r_TS, r_P32), writes=(pir,))
                P.op("dve", lambda e, pi=pi: e.tensor_reduce(out=I64s[:], in_=pi[0:64, 0:512].rearrange("p (j k) -> p j k", k=4), axis=AX.X, op=ALU.add),
                     reads=(pir,), writes=(r_I64,))
                P.tt("dve", I64s[:, 1:128], I64s[:, 1:128], pi[0:64, 3:508:4], ALU.add, reads=(pir, r_I64), writes=(r_I64,))
                P.tt("dve", SCOs[:], I64s[:], FBS, ALU.add, reads=(r_I64, r_TS), writes=(r_SCO,))
                P.op("dve", lambda e: e.max(out=M8s[:, 0:8], in_=SCOs[:]), reads=(r_SCO,), writes=(r_M8,))
                P.op("dve", lambda e: e.match_replace(out=WRKs[:], in_to_replace=M8s[:, 0:8], in_values=SCOs[:], imm_value=-3.0e38),
                     reads=(r_SCO, r_M8), writes=(r_WRK,))
                P.op("dve", lambda e: e.max(out=M8s[:, 8:16], in_=WRKs[:]), reads=(r_WRK,), writes=(r_M8,))
                P.ts("dve", SELs[:], SCOs[:], M8s[:, 14:15], ALU.is_lt, -BIG, ALU.mult, reads=(r_SCO, r_M8), writes=(r_SEL,))
                pt, ptr_ = PTR.next()
                P.tr(pt[:, 0:64], SELs[:], ident[0:64, 0:64], reads=(r_SEL, r_ident), writes=(ptr_,))
                P.cp("act", MBT[:], pt[:, 0:64], reads=(ptr_,), writes=(r_MBT,))
                pt, ptr_ = PTR.next()
                for cc in range(4):
                    P.tr(pt[:, cc * 64:(cc + 1) * 64], PBs[:, 128 * cc:128 * cc + 128], ident[0:64, 0:64], reads=(r_PBs, r_ident), writes=(ptr_,), signal=(cc == 3))
                P.cp("act", PcTs[:], pt[:, 0:256].rearrange("p (c r) -> p c r", r=64), reads=(ptr_,), writes=(r_PcT,))
                ff = [True, True, True]
                for cc in range(4):
                    for h in range(16):
                        o_, ores, b = oa(h)
                        P.mm(o_[:, 0:64], PcTs[:, cc, 4 * h:4 * h + 4], Vcs[:, cc, h // 4, :], start=ff[b], stop=(cc == 3), reads=(r_PcT, r_Vcs), writes=(ores,),
                             signal=(h % 6 == 5 or h == 15), sgc=True)
                        ff[b] = False
                combine(bi, 0, True, True)
                if cfg.get("dbg") and bi == 0:
                    DV = A("DV", [128, 1024], F32)
                    DPT = A("DPT", [128, 256], F32)
                    r_DV = Res()
                    P.dma("pool", dbg_p, P32[:], reads=(r_P32,))
                    P.cp("dve", DV[:], Vcs[:].rearrange("p a b d -> p (a b d)"), reads=(r_Vcs,), writes=(r_DV,))
                    P.dma("pool", dbg_v, DV[:], reads=(r_DV,))
                    P.cp("dve", DPT[:], PcTs[:].rearrange("p a r -> p (a r)"), reads=(r_PcT,), writes=(r_DV,))
                    P.dma("pool", dbg_pt, DPT[:], reads=(r_DV,))
                if cfg.get("dbg"):
                    P.dma("pool", dbg_d[bi, 0].rearrange("q (h d) -> q h d", d=64), OSUM[:], reads=(r_OSUM,))
                ff = [True, True, True]
                for pg in range(16):
                    pgt, pgr = PG.next()
                    gather4(csel_d, bi, pg, pgt, pgr)
                    vp, vpr = VP.next()
                    P.cp("pool", vp[:, :, :, 0:64], pgt[:, :, 256:512].rearrange("p s (g d) -> p s g d", d=64), reads=(pgr,), writes=(vpr,))
                    kt, ktr = KT.next()
                    for blk in range(2):
                        pt, ptr_ = trans_block(pgt, pgr, 128 * blk, None)
                        eng = "act" if blk == 0 else "dve"
                        P.cp(eng, kt[0:64, 2 * blk, :], pt[0:64, :], reads=(ptr_,), writes=(ktr,))
                        P.cp(eng, kt[0:64, 2 * blk + 1, :], pt[64:128, :], reads=(ptr_,), writes=(ktr,))
                    attn_chunks(bi, kt, ktr, vp, vpr, 4,
                                lambda cs, pg=pg: (EFULL[:, (4 * pg + cs) * 128:(4 * pg + cs + 1) * 128], MBT[:], r_MBT),
                                lambda cs, pg=pg: B2S[:, 4 * pg + cs, :], ff)
                new_chunk(bi, KNS, 0, ff)
                combine(bi, 1, False, False)
                if cfg.get("dbg"):
                    P.dma("pool", dbg_d[bi, 1].rearrange("q (h d) -> q h d", d=64), OSUM[:], reads=(r_OSUM,))
                ff = [True, True, True]
                pgt, pgr = PG.next()
                for s_ in range(4):
                    P.dma("pool", pgt[:, s_, :], cwin_d[bi, 128 * s_:128 * s_ + 128, :], writes=(pgr,))
                vp, vpr = VP.next()
                P.cp("pool", vp[:, :, :, 0:64], pgt[:, :, 256:512].rearrange("p s (g d) -> p s g d", d=64), reads=(pgr,), writes=(vpr,))
                kt, ktr = KT.next()
                for blk in range(2):
                    pt, ptr_ = trans_block(pgt, pgr, 128 * blk, None)
                    eng = "act" if blk == 0 else "dve"
                    P.cp(eng, kt[0:64, 2 * blk, :], pt[0:64, :], reads=(ptr_,), writes=(ktr,))
                    P.cp(eng, kt[0:64, 2 * blk + 1, :], pt[64:128, :], reads=(ptr_,), writes=(ktr,))
                attn_chunks(bi, kt, ktr, vp, vpr, 4,
                            lambda cs: (ident[:], WM0, r_TS) if cs == 0 else None,
                            lambda cs: B2W[:, cs, :], ff)
                new_chunk(bi, KNW, 1, ff)
                combine(bi, 2, False, False)
                if cfg.get("dbg"):
                    P.dma("pool", dbg_d[bi, 2].rearrange("q (h d) -> q h d", d=64), OSUM[:], reads=(r_OSUM,))
                P.cp("dve", OBs[:], OSUM[:].rearrange("p h d -> p (h d)"), reads=(r_OSUM,), writes=(r_OBs,))
                pt, ptr_ = PTR.next()
                for kc in range(8):
                    P.tr(pt[:, kc * 4:kc * 4 + 4], OBs[0:4, kc * 128:(kc + 1) * 128], ident[0:4, 0:4], reads=(r_OBs, r_ident), writes=(ptr_,), signal=(kc == 7))
                P.cp("act", oT[:, :, 4 * bi:4 * bi + 4], pt[:, 0:32].rearrange("p (k t) -> p k t", t=4), reads=(ptr_,), writes=(r_oT,))
            for c in range(8):
                w, wr = P.slab(s_d2[:, 512 * c:512 * c + 512], 8, 512)
                if S.dry:
                    continue
                pa, par = PD.next()
                proj(pa[:, 0:16], par, [w[:, k, 0:128] for k in range(8)], rh, (r_hT, wr))
                proj(pa[:, 16:32], par, [w[:, k, 128:256] for k in range(8)], rh, (r_hT, wr))
                proj(pa[:, 32:48], par, [w[:, k, 256:384] for k in range(8)], [YA[:, k, :] for k in range(8)], (r_YA, wr))
                proj(pa[:, 48:64], par, [w[:, k, 384:512] for k in range(8)], [oT[:, k, :] for k in range(8)], (r_oT, wr))
                sigmoid_from(SG[:, 0:32], pa[:, 0:32], None, (par,), (r_SG,), None)
                P.tt("dve", SG[:, 0:32], SG[:, 0:32], pa[:, 32:64], ALU.mult, reads=(r_SG, par), writes=(r_SG,))
                P.tt("pool", zT[:, c, :], SG[:, 0:16], SG[:, 16:32], ALU.add, reads=(r_SG,), writes=(r_zT,))
            for i in range(2):
                w, wr = P.slab(s_out[:, 512 * i:512 * i + 512], 8, 512)
                if S.dry:
                    continue
                pd, pdr = PD.next()
                proj(pd[0:16, :], pdr, [zT[:, k, :] for k in range(8)], [w[:, k, :] for k in range(8)], (r_zT, wr))
                tm, tmr = TM.next()
                P.tt("dve", tm[0:16, :], pd[0:16, :], msec("G1")[0:16, 512 * i:512 * i + 512], ALU.mult, reads=(pdr, r_MOD), writes=(tmr,))
                P.tt("pool", XS[0:16, 512 * i:512 * i + 512], tm[0:16, :], XS[0:16, 512 * i:512 * i + 512], ALU.add, reads=(tmr, r_XS), writes=(r_XS,))
            if not S.dry:
                norm_mod(XS[0:16, :], r_XS, 16, "M2", "SH2", (hT, r_hT), 0)
            for fc in range(NFC):
                w, wr = P.slab(s_gu[:, 256 * fc:256 * fc + 256], 8, 256)
                if S.dry:
                    continue
                pd, pdr = PD.next()
                proj(pd[:, 0:16], pdr, [w[:, k, 0:128] for k in range(8)], rh, (r_hT, wr))
                proj(pd[:, 16:32], pdr, [w[:, k, 128:256] for k in range(8)], rh, (r_hT, wr))
                tm, tmr = TM.next()
                sigmoid_from(tm[:, 0:16], pd[:, 0:16], None, (pdr,), (tmr,), None)
                P.tt("dve", tm[:, 0:16], tm[:, 0:16], pd[:, 0:16], ALU.mult, reads=(tmr, pdr), writes=(tmr,))
                P.tt("dve", aT[:, fc, :], tm[:, 0:16], pd[:, 16:32], ALU.mult, reads=(tmr, pdr), writes=(r_aT,))
            for i in range(2):
                pd, pdr = None, None
                for s_ in range(3):
                    nk = 8 if s_ < 2 else 6
                    w, wr = P.slab(s_down[1024 * s_:1024 * s_ + 128 * nk, 512 * i:512 * i + 512], nk, 512)
                    if S.dry:
                        continue
                    if pd is None:
                        pd, pdr = PD.next()
                    for k in range(nk):
                        fc = 8 * s_ + k
                        P.mm(pd[0:16, :], aT[:, fc, :], w[:, k, :], start=(fc == 0), stop=(fc == NFC - 1), reads=(r_aT, wr), writes=(pdr,), signal=(k == nk - 1))
                if S.dry:
                    continue
                tm, tmr = TM.next()
                P.tt("dve", tm[0:16, :], pd[0:16, :], msec("G2")[0:16, 512 * i:512 * i + 512], ALU.mult, reads=(pdr, r_MOD), writes=(tmr,))
                P.tt("pool", XS[0:16, 512 * i:512 * i + 512], tm[0:16, :], XS[0:16, 512 * i:512 * i + 512], ALU.add, reads=(tmr, r_XS), writes=(r_XS,))
            if not S.dry:
                nf, nfr = xk.next()
                P.dma("pool", nf[:], norms[2:3, :].to_broadcast((128, D)), writes=(nfr,))
                rs, rsr = rstd_of(XS[0:16, :], r_XS, 16)
                yt, ytr = xk.next()
                P.stt(yt[0:16, :], XS[0:16, :], rs, nf[0:16, :], ALU.mult, ALU.mult, reads=(r_XS, rsr, nfr), writes=(ytr,))
                P.dma("pool", ys_d, yt[0:16, :], reads=(ytr,))
                S.barrier()

    def body():
        nonlocal KE, VS, KW, VW, KcT, VcT, Vc, CONVO
        stop = cfg.get("stop", 9)
        pes = ExitStack()
        KE = P.sb(pes, "KE", [128, 4, T], BF16)
        VS = P.sb(pes, "VS", [128, 32, 4, 66], BF16)
        KW = P.sb(pes, "KW", [128, 4, 1024], BF16)
        VW = P.sb(pes, "VW", [128, 8, 4, 66], BF16)
        KcT = P.sb(pes, "KcT", [64, 4, 260], BF16)
        VcT = P.sb(pes, "VcT", [64, 4, 260], BF16)
        Vc = P.sb(pes, "Vc", [128, 2, 4, 64], BF16)
        CONVO = P.sb(pes, "CONVO", [128, 8, 2], F32)
        for g in range(4):
            P.dma("sp", KE[64:128, g, :], t_E, writes=(r_KE[g],))
        P.memset("pool", VS[:, :, :, 64:66], 1.0, writes=(r_VS,))
        P.memset("pool", VW[:, :, :, 64:66], 1.0, writes=(r_VW,))
        P.memset("pool", KW[64:128, :, :], 0.0, writes=(r_KW,))
        P.memset("pool", KW[64:65, :, :], 1.0, writes=(r_KW,))
        P.memset("pool", KcT[:], 0.0, writes=(r_Kc,))
        P.memset("pool", VcT[:], 0.0, writes=(r_VcT,))
        P.memset("pool", Vc[:], 0.0, writes=(r_Vc,))
        P.memset("pool", CONVO[:], 0.0, writes=(r_CONVO,))
        if stop >= 2:
            compute_mod(0)
        for m in range(NU):
            if stop >= 3:
                kv_pass(m)
            if stop >= 4:
                own_unit(m)
        if not S.dry:
            with nc.allow_non_contiguous_dma(reason="tiny conv state"):
                for c in range(8):
                    P.dma("pool", oconv[:, c * 128:(c + 1) * 128].rearrange("k p -> p k"), CONVO[:, c, :], reads=(r_CONVO,), allow_slow_non_contiguous=True)
        S.barrier()
        pes.close()
        if cfg.get("sample", True):
            sample_phase()

    return P, body


def _lay(items):
    d, off = {}, 0
    for n, sz in items:
        d[n] = (off, sz)
        off += sz
    return d, off


TF_OFF, TF_N = _lay([("QPOS", 16), ("CI31", 256), ("CPOS", 256), ("QSB", 256), ("POSQ", 256), ("ALB", 576), ("HF", 32), ("EPS", 8)])
TB_OFF, TB_N = _lay([("DT", 512), ("WT", 1536), ("FBT", 1024)])


TSF_OFF, TSF_N = _lay([("SLC", 8), ("NQB", 8), ("CPS", 512), ("FBS", 128), ("SELG", 64), ("L2", 128), ("B2S", 4096), ("B2W", 256), ("NEWB", 64), ("PIO", 8)])
TSB_OFF, TSB_N = _lay([("WM0", 64), ("EFULL", PAST)])


def make_sample_tables():
    tf_ = np.zeros((128, TSF_N), np.float64)
    tb_ = np.zeros((128, TSB_N), np.float64)
    rho = np.arange(64)
    hh, qq = rho // 4, rho % 4
    sl = np.array(SLOPES)[hh]

    def put(tab, off, name, arr, rows):
        o, n = off[name]
        tab[0:rows, o:o + n] = np.asarray(arr).reshape(rows, n)
    put(tf_, TSF_OFF, "SLC", np.repeat(sl[:, None], 8, 1), 64)
    put(tf_, TSF_OFF, "NQB", np.repeat((-sl * qq)[:, None], 8, 1), 64)
    c = np.arange(512)
    put(tf_, TSF_OFF, "CPS", np.broadcast_to(16.0 * c + 15.5 - PAST, (64, 512)), 64)
    fbs = np.zeros((64, 128))
    fbs[:, 0] = 1000.0
    fbs[:, 127] = 1000.0
    put(tf_, TSF_OFF, "FBS", fbs, 64)
    selg = ((hh[:, None] // 4 == hh[None, :] // 4) & (qq[:, None] == qq[None, :])).astype(np.float64)
    put(tf_, TSF_OFF, "SELG", selg, 64)
    put(tf_, TSF_OFF, "L2", np.stack([np.arange(128.0), np.ones(128)]), 2)
    ck = np.arange(64)
    b2s = np.stack([np.broadcast_to(sl[None, :], (64, 64)), sl[None, :] * (128.0 * ck[:, None] - PAST - qq[None, :])])
    put(tf_, TSF_OFF, "B2S", b2s, 2)
    cw = np.arange(4)
    b2w = np.stack([np.broadcast_to(sl[None, :], (4, 64)), sl[None, :] * (PAST - 512 + 128.0 * cw[:, None] - PAST - qq[None, :])])
    put(tf_, TSF_OFF, "B2W", b2w, 2)
    j = np.arange(4)[:, None]
    put(tf_, TSF_OFF, "NEWB", np.where(j <= qq[None, :], sl[None, :] * (j - qq[None, :]), -BIG), 4)
    put(tf_, TSF_OFF, "PIO", np.repeat(np.arange(128.0)[:, None], 8, 1), 128)
    pk = np.arange(128)[:, None]
    put(tb_, TSB_OFF, "WM0", np.where(pk <= qq[None, :], -BIG, 0.0), 128)
    put(tb_, TSB_OFF, "EFULL", (np.arange(PAST)[None, :] // 64 == np.arange(128)[:, None]).astype(np.float64), 128)
    return tf_.astype(np.float32), tb_.astype(NPBF)


def make_tables(p):
    Sp = SUBS[p]
    f = np.arange(128)
    tfv = np.zeros((128, TF_N), np.float32)
    tbv = np.zeros((128, TB_N), np.float32)

    def put(tab, off, name, arr):
        o, n = off[name]
        tab[:, o:o + n] = arr.reshape(128, n)
    qpos = np.zeros((128, 8, 2), np.float64)
    for m in range(8):
        for a in range(2):
            qpos[:, m, a] = 512 * m + 128 * Sp[a] + f
    put(tfv, TF_OFF, "QPOS", qpos)
    ci = np.arange(256)
    put(tfv, TF_OFF, "CI31", np.broadcast_to(16.0 * ci + 31, (128, 256)))
    put(tfv, TF_OFF, "CPOS", np.broadcast_to(16.0 * ci + 15.5, (128, 256)))
    sl = np.array(SLOPES, np.float64)
    put(tfv, TF_OFF, "QSB", -qpos[:, :, :, None] * sl[None, None, None, :])
    posq = np.concatenate([128 * Sp[a] + f for a in range(2)]).astype(np.float64)
    put(tfv, TF_OFF, "POSQ", np.broadcast_to(posq, (128, 256)))
    rel = np.arange(36) - 32
    put(tfv, TF_OFF, "ALB", sl[None, :, None] * (128.0 * rel[None, None, :] + f[:, None, None]))
    hf = np.ones((128, 8, 4))
    for a in range(2):
        if Sp[a] == 0:
            hf[:, 0, 2 * a:2 * a + 2] = 0.0
    put(tfv, TF_OFF, "HF", hf)
    put(tfv, TF_OFF, "EPS", np.full((128, 8), EPS))
    pk = f[:, None]
    qf = f[None, :]
    tri_l = np.where(pk <= qf, 0.0, -BIG)
    tri_u = np.where(pk > qf, 0.0, -BIG)
    neg = np.full((128, 128), -BIG)
    zero = np.zeros((128, 128))
    dt = np.zeros((128, 4, 128))
    for i, (a, r) in enumerate(((0, 0), (0, 1), (1, 2), (1, 3))):
        dt[:, i, :] = tri_l if r == Sp[a] else zero
    put(tbv, TB_OFF, "DT", dt)
    wt = np.zeros((128, 12, 128))
    for i in range(12):
        a, rp = (0, i) if i < 6 else (1, i - 6 + 2)
        d = Sp[a] - (rp - 4)
        wt[:, i, :] = neg if (d < 0 or d > 4) else (tri_l if d == 0 else (tri_u if d == 4 else zero))
    put(tbv, TB_OFF, "WT", wt)
    fbt = np.zeros((128, 16, 64))
    j = np.arange(64)[None, :]
    for m in range(8):
        for a in range(2):
            cur = (qpos[:, m, a] // 64)[:, None]
            v = np.where(j > cur, -1e30, np.where((j == 0) | (j == cur) | (j == cur - 1), 1000.0, 0.0))
            fbt[:, 2 * m + a, :] = v
    put(tbv, TB_OFF, "FBT", fbt)
    return tfv, tbv.astype(NPBF)


_CACHE = {}


def get_program():
    key = tuple(sorted(CFG.items()))
    if key not in _CACHE:
        P, body = build_program(CFG)
        P.S.dry = True
        body()
        P.S.dry = False
        body()
        P.S.finish()
        _CACHE[key] = P
    return _CACHE[key]


def kernel(x_prompt, x_sample, c_prompt, c_sample, cache_cmp, cache_sel, cache_win, state_conv, page_table,
           w_ada, b_ada, norm1, w_in, w_conv, b_conv, w_out_conv, pe_cmp, w_phi1, w_phi2, w_o_nsa, w_out,
           norm2, w_gate, w_up, w_down, norm_f):
    A = lambda v: np.ascontiguousarray(np.asarray(v))
    x_prompt, x_sample, c_prompt, c_sample = A(x_prompt), A(x_sample), A(c_prompt), A(c_sample)
    win = A(w_in)[0]
    cols = lambda o, n: win[:, o:o + n]
    BG, CGo, XI, Qo, KC, VCo, KSo, VSo, KWo, VWo, NG, MG = 0, 1024, 2048, 3072, 4096, 4352, 4608, 4864, 5120, 5376, 5632, 5680
    parts = []
    for c in range(8):
        parts += [cols(BG + 128 * c, 128), cols(CGo + 128 * c, 128), cols(XI + 128 * c, 128)]
    parts.append(cols(Qo, 1024))
    for base in (KSo, KWo, KC, VCo):
        for g in range(4):
            parts += [cols(base + 64 * g, 64), cols(base + 64 * g, 64)]
    parts.append(cols(KC, 1536))
    parts += [cols(NG, 48), np.zeros((D, 16), np.float32)]
    w_inx = np.ascontiguousarray(np.concatenate(parts, axis=1))
    assert w_inx.shape == (D, NCX)
    woc, won = A(w_out_conv)[0], A(w_o_nsa)[0]
    parts = []
    for c in range(8):
        parts += [cols(MG + 128 * c, 128), cols(MG + 1024 + 128 * c, 128), woc[:, 128 * c:128 * c + 128], won[:, 128 * c:128 * c + 128]]
    w_d2 = np.ascontiguousarray(np.concatenate(parts, axis=1))
    wg, wu = A(w_gate)[0], A(w_up)[0]
    parts = []
    for fc in range(NFC):
        parts += [wg[:, 128 * fc:128 * fc + 128], wu[:, 128 * fc:128 * fc + 128]]
    w_gu = np.ascontiguousarray(np.concatenate(parts, axis=1))
    wc4 = np.concatenate([A(w_conv)[0], A(b_conv)], axis=0)
    wcv = np.ascontiguousarray(wc4.reshape(4, 8, 128).transpose(2, 1, 0))
    norms = np.ascontiguousarray(np.stack([A(norm1)[0], A(norm2)[0], A(norm_f)], axis=0))
    common = dict(w_inx=w_inx, w_d2=w_d2, w_ada=A(w_ada)[0], b_ada=A(b_ada), norms=norms, w_out=A(w_out)[0], w_gu=w_gu,
                  w_down=A(w_down)[0], w_phi1=A(w_phi1)[0].reshape(4096, 128), w_phi2=A(w_phi2)[0], pe_cmp=A(pe_cmp)[0], wcv=wcv,
                  t_ident=np.eye(128, dtype=np.float32))
    tE = (np.arange(T)[None, :] // 64 == np.arange(64)[:, None]).astype(np.float32).astype(NPBF)
    tabs = [make_tables(p) for p in range(2)]
    if CFG.get("sample", True):
        stab = make_sample_tables()
        ccmp_full = A(cache_cmp)[0].reshape(NPHYS * 128, 512)
        csel_full = A(cache_sel)[0].reshape(NPHYS * 128, 512)
    in_maps = []
    for c in range(8):
        b, p = c // 2, c % 2
        Sp = SUBS[p]
        xo = np.zeros((8, 260, D), np.float32)
        for m in range(8):
            for a in range(2):
                st = 512 * m + 128 * Sp[a]
                xo[m, a * 128:(a + 1) * 128] = x_prompt[b, st:st + 128]
                if st > 0:
                    xo[m, 256 + 2 * a:258 + 2 * a] = x_prompt[b, st - 2:st]
        ct = np.zeros((2, 128, D), np.float32)
        ct[0] = c_prompt[b][None, :]
        for t_ in range(16):
            ct[1, t_] = c_sample[4 * c + t_ // 4]
        mp = dict(common)
        mp.update(xall=x_prompt[b], xown=xo, ctok=ct, t_E=tE, t_f32=tabs[p][0], t_bf=tabs[p][1])
        if CFG.get("sample", True):
            mp.update(xs=np.ascontiguousarray(x_sample[4 * c:4 * c + 4].reshape(16, D)),
                      sconv=np.ascontiguousarray(A(state_conv)[0, 4 * c:4 * c + 4]),
                      cwin=np.ascontiguousarray(A(cache_win)[0, 4 * c:4 * c + 4].reshape(4, 512, 512)),
                      ptab=np.ascontiguousarray(A(page_table)[4 * c:4 * c + 4].astype(np.int32)),
                      ccmp=ccmp_full, csel=csel_full, ts_f32=stab[0], ts_bf=stab[1])
        in_maps.append(mp)
    P = get_program()
    ncores = CFG.get("ncores", 8)
    res = run_bass_kernel_spmd(P.nc, in_maps[:ncores], core_ids=list(range(ncores)))
    R = list(res.results)
    while len(R) < 8:
        R.append(R[0])
    y_prompt = np.zeros((4, T, D), np.float32)
    for c in range(8):
        b, p = c // 2, c % 2
        Sp = SUBS[p]
        for m in range(8):
            for a in range(2):
                st = 512 * m + 128 * Sp[a]
                y_prompt[b, st:st + 128] = R[c]["yown"][m, a * 128:(a + 1) * 128]
    new_cmp_p = np.stack([R[2 * b]["okv0"].reshape(T, 2, 4, 64) for b in range(4)])[None]
    new_sel_p = np.stack([R[2 * b]["okv1"].reshape(T, 2, 4, 64) for b in range(4)])[None]
    new_win_p = np.stack([R[2 * b]["owin"].reshape(512, 2, 4, 64) for b in range(4)])[None]
    new_conv_p = np.stack([R[2 * b]["oconv"] for b in range(4)])[None]
    z = lambda *s: np.zeros(s, np.float32)
    if not CFG.get("sample", True):
        return (y_prompt, z(32, 4, D), new_cmp_p, new_sel_p, new_win_p, new_conv_p,
                z(1, 32, 4, 2, 4, 64), z(1, 32, 4, 2, 4, 64), z(1, 32, 512, 2, 4, 64), z(1, 32, 2, 1024))
    y_sample = np.concatenate([R[c]["ys"].reshape(4, 4, D) for c in range(8)], axis=0)
    new_cmp_s = np.concatenate([R[c]["oskv0"].reshape(4, 4, 2, 4, 64) for c in range(8)], axis=0)[None]
    new_sel_s = np.concatenate([R[c]["oskv1"].reshape(4, 4, 2, 4, 64) for c in range(8)], axis=0)[None]
    new_win_s = np.concatenate([R[c]["owin_s"].reshape(4, 512, 2, 4, 64) for c in range(8)], axis=0)[None]
    new_conv_s = np.concatenate([R[c]["oconv_s"] for c in range(8)], axis=0)[None]
    return (y_prompt, y_sample, new_cmp_p, new_sel_p, new_win_p, new_conv_p, new_cmp_s, new_sel_s, new_win_s, new_conv_s)
```

```python
import numpy as np
import ml_dtypes
from contextlib import ExitStack
import concourse.bass as bass
import concourse.mybir as mybir
from concourse.bass_utils import run_bass_kernel_spmd

F32 = mybir.dt.float32
BF16 = mybir.dt.bfloat16
I32 = mybir.dt.int32
AF = mybir.ActivationFunctionType
ALU = mybir.AluOpType
AX = mybir.AxisListType
NPBF = ml_dtypes.bfloat16

D = 1024
T = 4096
NH = 16
HD = 64
DFF = 2816
NFC = DFF // 128
BIG = 30000.0
EPS = 1e-6
SLOPES = [2.0 ** (-(h + 1) / 2.0) for h in range(16)]
SUBS = ((0, 3), (1, 2))
PAST = 8192
NPG = 64
NPHYS = 2560

CFG = dict(units=8, sample=True, attn=True)


class Eng:
    def __init__(self, name):
        self.name = name
        self.q = []
        self.cnt = 0
        self.sid = None
        self.waited = {}
        self.dsids = []
        self.di = 0


class Res:
    __slots__ = ("w", "r")

    def __init__(self):
        self.w = {}
        self.r = {}


class Sched:
    def __init__(self, nc, es):
        self.nc = nc
        self.sems = []
        self.dry = False
        self.engs = {}
        for n in ("pe", "act", "dve", "pool", "sp"):
            e = Eng(n)
            e.sid = self.newsem(es, "c_" + n)
            self.engs[n] = e
        for n, k in (("sp", 12), ("pool", 12), ("act", 4)):
            self.engs[n].dsids = [self.newsem(es, "d_%s%d" % (n, i)) for i in range(k)]

    def newsem(self, es, name):
        s = es.enter_context(self.nc.semaphore(name))
        self.sems.append(s)
        return len(self.sems) - 1

    def _wait(self, eng, need):
        for sid, v in need.items():
            if eng.waited.get(sid, 0) < v:
                eng.waited[sid] = v
                sem = self.sems[sid]
                eng.q.append(lambda e, sem=sem, v=v: e.wait_ge(sem, v))

    def op(self, en, fn, reads=(), writes=(), signal=True, dma=False):
        if self.dry:
            return
        eng = self.engs[en]
        need = {}

        def add(tok):
            if need.get(tok[0], 0) < tok[1]:
                need[tok[0]] = tok[1]

        me = en + (":dma" if dma else "")
        for r in reads:
            for tok in r.w.values():
                add(tok)
        for w in writes:
            for tok in w.w.values():
                if en != "pe" or dma or tok[2] != me:
                    add(tok)
            for tok in w.r.values():
                if en != "pe" or dma or tok[2] != me:
                    add(tok)
        if dma:
            k = eng.di % len(eng.dsids)
            sid = eng.dsids[k]
            val = 16 * (eng.di // len(eng.dsids) + 1)
            eng.di += 1
            if val > 16:
                add((sid, val - 16, me))
            tok = (sid, val, me)
            self._wait(eng, need)
            sem = self.sems[sid]
            eng.q.append(lambda e, fn=fn, sem=sem: fn(e).then_inc(sem, 16))
        else:
            self._wait(eng, need)
            if signal:
                eng.cnt += 1
                tok = (eng.sid, eng.cnt, me)
                sem = self.sems[eng.sid]
                eng.q.append(lambda e, fn=fn, sem=sem: fn(e).then_inc(sem, 1))
            else:
                tok = (eng.sid, eng.cnt + 1, me)
                eng.q.append(lambda e, fn=fn: fn(e))
        for r in reads:
            r.r[tok[0]] = tok
        for w in writes:
            w.w = {tok[0]: tok}
            w.r = {}
        return tok

    def barrier(self):
        if self.dry:
            return
        need = {}
        for e in self.engs.values():
            if e.cnt:
                need[e.sid] = e.cnt
            for k, sid in enumerate(e.dsids):
                n = (e.di - k + len(e.dsids) - 1) // len(e.dsids) if e.di > k else 0
                if n:
                    need[sid] = 16 * n
        for e in self.engs.values():
            self._wait(e, dict(need))

    def finish(self):
        self.barrier()
        nc = self.nc
        q = self.engs
        with nc.Block() as block:
            @block.tensor
            def _(e):
                for f in q["pe"].q:
                    f(e)

            @block.scalar
            def _(e):
                for f in q["act"].q:
                    f(e)

            @block.vector
            def _(e):
                for f in q["dve"].q:
                    f(e)

            @block.gpsimd
            def _(e):
                for f in q["pool"].q:
                    f(e)

            @block.sync
            def _(e):
                for f in q["sp"].q:
                    f(e)


class Ring:
    def __init__(self, aps):
        self.aps = aps
        self.res = [Res() for _ in aps]
        self.i = 0

    def next(self):
        k = self.i % len(self.aps)
        self.i += 1
        return self.aps[k], self.res[k]


class Prog:
    def __init__(self):
        self.nc = bass.Bass("TRN2", target_bir_lowering=False)
        self.es = ExitStack()
        self.S = Sched(self.nc, self.es)
        self.slab_reqs = []
        self.slab_i = 0
        self.slab_issued = 0
        self.uid = 0

    def din(self, name, shape, dt=F32):
        return self.nc.dram_tensor(name, list(shape), dt, kind="ExternalInput").ap()

    def dout(self, name, shape, dt=F32):
        return self.nc.dram_tensor(name, list(shape), dt, kind="ExternalOutput").ap()

    def sb(self, es, name, shape, dt):
        self.uid += 1
        return es.enter_context(self.nc.sbuf_tensor("%s_%d" % (name, self.uid), list(shape), dt))

    def ps(self, es, name, shape, dt):
        self.uid += 1
        return es.enter_context(self.nc.psum_tensor("%s_%d" % (name, self.uid), list(shape), dt))

    def op(self, *a, **k):
        return self.S.op(*a, **k)

    def dma(self, q, out, in_, reads=(), writes=(), **kw):
        self.S.op(q, lambda e, out=out, in_=in_, kw=kw: e.dma_start(out=out, in_=in_, **kw), reads, writes, dma=True)

    def mm(self, out, lhsT, rhs, start, stop, reads=(), writes=(), signal=None, sgc=False):
        if signal is None:
            signal = stop
        self.S.op("pe", lambda e, out=out, lhsT=lhsT, rhs=rhs, start=start, stop=stop, sgc=sgc:
                  e.matmul(out, lhsT=lhsT, rhs=rhs, start=start, stop=stop, skip_group_check=sgc), reads, writes, signal=signal)

    def tr(self, out, in_, ident, reads=(), writes=(), signal=True):
        self.S.op("pe", lambda e, out=out, in_=in_, ident=ident: e.transpose(out, in_, ident), reads, writes, signal=signal)

    def act(self, out, in_, func, reads=(), writes=(), eng="act", **kw):
        self.S.op(eng, lambda e, out=out, in_=in_, func=func, kw=kw: e.activation(out=out, in_=in_, func=func, **kw), reads, writes)

    def tt(self, eng, out, in0, in1, op, reads=(), writes=()):
        self.S.op(eng, lambda e, out=out, in0=in0, in1=in1, op=op: e.tensor_tensor(out=out, in0=in0, in1=in1, op=op), reads, writes)

    def ts(self, eng, out, in0, s1, op0, s2=None, op1=None, reads=(), writes=(), accum=None):
        def f(e, out=out, in0=in0, s1=s1, op0=op0, s2=s2, op1=op1, accum=accum):
            kw = {}
            if op1 is not None:
                kw["op1"] = op1
            if accum is not None:
                kw["accum_out"] = accum
            return e.tensor_scalar(out=out, in0=in0, scalar1=s1, scalar2=s2, op0=op0, **kw)
        self.S.op(eng, f, reads, writes)

    def stt(self, out, in0, scalar, in1, op0, op1, reads=(), writes=()):
        self.S.op("dve", lambda e, out=out, in0=in0, scalar=scalar, in1=in1, op0=op0, op1=op1:
                  e.scalar_tensor_tensor(out=out, in0=in0, scalar=scalar, in1=in1, op0=op0, op1=op1), reads, writes)

    def cp(self, eng, out, in_, reads=(), writes=()):
        if eng == "act":
            self.S.op("act", lambda e, out=out, in_=in_: e.copy(out=out, in_=in_), reads, writes)
        else:
            self.S.op(eng, lambda e, out=out, in_=in_: e.tensor_copy(out=out, in_=in_), reads, writes)

    def memset(self, eng, ap, val, writes=()):
        self.S.op(eng, lambda e, ap=ap, val=val: e.memset(ap, val), (), writes)

    def slab(self, src, nk, ncols):
        if self.S.dry:
            self.slab_reqs.append((src, nk, ncols))
            return None, None
        k = self.slab_i
        self.slab_i += 1
        nb = len(self.slabs)
        while self.slab_issued < min(len(self.slab_reqs), k + nb):
            i = self.slab_issued
            s_, nk_, nc_ = self.slab_reqs[i]
            buf = self.slabs[i % nb]
            dst = buf[:, 0:nk_ * nc_].rearrange("p (k c) -> p k c", c=nc_)
            q = "sp"
            self.dma(q, dst, s_.rearrange("(k p) c -> p k c", p=128), reads=(), writes=(self.slab_res[i % nb],))
            self.slab_issued += 1
        buf = self.slabs[k % nb]
        return buf[:, 0:nk * ncols].rearrange("p (k c) -> p k c", c=ncols), self.slab_res[k % nb]


FM1_0 = 0
Q_0 = 3072
KD_0 = 4096
KV_0 = 6144
G_0 = 7680
NCX = 7744
ND2 = 4096


def build_program(cfg):
    P = Prog()
    nc = P.nc
    es = P.es
    S = P.S
    NU = cfg["units"]

    xall = P.din("xall", [T, D])
    xown = P.din("xown", [8, 260, D])
    ctok = P.din("ctok", [2, 128, D])
    w_inx = P.din("w_inx", [D, NCX])
    w_d2 = P.din("w_d2", [D, ND2])
    w_ada = P.din("w_ada", [D, 6 * D])
    b_ada = P.din("b_ada", [1, 6 * D])
    norms = P.din("norms", [3, D])
    w_out = P.din("w_out", [D, D])
    w_gu = P.din("w_gu", [D, 2 * DFF])
    w_down = P.din("w_down", [DFF, D])
    w_phi1 = P.din("w_phi1", [2 * 2048, 128])
    w_phi2 = P.din("w_phi2", [2, 128, 64])
    pe_cmp = P.din("pe_cmp", [2, 32, 64])
    wcv = P.din("wcv", [128, 8, 4])
    t_ident = P.din("t_ident", [128, 128])
    t_E = P.din("t_E", [64, T], BF16)
    t_f32 = P.din("t_f32", [128, TF_N])
    t_bf = P.din("t_bf", [128, TB_N], BF16)
    yown = P.dout("yown", [8, 256, D])
    okv = [P.dout("okv%d" % i, [T, 512]) for i in range(2)]
    owin = P.dout("owin", [512, 512])
    oconv = P.dout("oconv", [2, D])
    if cfg.get("sample", True):
        xs_d = P.din("xs", [16, D])
        sconv_d = P.din("sconv", [4, 2, D])
        cwin_d = P.din("cwin", [4, 512, 512])
        ptab_d = P.din("ptab", [4, NPG], I32)
        ccmp_d = P.din("ccmp", [NPHYS * 128, 512])
        csel_d = P.din("csel", [NPHYS * 128, 512])
        ts_f32 = P.din("ts_f32", [128, TSF_N])
        ts_bf = P.din("ts_bf", [128, TSB_N], BF16)
        ys_d = P.dout("ys", [16, D])
        oskv = [P.dout("oskv%d" % i, [16, 512]) for i in range(2)]
        owin_s = P.dout("owin_s", [4, 512, 512])
        oconv_s = P.dout("oconv_s", [4, 2, D])
        if cfg.get("dbg"):
            dbg_d = P.dout("dbg", [4, 3, 4, D])
    def scr(name, shape):
        return nc.dram_tensor(name, list(shape), BF16).ap()
    s_inx = scr("s_inx", [D, NCX])
    s_d2 = scr("s_d2", [D, ND2])
    s_ada = scr("s_ada", [D, 6 * D])
    s_out = scr("s_out", [D, D])
    s_gu = scr("s_gu", [D, 2 * DFF])
    s_down = scr("s_down", [DFF, D])
    s_phi1 = scr("s_phi1", [2 * 2048, 128])

    def sb(name, shape, dt):
        return P.sb(es, name, shape, dt)
    ident = sb("ident", [128, 128], BF16)
    TF = sb("TF", [128, TF_N], F32)
    TB = sb("TB", [128, TB_N], BF16)
    MOD = sb("MOD", [128, 6 * D], F32)
    KE = VS = KW = VW = KcT = VcT = Vc = CONVO = None
    R2 = sb("R2", [128, 8, 562], BF16)
    W2 = sb("W2", [128, 2, 64], BF16)
    PET = sb("PET", [128, 2, 16], BF16)
    PEB = sb("PEB", [128, 2], F32)
    WCV = sb("WCV", [128, 8, 4], F32)
    NSL = 2
    P.slabs = [sb("slab%d" % i, [128, 4096], BF16) for i in range(NSL)]
    P.slab_res = [Res() for _ in range(NSL)]
    xk = Ring([sb("xk%d" % i, [128, D], F32) for i in range(2)])
    hb = Ring([sb("hb%d" % i, [128, D], BF16) for i in range(2)])
    sm = Ring([sb("sm%d" % i, [128, 8], F32) for i in range(6)])
    PD = Ring([P.ps(es, "PD%d" % i, [128, 512], F32)[:] for i in range(3)])
    PTRt = P.ps(es, "PTR", [128, 1024], BF16)
    PTR = Ring([PTRt[:]])
    PSt = [P.ps(es, "PS%d" % i, [128, 512], F32) for i in range(2)]
    PS = Ring([PSt[0][:, 0:256], PSt[1][:, 0:256]])
    POt = [P.ps(es, "PO%d" % i, [128, 512], F32) for i in range(2)]
    PO = Ring([POt[0][:, 0:130], POt[1][:, 0:130]])
    r_ident, r_TF, r_TB, r_MOD, r_NFR = Res(), Res(), Res(), Res(), Res()
    r_KE = [Res() for _ in range(4)]
    r_VS, r_KW, r_VW, r_Kc, r_VcT, r_Vc, r_R2, r_W2, r_PEB, r_WCV, r_CONVO = (Res() for _ in range(11))

    def tf(name):
        o, n = TF_OFF[name]
        return TF[:, o:o + n]

    def tb(name):
        o, n = TB_OFF[name]
        return TB[:, o:o + n]
    QPOS = tf("QPOS").rearrange("p (m a) -> p m a", a=2)
    CI31 = tf("CI31")
    CPOS = tf("CPOS")
    QSB = tf("QSB").rearrange("p (m a h) -> p m a h", a=2, h=16)
    FBT = tb("FBT").rearrange("p (u j) -> p u j", j=64)
    POSQ = tf("POSQ")
    ALB = tf("ALB").rearrange("p (h r) -> p h r", r=36)
    HF = tf("HF").rearrange("p (m k) -> p m k", k=4)
    DT = tb("DT").rearrange("p (i q) -> p i q", q=128)
    WT = tb("WT").rearrange("p (i q) -> p i q", q=128)

    MSEC = dict(SH1=0, M1=1, G1=2, SH2=3, M2=4, G2=5)

    def msec(name):
        k = MSEC[name]
        return MOD[:, k * D:(k + 1) * D]

    def precast(src, dst, rows, cols):
        for r0 in range(0, rows, 128):
            for c0 in range(0, cols, 4096):
                cw = min(4096, cols - c0)
                k = P.slab_i
                P.slab_i += 1
                buf, rr = P.slabs[k % NSL], P.slab_res[k % NSL]
                P.dma("pool", buf[:, 0:cw], src[r0:r0 + 128, c0:c0 + cw], writes=(rr,), max_dma_last_dim=4096)
                P.dma("sp", dst[r0:r0 + 128, c0:c0 + cw], buf[:, 0:cw], reads=(rr,))

    def rstd_of(xt, xr, n):
        junk, jr = hb.next()
        s1, s1r = sm.next()
        P.act(junk[0:n, :], xt, AF.Square, reads=(xr,), writes=(jr, s1r), accum_out=s1[0:n, 0:1])
        P.act(s1[0:n, 1:2], s1[0:n, 0:1], AF.Ln, reads=(s1r,), writes=(s1r,), scale=1.0 / D, bias=TF[0:n, TF_OFF["EPS"][0]:TF_OFF["EPS"][0] + 1])
        P.act(s1[0:n, 2:3], s1[0:n, 1:2], AF.Exp, reads=(s1r,), writes=(s1r,), scale=-0.5)
        return s1[0:n, 2:3], s1r

    def norm_mod(xt, xr, n, Mn, SHn, dst, col):
        rs, rsr = rstd_of(xt, xr, n)
        tmp, tr_ = xk.next()
        P.stt(tmp[0:n, :], xt, rs, msec(Mn)[0:n, :], ALU.mult, ALU.mult, reads=(xr, rsr, r_MOD), writes=(tr_,))
        h, hr = hb.next()
        P.tt("pool", h[0:n, :], tmp[0:n, :], msec(SHn)[0:n, :], ALU.add, reads=(tr_, r_MOD), writes=(hr,))
        dstap, dstres = dst
        pt, ptr_ = PTR.next()
        for kc in range(8):
            P.tr(pt[:, kc * 128:kc * 128 + n], h[0:n, kc * 128:(kc + 1) * 128], ident[0:n, 0:n],
                 reads=(hr, r_ident), writes=(ptr_,), signal=(kc == 7))
        ptv = pt.rearrange("p (j t) -> p j t", t=128)
        P.nm_i = getattr(P, "nm_i", 0) + 1
        P.cp("act" if P.nm_i % 2 == 0 else "dve", dstap[:, 0:8, col:col + n], ptv[:, 0:8, 0:n], reads=(ptr_,), writes=(dstres,))

    def proj(out_ps, out_res, lhs_list, rhs_list, reads):
        n = len(lhs_list)
        for k in range(n):
            P.mm(out_ps, lhs_list[k], rhs_list[k], start=(k == 0), stop=(k == n - 1), reads=reads, writes=(out_res,))

    def sigmoid_from(out, src, shape_cols, reads, writes, tmpring):
        P.act(out, src, AF.Exp, reads=reads, writes=writes, scale=-1.0)
        P.ts("dve", out, out, 1.0, ALU.add, reads=writes, writes=writes)
        P.op("dve", lambda e, out=out: e.reciprocal(out=out, in_=out), reads=writes, writes=writes)

    P.dma("pool", ident[:], t_ident, writes=(r_ident,))
    P.dma("sp", TF[:], t_f32, writes=(r_TF,))
    P.dma("sp", TB[:], t_bf, writes=(r_TB,))
    P.dma("sp", WCV[:], wcv, writes=(r_WCV,))
    P.dma("pool", W2[:], w_phi2.rearrange("k p d -> p k d"), writes=(r_W2,))
    with nc.allow_non_contiguous_dma(reason="tiny pe table"):
        P.dma("pool", PET[:], pe_cmp.rearrange("k (j s) d -> (s d) k j", s=2), writes=(r_TB,), allow_slow_non_contiguous=True)
    P.memset("pool", R2[:], 0.0, writes=(r_R2,))
    if not S.dry:
        precast(w_phi1, s_phi1, 4096, 128)
        precast(w_ada, s_ada, D, 6 * D)
        precast(w_inx, s_inx, D, NCX)
        precast(w_d2, s_d2, D, ND2)
        precast(w_out, s_out, D, D)
        precast(w_gu, s_gu, D, 2 * DFF)
        precast(w_down, s_down, DFF, D)
        S.barrier()
        P.slab_i = 0
        P.slab_res = [Res() for _ in range(NSL)]

    def compute_mod(kind):
        with ExitStack() as ph:
            cT = P.sb(ph, "cT", [128, 8, 128], BF16)
            bt = Ring([P.sb(ph, "bt%d" % i, [128, 512], F32) for i in range(2)])
            r_cT = Res()
            ct, cr = xk.next()
            P.dma("pool", ct[:], ctok[kind], writes=(cr,))
            e1, e1r = xk.next()
            sigmoid_from(e1[:], ct[:], None, (cr,), (e1r,), None)
            sl, slr = hb.next()
            P.tt("dve", sl[:], ct[:], e1[:], ALU.mult, reads=(cr, e1r), writes=(slr,))
            mstop = cfg.get("mstop", 9)
            if mstop < 2:
                S.barrier()
                return
            pt, ptr_ = PTR.next()
            for kc in range(8):
                P.tr(pt[:, kc * 128:(kc + 1) * 128], sl[:, kc * 128:(kc + 1) * 128], ident[:], reads=(slr, r_ident), writes=(ptr_,), signal=(kc == 7))
            P.cp("act", cT[:], pt.rearrange("p (j t) -> p j t", t=128), reads=(ptr_,), writes=(r_cT,))
            if mstop < 3:
                S.barrier()
                return
            for n in range(12 if mstop >= 4 else 1):
                w, wr = P.slab(s_ada[:, n * 512:(n + 1) * 512], 8, 512)
                b_, br = bt.next()
                P.dma("pool", b_[:], b_ada[0:1, n * 512:(n + 1) * 512].to_broadcast((128, 512)), writes=(br,))
                if S.dry:
                    continue
                pd, pdr = PD.next()
                proj(pd, pdr, [cT[:, k, :] for k in range(8)], [w[:, k, :] for k in range(8)], (r_cT, wr))
                P.tt("dve", MOD[:, n * 512:(n + 1) * 512], pd, b_[:], ALU.add, reads=(pdr, br), writes=(r_MOD,))
            if mstop < 5:
                S.barrier()
                return
            for sec, nrow in ((1, 0), (4, 1)):
                nr, nrr = xk.next()
                P.dma("pool", nr[:], norms[nrow:nrow + 1, :].to_broadcast((128, D)), writes=(nrr,))
                P.stt(MOD[:, sec * D:(sec + 1) * D], MOD[:, sec * D:(sec + 1) * D], 1.0, nr[:], ALU.add, ALU.mult,
                      reads=(r_MOD, nrr), writes=(r_MOD,))
            S.barrier()

    def compress_block(sb_, KcTd, rKc, VcTd, rVc, XG, TG, GG, r_XG, r_TG, r_GG, R2x=None, rR2=None, nb=34, ntok=512):
        if R2x is None:
            R2x, rR2 = R2, r_R2
        pc, pcr = PD.next()
        for kv in range(2):
            w1, w1r = P.slab(s_phi1[kv * 2048:(kv + 1) * 2048, :], 16, 128)
            if S.dry:
                continue
            for g in range(4):
                gi = kv * 4 + g
                for j in range(16):
                    P.mm(pc[:, gi * nb:(gi + 1) * nb], w1[:, j, :], R2x[:, gi, 2 * j:2 * j + 16 * (nb - 1) + 1:16],
                         start=(j == 0), stop=(j == 15), reads=(w1r, rR2), writes=(pcr,))
            if not hasattr(P, "peb_done"):
                pb_, pbr = PD.next()
                for j in range(16):
                    P.mm(pb_[:, 0:1], w1[:, j, :], PET[:, kv, j:j + 1], start=(j == 0), stop=(j == 15), reads=(w1r, r_TB), writes=(pbr,))
                P.cp("dve", PEB[:, kv:kv + 1], pb_[:, 0:1], reads=(pbr,), writes=(r_PEB,))
        if S.dry:
            return
        P.peb_done = True
        for kv in range(2):
            P.ts("dve", XG[:, kv * 4 * nb:(kv + 1) * 4 * nb], pc[:, kv * 4 * nb:(kv + 1) * 4 * nb], PEB[:, kv:kv + 1], ALU.add,
                 reads=(pcr, r_PEB), writes=(r_XG,))
        P.tt("dve", TG[:], XG[:], XG[:], ALU.mult, reads=(r_XG,), writes=(r_TG,))
        P.ts("dve", TG[:], TG[:], 0.044715, ALU.mult, 1.0, ALU.add, reads=(r_TG,), writes=(r_TG,))
        P.tt("dve", TG[:], TG[:], XG[:], ALU.mult, reads=(r_TG, r_XG), writes=(r_TG,))
        P.act(TG[:], TG[:], AF.Exp, reads=(r_TG,), writes=(r_TG,), scale=-1.5957691216057308)
        P.ts("dve", TG[:], TG[:], 1.0, ALU.add, reads=(r_TG,), writes=(r_TG,))
        P.op("dve", lambda e: e.reciprocal(out=TG[:], in_=TG[:]), reads=(r_TG,), writes=(r_TG,))
        P.tt("dve", GG[:], XG[:], TG[:], ALU.mult, reads=(r_TG, r_XG), writes=(r_GG,))
        p2, p2r = PD.next()
        for gi in range(8):
            P.mm(p2[0:64, gi * nb:(gi + 1) * nb], W2[:, gi // 4, :], GG[:, gi * nb:(gi + 1) * nb], start=True, stop=True,
                 reads=(r_W2, r_GG), writes=(p2r,), signal=(gi == 7))
        base = (ntok // 16) * sb_
        P.cp("act", KcTd[0:64, :, base + 1:base + 1 + nb], p2[0:64, 0:4 * nb].rearrange("p (g i) -> p g i", i=nb), reads=(p2r,), writes=(rKc,))
        P.cp("act", VcTd[0:64, :, base + 1:base + 1 + nb], p2[0:64, 4 * nb:8 * nb].rearrange("p (g i) -> p g i", i=nb), reads=(p2r,), writes=(rVc,))
        P.cp("pool", R2x[:, :, 0:16], R2x[:, :, ntok:ntok + 16], reads=(rR2,), writes=(rR2,))

    def kv_pass(sb_):
        slot = sb_ % 2
        with ExitStack() as ph:
            hT = P.sb(ph, "hTkv", [128, 8, 512], BF16)
            r_hT = Res()
            stage = Ring([P.sb(ph, "stg%d" % i, [128, 512], F32) for i in range(2)])
            XG = P.sb(ph, "XG", [128, 272], F32)
            TG = P.sb(ph, "TG", [128, 272], F32)
            GG = P.sb(ph, "GG", [128, 272], BF16)
            r_XG, r_TG, r_GG = Res(), Res(), Res()
            for tt_ in range(4):
                xt, xr = xk.next()
                P.dma("pool", xt[:], xall[sb_ * 512 + tt_ * 128: sb_ * 512 + tt_ * 128 + 128, :], writes=(xr,))
                norm_mod(xt[:], xr, 128, "M1", "SH1", (hT, r_hT), tt_ * 128)
            kstop = cfg.get("kstop", 9)
            for i in range(8 if kstop >= 2 else 0):
                w, wr = P.slab(s_inx[:, KD_0 + 256 * i: KD_0 + 256 * i + 256], 8, 256)
                if S.dry:
                    continue
                for j in range(2):
                    ti = 2 * i + j
                    kind, g = ti // 4, ti % 4
                    pd, pdr = PD.next()
                    proj(pd, pdr, [w[:, k, j * 128:(j + 1) * 128] for k in range(8)], [hT[:, k, :] for k in range(8)], (r_hT, wr))
                    if kind == 0:
                        P.cp("act", KE[0:64, g, sb_ * 512:(sb_ + 1) * 512], pd[0:64, :], reads=(pdr,), writes=(r_KE[g],))
                    elif kind == 1:
                        P.cp("act", KW[0:64, g, slot * 512:(slot + 1) * 512], pd[0:64, :], reads=(pdr,), writes=(r_KW,))
                    else:
                        gi = (kind - 2) * 4 + g
                        P.cp("dve", R2[0:64, gi, 16:528], pd[0:64, :], reads=(pdr,), writes=(r_R2,))
                        P.cp("act", R2[64:128, gi, 15:527], pd[64:128, :], reads=(pdr,), writes=(r_R2,))
            compress_block(sb_, KcT, r_Kc, VcT, r_VcT, XG, TG, GG, r_XG, r_TG, r_GG)
            if not S.dry:
                for cc in range((32 * sb_ + 31) // 128 + 1):
                    for g in range(4 if kstop >= 3.4 else 0):
                        pt, ptr_ = PTR.next()
                        P.tr(pt[:, 0:64], VcT[0:64, g, 128 * cc + 2:128 * cc + 130], ident[0:64, 0:64], reads=(r_VcT, r_ident), writes=(ptr_,))
                        P.cp("dve", Vc[:, cc, g, :], pt[:, 0:64], reads=(ptr_,), writes=(r_Vc,))
            for i in range(3 if kstop >= 5 else 0):
                w, wr = P.slab(s_inx[:, KV_0 + 512 * i: KV_0 + 512 * i + 512], 8, 512)
                if S.dry:
                    continue
                for tt_ in range(4):
                    pd, pdr = PD.next()
                    proj(pd, pdr, [hT[:, k, tt_ * 128:(tt_ + 1) * 128] for k in range(8)], [w[:, k, :] for k in range(8)], (r_hT, wr))
                    st, sr = stage.next()
                    P.cp("act" if tt_ % 2 == 0 else "dve", st[:], pd, reads=(pdr,), writes=(sr,))
                    t0 = sb_ * 512 + tt_ * 128
                    if i < 2:
                        P.dma("pool", okv[i][t0:t0 + 128, :], st[:], reads=(sr,))
                    elif sb_ == 7:
                        P.dma("pool", owin[tt_ * 128:(tt_ + 1) * 128, :], st[:], reads=(sr,))
                    if i == 1:
                        P.cp("pool", VS[:, sb_ * 4 + tt_, :, 0:64], st[:, 256:512].rearrange("p (g d) -> p g d", d=64), reads=(sr,), writes=(r_VS,))
                    elif i == 2:
                        P.cp("pool", VW[:, slot * 4 + tt_, :, 0:64], st[:, 256:512].rearrange("p (g d) -> p g d", d=64), reads=(sr,), writes=(r_VW,))
            S.barrier()

    def attention(m, QB, r_QB, GATES, r_G, OB, r_OB, ph):
        N = 32 * m + 32
        NS = N // 4
        ncc = (N + 127) // 128
        CVB = P.sb(ph, "CVB", [128, 256], BF16)
        SC = Ring([P.sb(ph, "SC%d" % i, [128, 256], F32) for i in range(2)])
        EC = Ring([P.sb(ph, "EC%d" % i, [128, 256], F32) for i in range(2)])
        PB = Ring([P.sb(ph, "PB%d" % i, [128, 256], BF16) for i in range(2)])
        PcT = Ring([P.sb(ph, "PcT%d" % i, [128, 2, 128], BF16) for i in range(2)])
        IMP = Ring([P.sb(ph, "IMP%d" % i, [128, 256], F32) for i in range(2)])
        I64 = P.sb(ph, "I64", [128, 64], F32)
        SCO = P.sb(ph, "SCO", [128, 64], F32)
        WRK = P.sb(ph, "WRK", [128, 64], F32)
        M8 = P.sb(ph, "M8", [128, 16], F32)
        SELB = Ring([P.sb(ph, "SELB%d" % i, [128, 128], BF16) for i in range(2)])
        PT = Ring([P.sb(ph, "PT%d" % i, [128, 256], BF16) for i in range(4)])
        OH = Ring([P.sb(ph, "OH%d" % i, [128, 2, 64], F32) for i in range(2)])
        RS = Ring([P.sb(ph, "RS%d" % i, [128, 4], F32) for i in range(4)])
        r_CVB, r_ZER, r_I64, r_SCO, r_WRK, r_M8 = (Res() for _ in range(6))
        for sbuf_, sres in zip(SELB.aps, SELB.res):
            P.memset("pool", sbuf_[:, 0:64], 0.0, writes=(sres,))
            P.memset("pool", sbuf_[:, 64:128], -BIG, writes=(sres,))
        for a in range(2):
            P.ts("dve", CVB[:, 0:N], CI31[:, 0:N], QPOS[:, m, a:a + 1], ALU.is_gt, -BIG, ALU.mult, reads=(r_TF,), writes=(r_CVB,))
            for g in range(4):
                imp, impr = IMP.next()
                for r in range(4):
                    h = 4 * g + r
                    pc, pcr = PS.next()
                    P.mm(pc[:, 0:N], QB[0:64, h, a * 128:(a + 1) * 128], KcT[0:64, g, 2:2 + N], start=True, stop=False, reads=(r_QB[h], r_Kc), writes=(pcr,))
                    P.mm(pc[:, 0:N], ident[:], CVB[:, 0:N], start=False, stop=True, reads=(r_ident, r_CVB), writes=(pcr,))
                    sc, scr_ = SC.next()
                    P.stt(sc[:, 0:N], CPOS[:, 0:N], SLOPES[h], pc[:, 0:N], ALU.mult, ALU.add, reads=(pcr, r_TF), writes=(scr_,))
                    ec, ecr = EC.next()
                    rs, rsr = RS.next()
                    P.act(ec[:, 0:N], sc[:, 0:N], AF.Exp, reads=(scr_, r_TF), writes=(ecr, rsr), bias=QSB[:, m, a, h:h + 1], accum_out=rs[:, 0:1])
                    P.ts("dve", rs[:, 1:2], rs[:, 0:1], 1e-30, ALU.add, reads=(rsr,), writes=(rsr,))
                    P.op("dve", lambda e, rs=rs: e.reciprocal(out=rs[:, 2:3], in_=rs[:, 1:2]), reads=(rsr,), writes=(rsr,))
                    if r == 0:
                        P.ts("dve", imp[:, 0:N], ec[:, 0:N], rs[:, 2:3], ALU.mult, reads=(ecr, rsr), writes=(impr,))
                    else:
                        P.stt(imp[:, 0:N], ec[:, 0:N], rs[:, 2:3], imp[:, 0:N], ALU.mult, ALU.add, reads=(ecr, rsr, impr), writes=(impr,))
                    pb, pbr = PB.next()
                    P.ts("dve", pb[:, 0:N], ec[:, 0:N], rs[:, 2:3], ALU.mult, reads=(ecr, rsr), writes=(pbr,))
                    pct, pctr = PcT.next()
                    for cc in range(ncc):
                        w_ = min(128, N - 128 * cc)
                        pt, ptr_ = PTR.next()
                        P.tr(pt[0:w_, 0:128], pb[:, 128 * cc:128 * cc + w_], ident[:], reads=(pbr, r_ident), writes=(ptr_,))
                        P.cp("act", pct[0:w_, cc, :], pt[0:w_, 0:128], reads=(ptr_,), writes=(pctr,))
                    po_, por = PO.next()
                    po = po_[:, 0:64]
                    for cc in range(ncc):
                        w_ = min(128, N - 128 * cc)
                        P.mm(po, pct[0:w_, cc, :], Vc[0:w_, cc, g, :], start=(cc == 0), stop=(cc == ncc - 1), reads=(pctr, r_Vc), writes=(por,))
                    P.ts("dve", OB[:, a, 64 * h:64 * h + 64], po, GATES[:, a, 3 * h:3 * h + 1], ALU.mult, reads=(por, r_G), writes=(r_OB,))
                P.op("dve", lambda e, imp=imp: e.tensor_reduce(out=I64[:, 0:NS], in_=imp[:, 0:N].rearrange("p (j k) -> p j k", k=4), axis=AX.X, op=ALU.add),
                     reads=(impr,), writes=(r_I64,))
                if NS > 1:
                    P.tt("dve", I64[:, 1:NS], I64[:, 1:NS], imp[:, 3:N - 4:4], ALU.add, reads=(impr, r_I64), writes=(r_I64,))
                P.tt("dve", SCO[:, 0:NS], I64[:, 0:NS], FBT[:, 2 * m + a, 0:NS], ALU.add, reads=(r_I64, r_TB), writes=(r_SCO,))
                sel, selr = SELB.next()
                if NS >= 16:
                    P.op("dve", lambda e: e.max(out=M8[:, 0:8], in_=SCO[:, 0:NS]), reads=(r_SCO,), writes=(r_M8,))
                    P.op("dve", lambda e: e.match_replace(out=WRK[:, 0:NS], in_to_replace=M8[:, 0:8], in_values=SCO[:, 0:NS], imm_value=-3.0e38),
                         reads=(r_SCO, r_M8), writes=(r_WRK,))
                    P.op("dve", lambda e: e.max(out=M8[:, 8:16], in_=WRK[:, 0:NS]), reads=(r_WRK,), writes=(r_M8,))
                    P.ts("dve", M8[:, 15:16], M8[:, 15:16], -1e29, ALU.max, reads=(r_M8,), writes=(r_M8,))
                    P.ts("dve", sel[:, 64:64 + NS], SCO[:, 0:NS], M8[:, 15:16], ALU.is_lt, -BIG, ALU.mult, reads=(r_SCO, r_M8), writes=(selr,))
                else:
                    P.ts("dve", sel[:, 64:64 + NS], SCO[:, 0:NS], -1e29, ALU.is_lt, -BIG, ALU.mult, reads=(r_SCO,), writes=(selr,))
                pt, ptr_ = PTR.next()
                P.tr(pt[:, 0:128], sel[:], ident[:], reads=(selr, r_ident), writes=(ptr_,))
                for r in range(4):
                    h = 4 * g + r
                    P.stt(QB[64:128, h, a * 128:(a + 1) * 128], POSQ[64:128, a * 128:(a + 1) * 128], -SLOPES[h], pt[64:128, 0:128],
                          ALU.mult, ALU.add, reads=(ptr_, r_TF), writes=(r_QB[h],))
        tiles = []
        for h in range(16):
            g = h // 4
            for br in (1, 2):
                lst = []
                if br == 1:
                    for j in range(4 * m + 4):
                        r = j - 4 * m
                        c0, c1 = (0, 256) if r <= 1 else (128, 256)
                        masks = {0: [(0, DT[:, 0, :])], 1: [(0, DT[:, 1, :])], 2: [(1, DT[:, 2, :])], 3: [(1, DT[:, 3, :])]}.get(r, [])
                        pv = [a for a in (0, 1) if (a == 1 or r <= 1)]
                        lst.append(dict(k=KE[:, g, j * 128:(j + 1) * 128], kres=r_KE[g], c0=c0, c1=c1, masks=masks,
                                        bias=ALB[:, h, 32 + r:33 + r], v=VS[:, j, g, 0:65], vres=r_VS, pv=pv))
                else:
                    for rp in range(8):
                        j = 4 * m - 4 + rp
                        if j < 0:
                            continue
                        sl = (j // 4) % 2
                        c0, c1 = (0, 128) if rp <= 1 else ((0, 256) if rp <= 5 else (128, 256))
                        masks = []
                        if rp <= 5:
                            masks.append((0, WT[:, rp, :]))
                        if rp >= 2:
                            masks.append((1, WT[:, 6 + rp - 2, :]))
                        pv = [a for a in (0, 1) if (a == 0 and rp <= 5) or (a == 1 and rp >= 2)]
                        lst.append(dict(k=KW[:, g, sl * 512 + (j % 4) * 128: sl * 512 + (j % 4) * 128 + 128], kres=r_KW, c0=c0, c1=c1, masks=masks,
                                        bias=ALB[:, h, 32 + rp - 4:33 + rp - 4], v=VW[:, sl * 4 + j % 4, g, 0:65], vres=r_VW, pv=pv))
                for a in (0, 1):
                    idx = [i for i, t_ in enumerate(lst) if a in t_["pv"]]
                    for i, t_ in enumerate(lst):
                        t_.setdefault("first", {})[a] = (i == idx[0])
                        t_.setdefault("last", {})[a] = (i == idx[-1])
                for i, t_ in enumerate(lst):
                    t_["h"] = h
                    t_["br"] = br
                    t_["end"] = (i == len(lst) - 1)
                    t_["begin"] = (i == 0)
                tiles.extend(lst)
        nt = len(tiles)
        state = {}

        def stageA(t_):
            ps, psr = PS.next()
            t_["ps"], t_["psr"] = ps, psr
            c0, c1 = t_["c0"], t_["c1"]
            nm = len(t_["masks"])
            P.mm(ps[:, c0:c1], t_["k"], QB[:, t_["h"], c0:c1], start=True, stop=(nm == 0), reads=(t_["kres"], r_QB[t_["h"]]), writes=(psr,))
            for i, (a, tab) in enumerate(t_["masks"]):
                P.mm(ps[:, a * 128:(a + 1) * 128], ident[:], tab, start=False, stop=(i == nm - 1), reads=(r_ident, r_TB), writes=(psr,))

        def stageB(t_):
            pt, ptr_ = PT.next()
            t_["pt"], t_["ptr"] = pt, ptr_
            c0, c1 = t_["c0"], t_["c1"]
            P.act(pt[:, c0:c1], t_["ps"][:, c0:c1], AF.Exp, reads=(t_["psr"], r_TF), writes=(ptr_,), bias=t_["bias"])

        def stageC(t_):
            if t_["begin"]:
                state["po"] = PO.next()
            po, por = state["po"]
            pov = po.rearrange("p (a d) -> p a d", d=65)
            for a in t_["pv"]:
                P.mm(pov[:, a, :], t_["pt"][:, a * 128:(a + 1) * 128], t_["v"], start=(t_["begin"] and a == t_["pv"][0]), stop=t_["last"][a],
                     reads=(t_["ptr"], t_["vres"]), writes=(por,), signal=True, sgc=True)
            if t_["end"]:
                h, br = t_["h"], t_["br"]
                rs, rsr = RS.next()
                P.op("dve", lambda e, rs=rs, pov=pov: e.reciprocal(out=rs[:, 0:2], in_=pov[:, :, 64]), reads=(por,), writes=(rsr,))
                P.tt("dve", rs[:, 2:4], rs[:, 0:2], GATES[:, :, 3 * h + br], ALU.mult, reads=(rsr, r_G), writes=(rsr,))
                if br == 1:
                    state["oh"] = OH.next()
                oh, ohr = state["oh"]
                for a in (0, 1):
                    if br == 1:
                        P.ts("dve", oh[:, a, :], pov[:, a, 0:64], rs[:, 2 + a:3 + a], ALU.mult, reads=(por, rsr), writes=(ohr,))
                    else:
                        P.stt(oh[:, a, :], pov[:, a, 0:64], rs[:, 2 + a:3 + a], oh[:, a, :], ALU.mult, ALU.add, reads=(por, rsr, ohr), writes=(ohr,))
                if br == 2:
                    P.tt("pool", OB[:, :, 64 * h:64 * h + 64], oh[:], OB[:, :, 64 * h:64 * h + 64], ALU.add, reads=(ohr, r_OB), writes=(r_OB,))

        for t in range(nt + 2):
            if t < nt:
                stageA(tiles[t])
            if 1 <= t <= nt:
                stageB(tiles[t - 1])
            if t >= 2:
                stageC(tiles[t - 2])

    def own_unit(m):
        with ExitStack() as ph:
            hT = P.sb(ph, "hTo", [128, 8, 260], BF16)
            REG = P.sb(ph, "REG", [128, 6144], BF16)
            YA = REG[:, 0:2048].rearrange("p (c t) -> p c t", t=256)
            oT = REG[:, 2048:4096].rearrange("p (c t) -> p c t", t=256)
            zT = REG[:, 4096:6144].rearrange("p (c t) -> p c t", t=256)
            aT = REG[:, 0:5632].rearrange("p (c t) -> p c t", t=256)
            XO = P.sb(ph, "XO", [128, 2, D], F32)
            QB = P.sb(ph, "QB", [128, 16, 256], BF16)
            OB = P.sb(ph, "OB", [128, 2, D], BF16)
            GATES = P.sb(ph, "GATES", [128, 2, 48], F32)
            CG = P.sb(ph, "CG", [128, 260], F32)
            UP = P.sb(ph, "UP", [128, 2, 130], F32)
            VV = P.sb(ph, "VV", [128, 2, 128], F32)
            SG = P.sb(ph, "SG", [128, 512], F32)
            TM = Ring([P.sb(ph, "TM%d" % i, [128, 512], F32) for i in range(2)])
            r_hT, r_YA, r_oT, r_zT, r_aT, r_G, r_OB, r_CG, r_UP, r_VV, r_SG, r_Z1, r_Z2 = (Res() for _ in range(13))
            r_XO = [Res(), Res()]
            r_QB = [Res() for _ in range(16)]
            TMN = (REG[:, 0:2048].bitcast(F32), r_aT)
            for a in range(2):
                P.dma("pool", XO[:, a, :], xown[m, a * 128:(a + 1) * 128, :], writes=(r_XO[a],))
                norm_mod(XO[:, a, :], r_XO[a], 128, "M1", "SH1", (hT, r_hT), a * 128)
            xh, xhr = xk.next()
            P.dma("pool", xh[0:4, :], xown[m, 256:260, :], writes=(xhr,))
            norm_mod(xh[0:4, :], xhr, 4, "M1", "SH1", (hT, r_hT), 256)
            for c in range(8):
                w, wr = P.slab(s_inx[:, FM1_0 + 384 * c: FM1_0 + 384 * c + 384], 8, 384)
                if S.dry:
                    continue
                pa, par = PD.next()
                pb_, pbr = PD.next()
                rh = [hT[:, k, 0:256] for k in range(8)]
                rhh = [hT[:, k, 256:260] for k in range(8)]
                proj(pa[:, 0:256], par, [w[:, k, 0:128] for k in range(8)], rh, (r_hT, wr))
                proj(pa[:, 256:512], par, [w[:, k, 128:256] for k in range(8)], rh, (r_hT, wr))
                proj(pb_[:, 0:256], pbr, [w[:, k, 256:384] for k in range(8)], rh, (r_hT, wr))
                proj(pb_[:, 256:260], pbr, [w[:, k, 128:256] for k in range(8)], rhh, (r_hT, wr))
                proj(pb_[:, 260:264], pbr, [w[:, k, 256:384] for k in range(8)], rhh, (r_hT, wr))
                P.cp("act", CG[:, 0:256], pa[:, 256:512], reads=(par,), writes=(r_CG,))
                P.cp("act", CG[:, 256:260], pb_[:, 256:260], reads=(pbr,), writes=(r_CG,))
                P.tt("dve", UP[:, :, 2:130], pb_[:, 0:256].rearrange("p (a t) -> p a t", t=128), CG[:, 0:256].rearrange("p (a t) -> p a t", t=128),
                     ALU.mult, reads=(pbr, r_CG), writes=(r_UP,))
                P.tt("dve", UP[:, :, 0:2], pb_[:, 260:264].rearrange("p (a t) -> p a t", t=2), CG[:, 256:260].rearrange("p (a t) -> p a t", t=2),
                     ALU.mult, reads=(pbr, r_CG), writes=(r_UP,))
                P.tt("dve", UP[:, :, 0:2], UP[:, :, 0:2], HF[:, m, :].rearrange("p (a t) -> p a t", t=2), ALU.mult, reads=(r_UP, r_TF), writes=(r_UP,))
                P.ts("dve", VV[:], UP[:, :, 0:128], WCV[:, c, 0:1], ALU.mult, WCV[:, c, 3:4], ALU.add, reads=(r_UP, r_WCV), writes=(r_VV,))
                P.stt(VV[:], UP[:, :, 1:129], WCV[:, c, 1:2], VV[:], ALU.mult, ALU.add, reads=(r_UP, r_WCV, r_VV), writes=(r_VV,))
                P.stt(VV[:], UP[:, :, 2:130], WCV[:, c, 2:3], VV[:], ALU.mult, ALU.add, reads=(r_UP, r_WCV, r_VV), writes=(r_VV,))
                P.tt("dve", YA[:, c, :].rearrange("p (a t) -> p a t", t=128), VV[:], pa[:, 0:256].rearrange("p (a t) -> p a t", t=128), ALU.mult,
                     reads=(r_VV, par), writes=(r_YA,))
                if m == 7:
                    P.cp("dve", CONVO[:, c, :], UP[:, 1, 128:130], reads=(r_UP,), writes=(r_CONVO,))
            w, wr = P.slab(s_inx[:, G_0:G_0 + 64], 8, 64)
            if not S.dry:
                for a in range(2):
                    pd, pdr = PD.next()
                    proj(pd[:, 0:48], pdr, [hT[:, k, a * 128:(a + 1) * 128] for k in range(8)], [w[:, k, 0:48] for k in range(8)], (r_hT, wr))
                    sigmoid_from(GATES[:, a, :], pd[:, 0:48], None, (pdr,), (r_G,), None)
            for i in range(4):
                w, wr = P.slab(s_inx[:, Q_0 + 256 * i: Q_0 + 256 * i + 256], 8, 256)
                if S.dry:
                    continue
                for j in range(2):
                    mt = 2 * i + j
                    pd, pdr = PD.next()
                    proj(pd[:, 0:256], pdr, [w[:, k, j * 128:(j + 1) * 128] for k in range(8)], [hT[:, k, 0:256] for k in range(8)], (r_hT, wr))
                    P.op("act", lambda e, pd=pd, mt=mt: e.mul(out=QB[0:64, 2 * mt, :], in_=pd[0:64, 0:256], mul=0.125), reads=(pdr,), writes=(r_QB[2 * mt],))
                    P.op("act", lambda e, pd=pd, mt=mt: e.mul(out=QB[0:64, 2 * mt + 1, :], in_=pd[64:128, 0:256], mul=0.125), reads=(pdr,), writes=(r_QB[2 * mt + 1],))
            if not S.dry:
                if cfg["attn"]:
                    attention(m, QB, r_QB, GATES, r_G, OB, r_OB, ph)
                else:
                    P.memset("pool", OB[:], 0.0, writes=(r_OB,))
                for a in range(2):
                    pt, ptr_ = PTR.next()
                    for kc in range(8):
                        P.tr(pt[:, kc * 128:(kc + 1) * 128], OB[:, a, kc * 128:(kc + 1) * 128], ident[:], reads=(r_OB, r_ident), writes=(ptr_,), signal=(kc == 7))
                    ptv = pt.rearrange("p (j t) -> p j t", t=128)
                    P.cp("act" if a == 0 else "dve", oT[:, 0:8, a * 128:(a + 1) * 128], ptv[:, 0:8, :], reads=(ptr_,), writes=(r_oT,))
            for c in range(8):
                w, wr = P.slab(s_d2[:, 512 * c:512 * c + 512], 8, 512)
                if S.dry:
                    continue
                pa, par = PD.next()
                pb_, pbr = PD.next()
                rh = [hT[:, k, 0:256] for k in range(8)]
                proj(pa[:, 0:256], par, [w[:, k, 0:128] for k in range(8)], rh, (r_hT, wr))
                proj(pa[:, 256:512], par, [w[:, k, 128:256] for k in range(8)], rh, (r_hT, wr))
                proj(pb_[:, 0:256], pbr, [w[:, k, 256:384] for k in range(8)], [YA[:, k, :] for k in range(8)], (r_YA, wr))
                proj(pb_[:, 256:512], pbr, [w[:, k, 384:512] for k in range(8)], [oT[:, k, :] for k in range(8)], (r_oT, wr))
                sigmoid_from(SG[:], pa, None, (par,), (r_SG,), None)
                P.tt("dve", SG[:], SG[:], pb_, ALU.mult, reads=(r_SG, pbr), writes=(r_SG,))
                P.tt("pool", zT[:, c, :], SG[:, 0:256], SG[:, 256:512], ALU.add, reads=(r_SG,), writes=(r_zT,))
            for i in range(2):
                w, wr = P.slab(s_out[:, 512 * i:512 * i + 512], 8, 512)
                if S.dry:
                    continue
                for a in range(2):
                    pd, pdr = PD.next()
                    proj(pd, pdr, [zT[:, k, a * 128:(a + 1) * 128] for k in range(8)], [w[:, k, :] for k in range(8)], (r_zT, wr))
                    tm, tmr = TM.next()
                    P.tt("dve", tm[:], pd, msec("G1")[:, 512 * i:512 * i + 512], ALU.mult, reads=(pdr, r_MOD), writes=(tmr,))
                    P.tt("pool", XO[:, a, 512 * i:512 * i + 512], tm[:], XO[:, a, 512 * i:512 * i + 512], ALU.add, reads=(tmr, r_XO[a]), writes=(r_XO[a],))
            if not S.dry:
                for a in range(2):
                    norm_mod(XO[:, a, :], r_XO[a], 128, "M2", "SH2", (hT, r_hT), a * 128)
                S.barrier()
            for fc in range(NFC):
                w, wr = P.slab(s_gu[:, 256 * fc:256 * fc + 256], 8, 256)
                if S.dry:
                    continue
                pd, pdr = PD.next()
                rh = [hT[:, k, 0:256] for k in range(8)]
                proj(pd[:, 0:256], pdr, [w[:, k, 0:128] for k in range(8)], rh, (r_hT, wr))
                proj(pd[:, 256:512], pdr, [w[:, k, 128:256] for k in range(8)], rh, (r_hT, wr))
                tm, tmr = TM.next()
                sigmoid_from(tm[:, 0:256], pd[:, 0:256], None, (pdr,), (tmr,), None)
                P.tt("dve", tm[:, 0:256], tm[:, 0:256], pd[:, 0:256], ALU.mult, reads=(tmr, pdr), writes=(tmr,))
                P.tt("dve", aT[:, fc, :], tm[:, 0:256], pd[:, 256:512], ALU.mult, reads=(tmr, pdr), writes=(r_aT,))
            for i in range(2):
                pds = None
                for s_ in range(3):
                    nk = 8 if s_ < 2 else 6
                    w, wr = P.slab(s_down[1024 * s_:1024 * s_ + 128 * nk, 512 * i:512 * i + 512], nk, 512)
                    if S.dry:
                        continue
                    if pds is None:
                        pds = [PD.next(), PD.next()]
                    for a in range(2):
                        pd, pdr = pds[a]
                        for k in range(nk):
                            fc = 8 * s_ + k
                            P.mm(pd, aT[:, fc, a * 128:(a + 1) * 128], w[:, k, :], start=(fc == 0), stop=(fc == NFC - 1), reads=(r_aT, wr), writes=(pdr,),
                                 signal=(k == nk - 1))
                if S.dry:
                    continue
                for a in range(2):
                    pd, pdr = pds[a]
                    tm, tmr = TM.next()
                    P.tt("dve", tm[:], pd, msec("G2")[:, 512 * i:512 * i + 512], ALU.mult, reads=(pdr, r_MOD), writes=(tmr,))
                    P.tt("pool", XO[:, a, 512 * i:512 * i + 512], tm[:], XO[:, a, 512 * i:512 * i + 512], ALU.add, reads=(tmr, r_XO[a]), writes=(r_XO[a],))
            if not S.dry:
                nf, nfr = TMN
                P.dma("pool", nf[:], norms[2:3, :].to_broadcast((128, D)), writes=(nfr,))
                for a in range(2):
                    rs, rsr = rstd_of(XO[:, a, :], r_XO[a], 128)
                    yt, ytr = xk.next()
                    P.stt(yt[:], XO[:, a, :], rs, nf[:], ALU.mult, ALU.mult, reads=(r_XO[a], rsr, nfr), writes=(ytr,))
                    P.dma("pool", yown[m, a * 128:(a + 1) * 128, :], yt[:], reads=(ytr,))
                S.barrier()

    def sample_phase():
        compute_mod(1)
        with ExitStack() as ph:
            def A(name, shape, dt):
                return P.sb(ph, name, shape, dt)
            TSF = A("TSF", [128, TSF_N], F32)
            TSB = A("TSB", [128, TSB_N], BF16)
            r_TS = Res()
            P.dma("sp", TSF[:], ts_f32, writes=(r_TS,))
            P.dma("sp", TSB[:], ts_bf, writes=(r_TS,))

            def tsf(name, rows=128):
                o, n = TSF_OFF[name]
                return TSF[0:rows, o:o + n]
            SLC = tsf("SLC", 64)
            NQB = tsf("NQB", 64)
            CPS = tsf("CPS", 64)
            FBS = tsf("FBS", 64)
            SELG = tsf("SELG", 64)
            L2 = tsf("L2", 2)
            B2S = tsf("B2S", 2).rearrange("p (c r) -> p c r", r=64)
            B2W = tsf("B2W", 2).rearrange("p (c r) -> p c r", r=64)
            NEWB = tsf("NEWB", 4)
            WM0 = TSB[:, TSB_OFF["WM0"][0]:TSB_OFF["WM0"][0] + 64]
            EFULL = TSB[:, TSB_OFF["EFULL"][0]:TSB_OFF["EFULL"][0] + PAST]
            PTB = A("PTB", [128, 4 * NPG], I32)
            IDX = A("IDX", [128, 4 * NPG], I32)
            r_IDX = Res()
            P.dma("pool", PTB[:], ptab_d.rearrange("b j -> (b j)").rearrange("(o n) -> o n", o=1).to_broadcast((128, 4 * NPG)), writes=(r_IDX,))
            P.ts("dve", IDX[:], PTB[:], 128.0, ALU.mult, tsf("PIO")[:, 0:1], ALU.add, reads=(r_IDX, r_TS), writes=(r_IDX,))
            XS = A("XS", [128, D], F32)
            hT = A("hTs", [128, 8, 16], BF16)
            YA = A("YAs", [128, 8, 16], BF16)
            oT = A("oTs", [128, 8, 16], BF16)
            zT = A("zTs", [128, 8, 16], BF16)
            aT = A("aTs", [128, NFC, 16], BF16)
            QBs = A("QBs", [64, 16, 16], BF16)
            KNS = A("KNS", [64, 4, 16], BF16)
            KNW = A("KNW", [64, 4, 16], BF16)
            VN = A("VN", [4, 4, 2, 4, 66], BF16)
            SCV = A("SCV", [128, 8, 4, 2], F32)
            CONVS = A("CONVS", [128, 8, 4, 2], F32)
            UPs = A("UPs", [128, 4, 6], F32)
            VVs = A("VVs", [128, 4, 4], F32)
            CGs = A("CGs", [128, 16], F32)
            GATS = A("GATS", [4, 4, 48], F32)
            STG = A("STG", [128, 3, 512], F32)
            SG = A("SGs", [128, 64], F32)
            TM = Ring([A("TMs%d" % i, [128, 512], F32) for i in range(2)])
            XG = A("XGs", [128, 512], F32)
            TG = A("TGs", [128, 512], F32)
            GG = A("GGs", [128, 512], BF16)
            R2s = A("R2s", [128, 8, 1042], BF16)
            r_R2s = Res()
            PG = Ring([A("PG%d" % i, [128, 4, 512], BF16) for i in range(2)])
            KT = Ring([A("KT%d" % i, [64, 4, 512], BF16) for i in range(2)])
            VP = Ring([A("VP%d" % i, [128, 4, 4, 66], BF16) for i in range(2)])
            KcTs = A("KcTs", [64, 4, 520], BF16)
            VcTs = A("VcTs", [64, 4, 520], BF16)
            Vcs = A("Vcs", [128, 4, 4, 64], BF16)
            QZ = A("QZ", [64, 4, 64], BF16)
            SCs = A("SCs", [64, 512], F32)
            ECs = A("ECs", [64, 512], F32)
            P32 = A("P32", [64, 512], F32)
            PBs = A("PBs", [64, 512], BF16)
            I64s = A("I64s", [64, 128], F32)
            SCOs = A("SCOs", [64, 128], F32)
            WRKs = A("WRKs", [64, 128], F32)
            M8s = A("M8s", [64, 16], F32)
            SELs = A("SELs", [64, 128], BF16)
            MBT = A("MBT", [128, 64], BF16)
            PcTs = A("PcTs", [128, 4, 64], BF16)
            PTs = Ring([A("PTs%d" % i, [128, 256], BF16) for i in range(2)])
            SN = A("SN", [4, 64], F32)
            PTn = A("PTn", [4, 64], BF16)
            OSUM = A("OSUM", [4, 16, 64], F32)
            OTMP = A("OTMP", [4, 6, 64], F32)
            OBs = A("OBs", [4, D], BF16)
            RSs = Ring([A("RSs%d" % i, [64, 16], F32) for i in range(4)])
            (r_XS, r_hT, r_YA, r_oT, r_zT, r_aT, r_QBs, r_KN, r_VN, r_SCV, r_CONVS, r_UP, r_VV, r_CG, r_GAT, r_STG, r_SG, r_XG, r_TG, r_GG,
             r_KcTs, r_VcTs, r_Vcs, r_QZ, r_SCs, r_ECs, r_P32, r_PBs, r_I64, r_SCO, r_WRK, r_M8, r_SEL, r_MBT, r_PcT, r_SN, r_PTn, r_OSUM,
             r_OTMP, r_OBs) = (Res() for _ in range(40))
            for vtile, rr in zip(VP.aps, VP.res):
                P.memset("pool", vtile[:, :, :, 64:66], 1.0, writes=(rr,))
            P.memset("pool", VN[:, :, :, :, 64:66], 1.0, writes=(r_VN,))
            P.memset("pool", QZ[:], 0.0, writes=(r_QZ,))
            P.memset("pool", KcTs[:], 0.0, writes=(r_KcTs,))
            P.memset("pool", VcTs[:], 0.0, writes=(r_VcTs,))
            P.memset("pool", oT[:], 0.0, writes=(r_oT,))
            P.memset("pool", P32[:], 0.0, writes=(r_P32,))
            P.dma("pool", XS[0:16, :], xs_d, writes=(r_XS,))
            norm_mod(XS[0:16, :], r_XS, 16, "M1", "SH1", (hT, r_hT), 0)
            for c in range(8):
                P.dma("pool", SCV[:, c, :, :], sconv_d[:, :, c * 128:(c + 1) * 128].rearrange("b k p -> p b k"), writes=(r_SCV,), allow_slow_non_contiguous=True)
            rh = [hT[:, k, :] for k in range(8)]
            for c in range(8):
                w, wr = P.slab(s_inx[:, FM1_0 + 384 * c: FM1_0 + 384 * c + 384], 8, 384)
                if S.dry:
                    continue
                pa, par = PD.next()
                proj(pa[:, 0:16], par, [w[:, k, 0:128] for k in range(8)], rh, (r_hT, wr))
                proj(pa[:, 16:32], par, [w[:, k, 128:256] for k in range(8)], rh, (r_hT, wr))
                proj(pa[:, 32:48], par, [w[:, k, 256:384] for k in range(8)], rh, (r_hT, wr))
                P.cp("dve", CGs[:], pa[:, 16:32], reads=(par,), writes=(r_CG,))
                P.tt("dve", UPs[:, :, 2:6], pa[:, 32:48].rearrange("p (b q) -> p b q", q=4), CGs[:].rearrange("p (b q) -> p b q", q=4), ALU.mult,
                     reads=(par, r_CG), writes=(r_UP,))
                P.cp("dve", UPs[:, :, 0:2], SCV[:, c, :, :], reads=(r_SCV,), writes=(r_UP,))
                P.ts("dve", VVs[:], UPs[:, :, 0:4], WCV[:, c, 0:1], ALU.mult, WCV[:, c, 3:4], ALU.add, reads=(r_UP, r_WCV), writes=(r_VV,))
                P.stt(VVs[:], UPs[:, :, 1:5], WCV[:, c, 1:2], VVs[:], ALU.mult, ALU.add, reads=(r_UP, r_WCV, r_VV), writes=(r_VV,))
                P.stt(VVs[:], UPs[:, :, 2:6], WCV[:, c, 2:3], VVs[:], ALU.mult, ALU.add, reads=(r_UP, r_WCV, r_VV), writes=(r_VV,))
                P.tt("dve", YA[:, c, :].rearrange("p (b q) -> p b q", q=4), VVs[:], pa[:, 0:16].rearrange("p (b q) -> p b q", q=4), ALU.mult,
                     reads=(r_VV, par), writes=(r_YA,))
                P.cp("dve", CONVS[:, c, :, :], UPs[:, :, 4:6], reads=(r_UP,), writes=(r_CONVS,))
            if not S.dry:
                for c in range(8):
                    P.dma("pool", oconv_s[:, :, c * 128:(c + 1) * 128].rearrange("b k p -> p b k"), CONVS[:, c, :, :], reads=(r_CONVS,), allow_slow_non_contiguous=True)
            w, wr = P.slab(s_inx[:, G_0:G_0 + 64], 8, 64)
            if not S.dry:
                pd, pdr = PD.next()
                for bi in range(4):
                    proj(pd[0:4, 48 * bi:48 * bi + 48], pdr, [hT[:, k, 4 * bi:4 * bi + 4] for k in range(8)], [w[:, k, 0:48] for k in range(8)], (r_hT, wr))
                sigmoid_from(GATS[:].rearrange("p b j -> p (b j)"), pd[0:4, 0:192], None, (pdr,), (r_GAT,), None)
            for i in range(4):
                w, wr = P.slab(s_inx[:, Q_0 + 256 * i: Q_0 + 256 * i + 256], 8, 256)
                if S.dry:
                    continue
                for j in range(2):
                    mt = 2 * i + j
                    pd, pdr = PD.next()
                    proj(pd[:, 0:16], pdr, [w[:, k, j * 128:(j + 1) * 128] for k in range(8)], rh, (r_hT, wr))
                    P.op("act", lambda e, pd=pd, mt=mt: e.mul(out=QBs[0:64, 2 * mt, :], in_=pd[0:64, 0:16], mul=0.125), reads=(pdr,), writes=(r_QBs,))
                    P.op("act", lambda e, pd=pd, mt=mt: e.mul(out=QBs[0:64, 2 * mt + 1, :], in_=pd[64:128, 0:16], mul=0.125), reads=(pdr,), writes=(r_QBs,))
            for i in range(4):
                w, wr = P.slab(s_inx[:, KD_0 + 256 * i: KD_0 + 256 * i + 256], 8, 256)
                if S.dry:
                    continue
                for j in range(2):
                    ti = 2 * i + j
                    kind, g = ti // 4, ti % 4
                    pd, pdr = PD.next()
                    proj(pd[:, 0:16], pdr, [w[:, k, j * 128:(j + 1) * 128] for k in range(8)], rh, (r_hT, wr))
                    P.cp("act", (KNS if kind == 0 else KNW)[0:64, g, :], pd[0:64, 0:16], reads=(pdr,), writes=(r_KN,))
            for i in range(3):
                w, wr = P.slab(s_inx[:, KV_0 + 512 * i: KV_0 + 512 * i + 512], 8, 512)
                if S.dry:
                    continue
                pd, pdr = PD.next()
                proj(pd[0:16, :], pdr, [hT[:, k, :] for k in range(8)], [w[:, k, :] for k in range(8)], (r_hT, wr))
                P.cp("act", STG[0:16, i, :], pd[0:16, :], reads=(pdr,), writes=(r_STG,))
                if i < 2:
                    P.dma("pool", oskv[i], STG[0:16, i, :], reads=(r_STG,))
                else:
                    for bi in range(4):
                        P.dma("pool", owin_s[bi, 508:512, :], STG[4 * bi:4 * bi + 4, i, :], reads=(r_STG,))
                if i >= 1:
                    for bi in range(4):
                        pv, pvr = PD.next()
                        proj(pv[0:4, 0:256], pvr, [hT[:, k, 4 * bi:4 * bi + 4] for k in range(8)], [w[:, k, 256:512] for k in range(8)], (r_hT, wr))
                        P.cp("dve", VN[0:4, bi, i - 1, :, 0:64], pv[0:4, 0:256].rearrange("p (g d) -> p g d", d=64), reads=(pvr,), writes=(r_VN,))
            if not S.dry:
                for bi in range(4):
                    P.dma("sp", owin_s[bi, 0:508, :], cwin_d[bi, 4:512, :])
            OAB = [(PO.aps[0], PO.res[0]), (PO.aps[1], PO.res[1]), (PD.aps[2], PD.res[2])]
            PDs = Ring([PD.aps[0], PD.aps[1]])
            PDs.res = [PD.res[0], PD.res[1]]
            POt_full = [POt[0][:], POt[1][:], PD.aps[2]]

            def oa(h):
                b = h // 6
                sl_ = h - 6 * b
                return POt_full[b][0:4, sl_ * 65:sl_ * 65 + 65], OAB[b][1], b

            regstate = {}

            def gather4(cache, bi, pg, dst, dres):
                for s_ in range(4):
                    j = 4 * pg + s_

                    def fn(e, j=j, s_=s_, dst=dst, cache=cache, bi=bi):
                        return e.indirect_dma_start(out=dst[:, s_, :], out_offset=None, in_=cache,
                                                    in_offset=bass.IndirectOffsetOnAxis(ap=IDX[:, bi * NPG + j:bi * NPG + j + 1], axis=0))
                    P.op("pool", fn, (r_IDX,), (dres,), dma=True)

            def trans_block(src, sres, col0, eng):
                pt, ptr_ = PTR.next()
                for s_ in range(4):
                    P.tr(pt[:, s_ * 128:(s_ + 1) * 128], src[:, s_, col0:col0 + 128], ident[:], reads=(sres, r_ident), writes=(ptr_,), signal=(s_ == 3))
                return pt[:, 0:512], ptr_

            def attn_chunks(bi, ktile, kres, vtile, vres, nchunks, mask_fn, bias_fn, first_flags):
                ps, psr = PS.next()
                for cs in range(nchunks):
                    for g in range(4):
                        P.mm(ps[:, cs * 64 + 16 * g:cs * 64 + 16 * g + 16], ktile[0:64, g, cs * 128:(cs + 1) * 128], QBs[0:64, 4 * g:4 * g + 4, 4 * bi:4 * bi + 4],
                             start=(cs == 0 and g == 0), stop=False, reads=(kres, r_QBs), writes=(psr,), signal=False, sgc=True)
                    mk = mask_fn(cs)
                    if mk is not None:
                        P.mm(ps[:, cs * 64:cs * 64 + 64], mk[0], mk[1], start=False, stop=False, reads=(mk[2],), writes=(psr,), signal=False, sgc=True)
                    P.mm(ps[:, cs * 64:cs * 64 + 64], L2, bias_fn(cs), start=False, stop=True, reads=(r_TS,), writes=(psr,), signal=True, sgc=True)
                pt_, ptr2 = PTs.next()
                P.act(pt_[:, 0:64 * nchunks], ps[:, 0:64 * nchunks], AF.Exp, reads=(psr,), writes=(ptr2,))
                for cs in range(nchunks):
                    for h in range(16):
                        o_, ores, b = oa(h)
                        P.mm(o_, pt_[:, cs * 64 + 4 * h:cs * 64 + 4 * h + 4], vtile[:, cs, h // 4, 0:65], start=first_flags[b], stop=False,
                             reads=(ptr2, vres), writes=(ores,), signal=(h % 6 == 5 or h == 15), sgc=True)
                        first_flags[b] = False

            def new_chunk(bi, KN, vsel, first_flags):
                pn, pnr = PS.next()
                for g in range(4):
                    P.mm(pn[0:4, 16 * g:16 * g + 16], KN[0:64, g, 4 * bi:4 * bi + 4], QBs[0:64, 4 * g:4 * g + 4, 4 * bi:4 * bi + 4],
                         start=(g == 0), stop=(g == 3), reads=(r_KN, r_QBs), writes=(pnr,), signal=(g == 3), sgc=True)
                P.tt("dve", SN[:], pn[0:4, 0:64], NEWB, ALU.add, reads=(pnr, r_TS), writes=(r_SN,))
                P.act(PTn[:], SN[:], AF.Exp, reads=(r_SN,), writes=(r_PTn,))
                for h in range(16):
                    o_, ores, b = oa(h)
                    P.mm(o_, PTn[0:4, 4 * h:4 * h + 4], VN[0:4, bi, vsel, h // 4, 0:65], start=first_flags[b], stop=True,
                         reads=(r_PTn, r_VN), writes=(ores,), signal=(h % 6 == 5 or h == 15), sgc=True)
                    first_flags[b] = False

            def combine(bi, br, normalized, first):
                for b in range(3):
                    h0 = 6 * b
                    nh = min(6, 16 - h0)
                    ov = POt_full[b][0:4, 0:nh * 65].rearrange("p (h d) -> p h d", d=65)
                    ores = OAB[b][1]
                    rs, rsr = RSs.next()
                    gsl = GATS[0:4, bi, 3 * h0 + br:3 * (h0 + nh - 1) + br + 1:3]
                    if normalized:
                        P.cp("dve", rs[0:4, 8:8 + nh], gsl, reads=(r_GAT,), writes=(rsr,))
                    else:
                        P.op("dve", lambda e, rs=rs, ov=ov, nh=nh: e.reciprocal(out=rs[0:4, 0:nh], in_=ov[:, :, 64]), reads=(ores,), writes=(rsr,))
                        P.tt("dve", rs[0:4, 8:8 + nh], rs[0:4, 0:nh], gsl, ALU.mult, reads=(rsr, r_GAT), writes=(rsr,))
                    wv = rs[0:4, 8:8 + nh]
                    wb = bass.AP(wv.tensor, wv.offset, [list(x) for x in wv.ap] + [[0, 64]])
                    if first:
                        P.tt("dve", OSUM[0:4, h0:h0 + nh, :], ov[:, :, 0:64], wb, ALU.mult, reads=(ores, rsr), writes=(r_OSUM,))
                    else:
                        P.tt("dve", OTMP[0:4, 0:nh, :], ov[:, :, 0:64], wb, ALU.mult, reads=(ores, rsr), writes=(r_OTMP,))
                        P.tt("dve", OSUM[0:4, h0:h0 + nh, :], OSUM[0:4, h0:h0 + nh, :], OTMP[0:4, 0:nh, :], ALU.add, reads=(r_OTMP, r_OSUM), writes=(r_OSUM,))

            for bi in range(cfg.get("sbatches", 4)):
                P.memset("pool", R2s[:], 0.0, writes=(r_R2s,))
                for G8 in range(8):
                    for hf in range(2):
                        pgt, pgr = PG.next()
                        gather4(ccmp_d, bi, 2 * G8 + hf, pgt, pgr)
                        c0 = 16 + 512 * hf
                        for blk in range(4):
                            pt, ptr_ = trans_block(pgt, pgr, 128 * blk, None)
                            gA = (blk // 2) * 4 + 2 * (blk % 2)
                            eng = "act" if blk % 2 == 0 else "dve"
                            P.cp(eng, R2s[0:64, gA, c0:c0 + 512], pt[0:64, :], reads=(ptr_,), writes=(r_R2s,))
                            P.cp(eng, R2s[64:128, gA, c0 - 1:c0 + 511], pt[0:64, :], reads=(ptr_,), writes=(r_R2s,))
                            P.cp(eng, R2s[64:128, gA + 1, c0 - 1:c0 + 511], pt[64:128, :], reads=(ptr_,), writes=(r_R2s,))
                            P.cp(eng, R2s[0:64, gA + 1, c0:c0 + 512], pt[64:128, :], reads=(ptr_,), writes=(r_R2s,))
                    compress_block(G8, KcTs, r_KcTs, VcTs, r_VcTs, XG, TG, GG, r_XG, r_TG, r_GG, R2x=R2s, rR2=r_R2s, nb=64, ntok=1024)
                for cc in range(4):
                    for g in range(4):
                        pt, ptr_ = PTR.next()
                        P.tr(pt[:, 0:64], VcTs[0:64, g, 128 * cc + 2:128 * cc + 130], ident[0:64, 0:64], reads=(r_VcTs, r_ident), writes=(ptr_,))
                        P.cp("dve", Vcs[:, cc, g, :], pt[:, 0:64], reads=(ptr_,), writes=(r_Vcs,))
                for g in range(4):
                    P.cp("dve", QZ[0:64, g, 16 * g:16 * g + 16].rearrange("p (r q) -> p r q", q=4), QBs[0:64, 4 * g:4 * g + 4, 4 * bi:4 * bi + 4],
                         reads=(r_QBs,), writes=(r_QZ,))
                pc, pcr = PDs.next()
                for g in range(4):
                    P.mm(pc[0:64, 0:512], QZ[0:64, g, :], KcTs[0:64, g, 2:514], start=(g == 0), stop=(g == 3), reads=(r_QZ, r_KcTs), writes=(pcr,))
                P.stt(SCs[:, 0:511], CPS[:, 0:511], SLC[:, 0:1], pc[0:64, 0:511], ALU.mult, ALU.add, reads=(pcr, r_TS), writes=(r_SCs,))
                rs, rsr = RSs.next()
                P.act(ECs[:, 0:511], SCs[:, 0:511], AF.Exp, reads=(r_SCs, r_TS), writes=(r_ECs, rsr), bias=NQB[:, 0:1], accum_out=rs[:, 0:1])
                P.ts("dve", rs[:, 1:2], rs[:, 0:1], 1e-30, ALU.add, reads=(rsr,), writes=(rsr,))
                P.op("dve", lambda e, rs=rs: e.reciprocal(out=rs[:, 2:3], in_=rs[:, 1:2]), reads=(rsr,), writes=(rsr,))
                P.ts("dve", P32[:, 0:511], ECs[:, 0:511], rs[:, 2:3], ALU.mult, reads=(r_ECs, rsr), writes=(r_P32,))
                P.cp("dve", PBs[:], P32[:], reads=(r_P32,), writes=(r_PBs,))
                pi, pir = PDs.next()
                P.mm(pi[0:64, 0:512], SELG, P32[:], start=True, stop=True, reads=(r_TS, r_P32), writes=(pir,))
                P.op("dve", lambda e, pi=pi: e.tensor_reduce(out=I64s[:], in_=pi[0:64, 0:512].rearrange("p (j k) -> p j k", k=4), axis=AX.X, op=ALU.add),
                     reads=(pir,), writes=(r_I64,))
                P.tt("dve", I64s[:, 1:128], I64s[:, 1:128], pi[0:64, 3:508:4], ALU.add, reads=(pir, r_I64), writes=(r_I64,))
                P.tt("dve", SCOs[:], I64s[:], FBS, ALU.add, reads=(r_I64, r_TS), writes=(r_SCO,))
                P.op("dve", lambda e: e.max(out=M8s[:, 0:8], in_=SCOs[:]), reads=(r_SCO,), writes=(r_M8,))
                P.op("dve", lambda e: e.match_replace(out=WRKs[:], in_to_replace=M8s[:, 0:8], in_values=SCOs[:], imm_value=-3.0e38),
                     reads=(r_SCO, r_M8), writes=(r_WRK,))
                P.op("dve", lambda e: e.max(out=M8s[:, 8:16], in_=WRKs[:]), reads=(r_WRK,), writes=(r_M8,))
                P.ts("dve", SELs[:], SCOs[:], M8s[:, 14:15], ALU.is_lt, -BIG, ALU.mult, reads=(r_SCO, r_M8), writes=(r_SEL,))
                pt, ptr_ = PTR.next()
                P.tr(pt[:, 0:64], SELs[:], ident[0:64, 0:64], reads=(r_SEL, r_ident), writes=(ptr_,))
                P.cp("act", MBT[:], pt[:, 0:64], reads=(ptr_,), writes=(r_MBT,))
                pt, ptr_ = PTR.next()
                for cc in range(4):
                    P.tr(pt[:, cc * 64:(cc + 1) * 64], PBs[:, 128 * cc:128 * cc + 128], ident[0:64, 0:64], reads=(r_PBs, r_ident), writes=(ptr_,), signal=(cc == 3))
                P.cp("act", PcTs[:], pt[:, 0:256].rearrange("p (c r) -> p c r", r=64), reads=(ptr_,), writes=(r_PcT,))
                ff = [True, True, True]
                for cc in range(4):
                    for h in range(16):
                        o_, ores, b = oa(h)
                        P.mm(o_[:, 0:64], PcTs[:, cc, 4 * h:4 * h + 4], Vcs[:, cc, h // 4, :], start=ff[b], stop=(cc == 3), reads=(r_PcT, r_Vcs), writes=(ores,),
                             signal=(h % 6 == 5 or h == 15), sgc=True)
                        ff[b] = False
                combine(bi, 0, True, True)
                if cfg.get("dbg2") and bi == 0:
                    DV = A("DV", [128, 1024], F32)
                    DPT = A("DPT", [128, 256], F32)
                    r_DV = Res()
                    P.dma("pool", dbg_p, P32[:], reads=(r_P32,))
                    P.cp("dve", DV[:], Vcs[:].rearrange("p a b d -> p (a b d)"), reads=(r_Vcs,), writes=(r_DV,))
                    P.dma("pool", dbg_v, DV[:], reads=(r_DV,))
                    P.cp("dve", DPT[:], PcTs[:].rearrange("p a r -> p (a r)"), reads=(r_PcT,), writes=(r_DV,))
                    P.dma("pool", dbg_pt, DPT[:], reads=(r_DV,))
                if cfg.get("dbg"):
                    P.dma("pool", dbg_d[bi, 0].rearrange("q (h d) -> q h d", d=64), OSUM[:], reads=(r_OSUM,))
                ff = [True, True, True]
                for pg in range(16):
                    pgt, pgr = PG.next()
                    gather4(csel_d, bi, pg, pgt, pgr)
                    vp, vpr = VP.next()
                    P.cp("pool", vp[:, :, :, 0:64], pgt[:, :, 256:512].rearrange("p s (g d) -> p s g d", d=64), reads=(pgr,), writes=(vpr,))
                    kt, ktr = KT.next()
                    for blk in range(2):
                        pt, ptr_ = trans_block(pgt, pgr, 128 * blk, None)
                        eng = "act" if blk == 0 else "dve"
                        P.cp(eng, kt[0:64, 2 * blk, :], pt[0:64, :], reads=(ptr_,), writes=(ktr,))
                        P.cp(eng, kt[0:64, 2 * blk + 1, :], pt[64:128, :], reads=(ptr_,), writes=(ktr,))
                    attn_chunks(bi, kt, ktr, vp, vpr, 4,
                                lambda cs, pg=pg: (EFULL[:, (4 * pg + cs) * 128:(4 * pg + cs + 1) * 128], MBT[:], r_MBT),
                                lambda cs, pg=pg: B2S[:, 4 * pg + cs, :], ff)
                new_chunk(bi, KNS, 0, ff)
                combine(bi, 1, False, False)
                if cfg.get("dbg"):
                    P.dma("pool", dbg_d[bi, 1].rearrange("q (h d) -> q h d", d=64), OSUM[:], reads=(r_OSUM,))
                ff = [True, True, True]
                pgt, pgr = PG.next()
                for s_ in range(4):
                    P.dma("pool", pgt[:, s_, :], cwin_d[bi, 128 * s_:128 * s_ + 128, :], writes=(pgr,))
                vp, vpr = VP.next()
                P.cp("pool", vp[:, :, :, 0:64], pgt[:, :, 256:512].rearrange("p s (g d) -> p s g d", d=64), reads=(pgr,), writes=(vpr,))
                kt, ktr = KT.next()
                for blk in range(2):
                    pt, ptr_ = trans_block(pgt, pgr, 128 * blk, None)
                    eng = "act" if blk == 0 else "dve"
                    P.cp(eng, kt[0:64, 2 * blk, :], pt[0:64, :], reads=(ptr_,), writes=(ktr,))
                    P.cp(eng, kt[0:64, 2 * blk + 1, :], pt[64:128, :], reads=(ptr_,), writes=(ktr,))
                attn_chunks(bi, kt, ktr, vp, vpr, 4,
                            lambda cs: (ident[:], WM0, r_TS) if cs == 0 else None,
                            lambda cs: B2W[:, cs, :], ff)
                new_chunk(bi, KNW, 1, ff)
                combine(bi, 2, False, False)
                if cfg.get("dbg"):
                    P.dma("pool", dbg_d[bi, 2].rearrange("q (h d) -> q h d", d=64), OSUM[:], reads=(r_OSUM,))
                P.cp("dve", OBs[:], OSUM[:].rearrange("p h d -> p (h d)"), reads=(r_OSUM,), writes=(r_OBs,))
                pt, ptr_ = PTR.next()
                for kc in range(8):
                    P.tr(pt[:, kc * 4:kc * 4 + 4], OBs[0:4, kc * 128:(kc + 1) * 128], ident[0:4, 0:4], reads=(r_OBs, r_ident), writes=(ptr_,), signal=(kc == 7))
                P.cp("act", oT[:, :, 4 * bi:4 * bi + 4], pt[:, 0:32].rearrange("p (k t) -> p k t", t=4), reads=(ptr_,), writes=(r_oT,))
            for c in range(8):
                w, wr = P.slab(s_d2[:, 512 * c:512 * c + 512], 8, 512)
                if S.dry:
                    continue
                pa, par = PD.next()
                proj(pa[:, 0:16], par, [w[:, k, 0:128] for k in range(8)], rh, (r_hT, wr))
                proj(pa[:, 16:32], par, [w[:, k, 128:256] for k in range(8)], rh, (r_hT, wr))
                proj(pa[:, 32:48], par, [w[:, k, 256:384] for k in range(8)], [YA[:, k, :] for k in range(8)], (r_YA, wr))
                proj(pa[:, 48:64], par, [w[:, k, 384:512] for k in range(8)], [oT[:, k, :] for k in range(8)], (r_oT, wr))
                sigmoid_from(SG[:, 0:32], pa[:, 0:32], None, (par,), (r_SG,), None)
                P.tt("dve", SG[:, 0:32], SG[:, 0:32], pa[:, 32:64], ALU.mult, reads=(r_SG, par), writes=(r_SG,))
                P.tt("pool", zT[:, c, :], SG[:, 0:16], SG[:, 16:32], ALU.add, reads=(r_SG,), writes=(r_zT,))
            for i in range(2):
                w, wr = P.slab(s_out[:, 512 * i:512 * i + 512], 8, 512)
                if S.dry:
                    continue
                pd, pdr = PD.next()
                proj(pd[0:16, :], pdr, [zT[:, k, :] for k in range(8)], [w[:, k, :] for k in range(8)], (r_zT, wr))
                tm, tmr = TM.next()
                P.tt("dve", tm[0:16, :], pd[0:16, :], msec("G1")[0:16, 512 * i:512 * i + 512], ALU.mult, reads=(pdr, r_MOD), writes=(tmr,))
                P.tt("pool", XS[0:16, 512 * i:512 * i + 512], tm[0:16, :], XS[0:16, 512 * i:512 * i + 512], ALU.add, reads=(tmr, r_XS), writes=(r_XS,))
            if not S.dry:
                norm_mod(XS[0:16, :], r_XS, 16, "M2", "SH2", (hT, r_hT), 0)
            for fc in range(NFC):
                w, wr = P.slab(s_gu[:, 256 * fc:256 * fc + 256], 8, 256)
                if S.dry:
                    continue
                pd, pdr = PD.next()
                proj(pd[:, 0:16], pdr, [w[:, k, 0:128] for k in range(8)], rh, (r_hT, wr))
                proj(pd[:, 16:32], pdr, [w[:, k, 128:256] for k in range(8)], rh, (r_hT, wr))
                tm, tmr = TM.next()
                sigmoid_from(tm[:, 0:16], pd[:, 0:16], None, (pdr,), (tmr,), None)
                P.tt("dve", tm[:, 0:16], tm[:, 0:16], pd[:, 0:16], ALU.mult, reads=(tmr, pdr), writes=(tmr,))
                P.tt("dve", aT[:, fc, :], tm[:, 0:16], pd[:, 16:32], ALU.mult, reads=(tmr, pdr), writes=(r_aT,))
            for i in range(2):
                pd, pdr = None, None
                for s_ in range(3):
                    nk = 8 if s_ < 2 else 6
                    w, wr = P.slab(s_down[1024 * s_:1024 * s_ + 128 * nk, 512 * i:512 * i + 512], nk, 512)
                    if S.dry:
                        continue
                    if pd is None:
                        pd, pdr = PD.next()
                    for k in range(nk):
                        fc = 8 * s_ + k
                        P.mm(pd[0:16, :], aT[:, fc, :], w[:, k, :], start=(fc == 0), stop=(fc == NFC - 1), reads=(r_aT, wr), writes=(pdr,), signal=(k == nk - 1))
                if S.dry:
                    continue
                tm, tmr = TM.next()
                P.tt("dve", tm[0:16, :], pd[0:16, :], msec("G2")[0:16, 512 * i:512 * i + 512], ALU.mult, reads=(pdr, r_MOD), writes=(tmr,))
                P.tt("pool", XS[0:16, 512 * i:512 * i + 512], tm[0:16, :], XS[0:16, 512 * i:512 * i + 512], ALU.add, reads=(tmr, r_XS), writes=(r_XS,))
            if not S.dry:
                nf, nfr = xk.next()
                P.dma("pool", nf[:], norms[2:3, :].to_broadcast((128, D)), writes=(nfr,))
                rs, rsr = rstd_of(XS[0:16, :], r_XS, 16)
                yt, ytr = xk.next()
                P.stt(yt[0:16, :], XS[0:16, :], rs, nf[0:16, :], ALU.mult, ALU.mult, reads=(r_XS, rsr, nfr), writes=(ytr,))
                P.dma("pool", ys_d, yt[0:16, :], reads=(ytr,))
                S.barrier()

    def body():
        nonlocal KE, VS, KW, VW, KcT, VcT, Vc, CONVO
        stop = cfg.get("stop", 9)
        pes = ExitStack()
        KE = P.sb(pes, "KE", [128, 4, T], BF16)
        VS = P.sb(pes, "VS", [128, 32, 4, 66], BF16)
        KW = P.sb(pes, "KW", [128, 4, 1024], BF16)
        VW = P.sb(pes, "VW", [128, 8, 4, 66], BF16)
        KcT = P.sb(pes, "KcT", [64, 4, 260], BF16)
        VcT = P.sb(pes, "VcT", [64, 4, 260], BF16)
        Vc = P.sb(pes, "Vc", [128, 2, 4, 64], BF16)
        CONVO = P.sb(pes, "CONVO", [128, 8, 2], F32)
        for g in range(4):
            P.dma("sp", KE[64:128, g, :], t_E, writes=(r_KE[g],))
        P.memset("pool", VS[:, :, :, 64:66], 1.0, writes=(r_VS,))
        P.memset("pool", VW[:, :, :, 64:66], 1.0, writes=(r_VW,))
        P.memset("pool", KW[64:128, :, :], 0.0, writes=(r_KW,))
        P.memset("pool", KW[64:65, :, :], 1.0, writes=(r_KW,))
        P.memset("pool", KcT[:], 0.0, writes=(r_Kc,))
        P.memset("pool", VcT[:], 0.0, writes=(r_VcT,))
        P.memset("pool", Vc[:], 0.0, writes=(r_Vc,))
        P.memset("pool", CONVO[:], 0.0, writes=(r_CONVO,))
        if stop >= 2:
            compute_mod(0)
        for m in range(NU):
            if stop >= 3:
                kv_pass(m)
            if stop >= 4:
                own_unit(m)
        if not S.dry:
            with nc.allow_non_contiguous_dma(reason="tiny conv state"):
                for c in range(8):
                    P.dma("pool", oconv[:, c * 128:(c + 1) * 128].rearrange("k p -> p k"), CONVO[:, c, :], reads=(r_CONVO,), allow_slow_non_contiguous=True)
        S.barrier()
        pes.close()
        if cfg.get("sample", True):
            sample_phase()

    return P, body


def _lay(items):
    d, off = {}, 0
    for n, sz in items:
        d[n] = (off, sz)
        off += sz
    return d, off


TF_OFF, TF_N = _lay([("QPOS", 16), ("CI31", 256), ("CPOS", 256), ("QSB", 256), ("POSQ", 256), ("ALB", 576), ("HF", 32), ("EPS", 8)])
TB_OFF, TB_N = _lay([("DT", 512), ("WT", 1536), ("FBT", 1024)])


TSF_OFF, TSF_N = _lay([("SLC", 8), ("NQB", 8), ("CPS", 512), ("FBS", 128), ("SELG", 64), ("L2", 128), ("B2S", 4096), ("B2W", 256), ("NEWB", 64), ("PIO", 8)])
TSB_OFF, TSB_N = _lay([("WM0", 64), ("EFULL", PAST)])


def make_sample_tables():
    tf_ = np.zeros((128, TSF_N), np.float64)
    tb_ = np.zeros((128, TSB_N), np.float64)
    rho = np.arange(64)
    hh, qq = rho // 4, rho % 4
    sl = np.array(SLOPES)[hh]

    def put(tab, off, name, arr, rows):
        o, n = off[name]
        tab[0:rows, o:o + n] = np.asarray(arr).reshape(rows, n)
    put(tf_, TSF_OFF, "SLC", np.repeat(sl[:, None], 8, 1), 64)
    put(tf_, TSF_OFF, "NQB", np.repeat((-sl * qq)[:, None], 8, 1), 64)
    c = np.arange(512)
    put(tf_, TSF_OFF, "CPS", np.broadcast_to(16.0 * c + 15.5 - PAST, (64, 512)), 64)
    fbs = np.zeros((64, 128))
    fbs[:, 0] = 1000.0
    fbs[:, 127] = 1000.0
    put(tf_, TSF_OFF, "FBS", fbs, 64)
    selg = ((hh[:, None] // 4 == hh[None, :] // 4) & (qq[:, None] == qq[None, :])).astype(np.float64)
    put(tf_, TSF_OFF, "SELG", selg, 64)
    put(tf_, TSF_OFF, "L2", np.stack([np.arange(128.0), np.ones(128)]), 2)
    ck = np.arange(64)
    b2s = np.stack([np.broadcast_to(sl[None, :], (64, 64)), sl[None, :] * (128.0 * ck[:, None] - PAST - qq[None, :])])
    put(tf_, TSF_OFF, "B2S", b2s, 2)
    cw = np.arange(4)
    b2w = np.stack([np.broadcast_to(sl[None, :], (4, 64)), sl[None, :] * (PAST - 512 + 128.0 * cw[:, None] - PAST - qq[None, :])])
    put(tf_, TSF_OFF, "B2W", b2w, 2)
    j = np.arange(4)[:, None]
    put(tf_, TSF_OFF, "NEWB", np.where(j <= qq[None, :], sl[None, :] * (j - qq[None, :]), -BIG), 4)
    put(tf_, TSF_OFF, "PIO", np.repeat(np.arange(128.0)[:, None], 8, 1), 128)
    pk = np.arange(128)[:, None]
    put(tb_, TSB_OFF, "WM0", np.where(pk <= qq[None, :], -BIG, 0.0), 128)
    put(tb_, TSB_OFF, "EFULL", (np.arange(PAST)[None, :] // 64 == np.arange(128)[:, None]).astype(np.float64), 128)
    return tf_.astype(np.float32), tb_.astype(NPBF)


def make_tables(p):
    Sp = SUBS[p]
    f = np.arange(128)
    tfv = np.zeros((128, TF_N), np.float32)
    tbv = np.zeros((128, TB_N), np.float32)

    def put(tab, off, name, arr):
        o, n = off[name]
        tab[:, o:o + n] = arr.reshape(128, n)
    qpos = np.zeros((128, 8, 2), np.float64)
    for m in range(8):
        for a in range(2):
            qpos[:, m, a] = 512 * m + 128 * Sp[a] + f
    put(tfv, TF_OFF, "QPOS", qpos)
    ci = np.arange(256)
    put(tfv, TF_OFF, "CI31", np.broadcast_to(16.0 * ci + 31, (128, 256)))
    put(tfv, TF_OFF, "CPOS", np.broadcast_to(16.0 * ci + 15.5, (128, 256)))
    sl = np.array(SLOPES, np.float64)
    put(tfv, TF_OFF, "QSB", -qpos[:, :, :, None] * sl[None, None, None, :])
    posq = np.concatenate([128 * Sp[a] + f for a in range(2)]).astype(np.float64)
    put(tfv, TF_OFF, "POSQ", np.broadcast_to(posq, (128, 256)))
    rel = np.arange(36) - 32
    put(tfv, TF_OFF, "ALB", sl[None, :, None] * (128.0 * rel[None, None, :] + f[:, None, None]))
    hf = np.ones((128, 8, 4))
    for a in range(2):
        if Sp[a] == 0:
            hf[:, 0, 2 * a:2 * a + 2] = 0.0
    put(tfv, TF_OFF, "HF", hf)
    put(tfv, TF_OFF, "EPS", np.full((128, 8), EPS))
    pk = f[:, None]
    qf = f[None, :]
    tri_l = np.where(pk <= qf, 0.0, -BIG)
    tri_u = np.where(pk > qf, 0.0, -BIG)
    neg = np.full((128, 128), -BIG)
    zero = np.zeros((128, 128))
    dt = np.zeros((128, 4, 128))
    for i, (a, r) in enumerate(((0, 0), (0, 1), (1, 2), (1, 3))):
        dt[:, i, :] = tri_l if r == Sp[a] else zero
    put(tbv, TB_OFF, "DT", dt)
    wt = np.zeros((128, 12, 128))
    for i in range(12):
        a, rp = (0, i) if i < 6 else (1, i - 6 + 2)
        d = Sp[a] - (rp - 4)
        wt[:, i, :] = neg if (d < 0 or d > 4) else (tri_l if d == 0 else (tri_u if d == 4 else zero))
    put(tbv, TB_OFF, "WT", wt)
    fbt = np.zeros((128, 16, 64))
    j = np.arange(64)[None, :]
    for m in range(8):
        for a in range(2):
            cur = (qpos[:, m, a] // 64)[:, None]
            v = np.where(j > cur, -1e30, np.where((j == 0) | (j == cur) | (j == cur - 1), 1000.0, 0.0))
            fbt[:, 2 * m + a, :] = v
    put(tbv, TB_OFF, "FBT", fbt)
    return tfv, tbv.astype(NPBF)


_CACHE = {}


def get_program():
    key = tuple(sorted(CFG.items()))
    if key not in _CACHE:
        P, body = build_program(CFG)
        P.S.dry = True
        body()
        P.S.dry = False
        body()
        P.S.finish()
        _CACHE[key] = P
    return _CACHE[key]


def kernel(x_prompt, x_sample, c_prompt, c_sample, cache_cmp, cache_sel, cache_win, state_conv, page_table,
           w_ada, b_ada, norm1, w_in, w_conv, b_conv, w_out_conv, pe_cmp, w_phi1, w_phi2, w_o_nsa, w_out,
           norm2, w_gate, w_up, w_down, norm_f):
    A = lambda v: np.ascontiguousarray(np.asarray(v))
    x_prompt, x_sample, c_prompt, c_sample = A(x_prompt), A(x_sample), A(c_prompt), A(c_sample)
    win = A(w_in)[0]
    cols = lambda o, n: win[:, o:o + n]
    BG, CGo, XI, Qo, KC, VCo, KSo, VSo, KWo, VWo, NG, MG = 0, 1024, 2048, 3072, 4096, 4352, 4608, 4864, 5120, 5376, 5632, 5680
    parts = []
    for c in range(8):
        parts += [cols(BG + 128 * c, 128), cols(CGo + 128 * c, 128), cols(XI + 128 * c, 128)]
    parts.append(cols(Qo, 1024))
    for base in (KSo, KWo, KC, VCo):
        for g in range(4):
            parts += [cols(base + 64 * g, 64), cols(base + 64 * g, 64)]
    parts.append(cols(KC, 1536))
    parts += [cols(NG, 48), np.zeros((D, 16), np.float32)]
    w_inx = np.ascontiguousarray(np.concatenate(parts, axis=1))
    assert w_inx.shape == (D, NCX)
    woc, won = A(w_out_conv)[0], A(w_o_nsa)[0]
    parts = []
    for c in range(8):
        parts += [cols(MG + 128 * c, 128), cols(MG + 1024 + 128 * c, 128), woc[:, 128 * c:128 * c + 128], won[:, 128 * c:128 * c + 128]]
    w_d2 = np.ascontiguousarray(np.concatenate(parts, axis=1))
    wg, wu = A(w_gate)[0], A(w_up)[0]
    parts = []
    for fc in range(NFC):
        parts += [wg[:, 128 * fc:128 * fc + 128], wu[:, 128 * fc:128 * fc + 128]]
    w_gu = np.ascontiguousarray(np.concatenate(parts, axis=1))
    wc4 = np.concatenate([A(w_conv)[0], A(b_conv)], axis=0)
    wcv = np.ascontiguousarray(wc4.reshape(4, 8, 128).transpose(2, 1, 0))
    norms = np.ascontiguousarray(np.stack([A(norm1)[0], A(norm2)[0], A(norm_f)], axis=0))
    common = dict(w_inx=w_inx, w_d2=w_d2, w_ada=A(w_ada)[0], b_ada=A(b_ada), norms=norms, w_out=A(w_out)[0], w_gu=w_gu,
                  w_down=A(w_down)[0], w_phi1=A(w_phi1)[0].reshape(4096, 128), w_phi2=A(w_phi2)[0], pe_cmp=A(pe_cmp)[0], wcv=wcv,
                  t_ident=np.eye(128, dtype=np.float32))
    tE = (np.arange(T)[None, :] // 64 == np.arange(64)[:, None]).astype(np.float32).astype(NPBF)
    tabs = [make_tables(p) for p in range(2)]
    if CFG.get("sample", True):
        stab = make_sample_tables()
        ccmp_full = A(cache_cmp)[0].reshape(NPHYS * 128, 512)
        csel_full = A(cache_sel)[0].reshape(NPHYS * 128, 512)
    in_maps = []
    for c in range(8):
        b, p = c // 2, c % 2
        Sp = SUBS[p]
        xo = np.zeros((8, 260, D), np.float32)
        for m in range(8):
            for a in range(2):
                st = 512 * m + 128 * Sp[a]
                xo[m, a * 128:(a + 1) * 128] = x_prompt[b, st:st + 128]
                if st > 0:
                    xo[m, 256 + 2 * a:258 + 2 * a] = x_prompt[b, st - 2:st]
        ct = np.zeros((2, 128, D), np.float32)
        ct[0] = c_prompt[b][None, :]
        for t_ in range(16):
            ct[1, t_] = c_sample[4 * c + t_ // 4]
        mp = dict(common)
        mp.update(xall=x_prompt[b], xown=xo, ctok=ct, t_E=tE, t_f32=tabs[p][0], t_bf=tabs[p][1])
        if CFG.get("sample", True):
            mp.update(xs=np.ascontiguousarray(x_sample[4 * c:4 * c + 4].reshape(16, D)),
                      sconv=np.ascontiguousarray(A(state_conv)[0, 4 * c:4 * c + 4]),
                      cwin=np.ascontiguousarray(A(cache_win)[0, 4 * c:4 * c + 4].reshape(4, 512, 512)),
                      ptab=np.ascontiguousarray(A(page_table)[4 * c:4 * c + 4].astype(np.int32)),
                      ccmp=ccmp_full, csel=csel_full, ts_f32=stab[0], ts_bf=stab[1])
        in_maps.append(mp)
    P = get_program()
    ncores = CFG.get("ncores", 8)
    res = run_bass_kernel_spmd(P.nc, in_maps[:ncores], core_ids=list(range(ncores)))
    R = list(res.results)
    while len(R) < 8:
        R.append(R[0])
    y_prompt = np.zeros((4, T, D), np.float32)
    for c in range(8):
        b, p = c // 2, c % 2
        Sp = SUBS[p]
        for m in range(8):
            for a in range(2):
                st = 512 * m + 128 * Sp[a]
                y_prompt[b, st:st + 128] = R[c]["yown"][m, a * 128:(a + 1) * 128]
    new_cmp_p = np.stack([R[2 * b]["okv0"].reshape(T, 2, 4, 64) for b in range(4)])[None]
    new_sel_p = np.stack([R[2 * b]["okv1"].reshape(T, 2, 4, 64) for b in range(4)])[None]
    new_win_p = np.stack([R[2 * b]["owin"].reshape(512, 2, 4, 64) for b in range(4)])[None]
    new_conv_p = np.stack([R[2 * b]["oconv"] for b in range(4)])[None]
    z = lambda *s: np.zeros(s, np.float32)
    if not CFG.get("sample", True):
        return (y_prompt, z(32, 4, D), new_cmp_p, new_sel_p, new_win_p, new_conv_p,
                z(1, 32, 4, 2, 4, 64), z(1, 32, 4, 2, 4, 64), z(1, 32, 512, 2, 4, 64), z(1, 32, 2, 1024))
    y_sample = np.concatenate([R[c]["ys"].reshape(4, 4, D) for c in range(8)], axis=0)
    new_cmp_s = np.concatenate([R[c]["oskv0"].reshape(4, 4, 2, 4, 64) for c in range(8)], axis=0)[None]
    new_sel_s = np.concatenate([R[c]["oskv1"].reshape(4, 4, 2, 4, 64) for c in range(8)], axis=0)[None]
    new_win_s = np.concatenate([R[c]["owin_s"].reshape(4, 512, 2, 4, 64) for c in range(8)], axis=0)[None]
    new_conv_s = np.concatenate([R[c]["oconv_s"] for c in range(8)], axis=0)[None]
    return (y_prompt, y_sample, new_cmp_p, new_sel_p, new_win_p, new_conv_p, new_cmp_s, new_sel_s, new_win_s, new_conv_s)
```

```python
import numpy as np
import ml_dtypes
from contextlib import ExitStack
import concourse.bass as bass
import concourse.mybir as mybir
from concourse.bass_utils import run_bass_kernel_spmd

F32 = mybir.dt.float32
BF16 = mybir.dt.bfloat16
I32 = mybir.dt.int32
AF = mybir.ActivationFunctionType
ALU = mybir.AluOpType
AX = mybir.AxisListType
NPBF = ml_dtypes.bfloat16

D = 1024
T = 4096
NH = 16
HD = 64
DFF = 2816
NFC = DFF // 128
BIG = 30000.0
EPS = 1e-6
SLOPES = [2.0 ** (-(h + 1) / 2.0) for h in range(16)]
SUBS = ((0, 3), (1, 2))
PAST = 8192
NPG = 64
NPHYS = 2560

CFG = dict(units=8, sample=True, attn=True)


class Eng:
    def __init__(self, name):
        self.name = name
        self.q = []
        self.cnt = 0
        self.sid = None
        self.waited = {}
        self.dsids = []
        self.di = 0


class Res:
    __slots__ = ("w", "r")

    def __init__(self):
        self.w = {}
        self.r = {}


class Sched:
    def __init__(self, nc, es):
        self.nc = nc
        self.sems = []
        self.dry = False
        self.engs = {}
        for n in ("pe", "act", "dve", "pool", "sp"):
            e = Eng(n)
            e.sid = self.newsem(es, "c_" + n)
            self.engs[n] = e
        for n, k in (("sp", 12), ("pool", 12), ("act", 4)):
            self.engs[n].dsids = [self.newsem(es, "d_%s%d" % (n, i)) for i in range(k)]

    def newsem(self, es, name):
        s = es.enter_context(self.nc.semaphore(name))
        self.sems.append(s)
        return len(self.sems) - 1

    def _wait(self, eng, need):
        for sid, v in need.items():
            if eng.waited.get(sid, 0) < v:
                eng.waited[sid] = v
                sem = self.sems[sid]
                eng.q.append(lambda e, sem=sem, v=v: e.wait_ge(sem, v))

    def op(self, en, fn, reads=(), writes=(), signal=True, dma=False):
        if self.dry:
            return
        eng = self.engs[en]
        need = {}

        def add(tok):
            if need.get(tok[0], 0) < tok[1]:
                need[tok[0]] = tok[1]

        me = en + (":dma" if dma else "")
        for r in reads:
            for tok in r.w.values():
                add(tok)
        for w in writes:
            for tok in w.w.values():
                if en != "pe" or dma or tok[2] != me:
                    add(tok)
            for tok in w.r.values():
                if en != "pe" or dma or tok[2] != me:
                    add(tok)
        if dma:
            k = eng.di % len(eng.dsids)
            sid = eng.dsids[k]
            val = 16 * (eng.di // len(eng.dsids) + 1)
            eng.di += 1
            if val > 16:
                add((sid, val - 16, me))
            tok = (sid, val, me)
            self._wait(eng, need)
            sem = self.sems[sid]
            eng.q.append(lambda e, fn=fn, sem=sem: fn(e).then_inc(sem, 16))
        else:
            self._wait(eng, need)
            if signal:
                eng.cnt += 1
                tok = (eng.sid, eng.cnt, me)
                sem = self.sems[eng.sid]
                eng.q.append(lambda e, fn=fn, sem=sem: fn(e).then_inc(sem, 1))
            else:
                tok = (eng.sid, eng.cnt + 1, me)
                eng.q.append(lambda e, fn=fn: fn(e))
        for r in reads:
            r.r[tok[0]] = tok
        for w in writes:
            w.w = {tok[0]: tok}
            w.r = {}
        return tok

    def barrier(self):
        if self.dry:
            return
        need = {}
        for e in self.engs.values():
            if e.cnt:
                need[e.sid] = e.cnt
            for k, sid in enumerate(e.dsids):
                n = (e.di - k + len(e.dsids) - 1) // len(e.dsids) if e.di > k else 0
                if n:
                    need[sid] = 16 * n
        for e in self.engs.values():
            self._wait(e, dict(need))

    def finish(self):
        self.barrier()
        nc = self.nc
        q = self.engs
        with nc.Block() as block:
            @block.tensor
            def _(e):
                for f in q["pe"].q:
                    f(e)

            @block.scalar
            def _(e):
                for f in q["act"].q:
                    f(e)

            @block.vector
            def _(e):
                for f in q["dve"].q:
                    f(e)

            @block.gpsimd
            def _(e):
                for f in q["pool"].q:
                    f(e)

            @block.sync
            def _(e):
                for f in q["sp"].q:
                    f(e)


class Ring:
    def __init__(self, aps):
        self.aps = aps
        self.res = [Res() for _ in aps]
        self.i = 0

    def next(self):
        k = self.i % len(self.aps)
        self.i += 1
        return self.aps[k], self.res[k]


class Prog:
    def __init__(self):
        self.nc = bass.Bass("TRN2", target_bir_lowering=False)
        self.es = ExitStack()
        self.S = Sched(self.nc, self.es)
        self.slab_reqs = []
        self.slab_i = 0
        self.slab_issued = 0
        self.uid = 0

    def din(self, name, shape, dt=F32):
        return self.nc.dram_tensor(name, list(shape), dt, kind="ExternalInput").ap()

    def dout(self, name, shape, dt=F32):
        return self.nc.dram_tensor(name, list(shape), dt, kind="ExternalOutput").ap()

    def sb(self, es, name, shape, dt):
        self.uid += 1
        return es.enter_context(self.nc.sbuf_tensor("%s_%d" % (name, self.uid), list(shape), dt))

    def ps(self, es, name, shape, dt):
        self.uid += 1
        return es.enter_context(self.nc.psum_tensor("%s_%d" % (name, self.uid), list(shape), dt))

    def op(self, *a, **k):
        return self.S.op(*a, **k)

    def dma(self, q, out, in_, reads=(), writes=(), **kw):
        self.S.op(q, lambda e, out=out, in_=in_, kw=kw: e.dma_start(out=out, in_=in_, **kw), reads, writes, dma=True)

    def mm(self, out, lhsT, rhs, start, stop, reads=(), writes=(), signal=None, sgc=False):
        if signal is None:
            signal = stop
        self.S.op("pe", lambda e, out=out, lhsT=lhsT, rhs=rhs, start=start, stop=stop, sgc=sgc:
                  e.matmul(out, lhsT=lhsT, rhs=rhs, start=start, stop=stop, skip_group_check=sgc), reads, writes, signal=signal)

    def tr(self, out, in_, ident, reads=(), writes=(), signal=True):
        self.S.op("pe", lambda e, out=out, in_=in_, ident=ident: e.transpose(out, in_, ident), reads, writes, signal=signal)

    def act(self, out, in_, func, reads=(), writes=(), eng="act", **kw):
        self.S.op(eng, lambda e, out=out, in_=in_, func=func, kw=kw: e.activation(out=out, in_=in_, func=func, **kw), reads, writes)

    def tt(self, eng, out, in0, in1, op, reads=(), writes=()):
        self.S.op(eng, lambda e, out=out, in0=in0, in1=in1, op=op: e.tensor_tensor(out=out, in0=in0, in1=in1, op=op), reads, writes)

    def ts(self, eng, out, in0, s1, op0, s2=None, op1=None, reads=(), writes=(), accum=None):
        def f(e, out=out, in0=in0, s1=s1, op0=op0, s2=s2, op1=op1, accum=accum):
            kw = {}
            if op1 is not None:
                kw["op1"] = op1
            if accum is not None:
                kw["accum_out"] = accum
            return e.tensor_scalar(out=out, in0=in0, scalar1=s1, scalar2=s2, op0=op0, **kw)
        self.S.op(eng, f, reads, writes)

    def stt(self, out, in0, scalar, in1, op0, op1, reads=(), writes=()):
        self.S.op("dve", lambda e, out=out, in0=in0, scalar=scalar, in1=in1, op0=op0, op1=op1:
                  e.scalar_tensor_tensor(out=out, in0=in0, scalar=scalar, in1=in1, op0=op0, op1=op1), reads, writes)

    def cp(self, eng, out, in_, reads=(), writes=()):
        if eng == "act":
            self.S.op("act", lambda e, out=out, in_=in_: e.copy(out=out, in_=in_), reads, writes)
        else:
            self.S.op(eng, lambda e, out=out, in_=in_: e.tensor_copy(out=out, in_=in_), reads, writes)

    def memset(self, eng, ap, val, writes=()):
        self.S.op(eng, lambda e, ap=ap, val=val: e.memset(ap, val), (), writes)

    def slab(self, src, nk, ncols):
        if self.S.dry:
            self.slab_reqs.append((src, nk, ncols))
            return None, None
        k = self.slab_i
        self.slab_i += 1
        nb = len(self.slabs)
        while self.slab_issued < min(len(self.slab_reqs), k + nb):
            i = self.slab_issued
            s_, nk_, nc_ = self.slab_reqs[i]
            buf = self.slabs[i % nb]
            dst = buf[:, 0:nk_ * nc_].rearrange("p (k c) -> p k c", c=nc_)
            q = "sp"
            self.dma(q, dst, s_.rearrange("(k p) c -> p k c", p=128), reads=(), writes=(self.slab_res[i % nb],))
            self.slab_issued += 1
        buf = self.slabs[k % nb]
        return buf[:, 0:nk * ncols].rearrange("p (k c) -> p k c", c=ncols), self.slab_res[k % nb]


FM1_0 = 0
Q_0 = 3072
KD_0 = 4096
KV_0 = 6144
G_0 = 7680
NCX = 7744
ND2 = 4096


def build_program(cfg):
    P = Prog()
    nc = P.nc
    es = P.es
    S = P.S
    NU = cfg["units"]

    xall = P.din("xall", [T, D])
    xown = P.din("xown", [8, 260, D])
    ctok = P.din("ctok", [2, 128, D])
    w_inx = P.din("w_inx", [D, NCX])
    w_d2 = P.din("w_d2", [D, ND2])
    w_ada = P.din("w_ada", [D, 6 * D])
    b_ada = P.din("b_ada", [1, 6 * D])
    norms = P.din("norms", [3, D])
    w_out = P.din("w_out", [D, D])
    w_gu = P.din("w_gu", [D, 2 * DFF])
    w_down = P.din("w_down", [DFF, D])
    w_phi1 = P.din("w_phi1", [2 * 2048, 128])
    w_phi2 = P.din("w_phi2", [2, 128, 64])
    pe_cmp = P.din("pe_cmp", [2, 32, 64])
    wcv = P.din("wcv", [128, 8, 4])
    t_ident = P.din("t_ident", [128, 128])
    t_E = P.din("t_E", [64, T], BF16)
    t_f32 = P.din("t_f32", [128, TF_N])
    t_bf = P.din("t_bf", [128, TB_N], BF16)
    yown = P.dout("yown", [8, 256, D])
    okv = [P.dout("okv%d" % i, [T, 512]) for i in range(2)]
    owin = P.dout("owin", [512, 512])
    oconv = P.dout("oconv", [2, D])
    if cfg.get("sample", True):
        xs_d = P.din("xs", [16, D])
        sconv_d = P.din("sconv", [4, 2, D])
        cwin_d = P.din("cwin", [4, 512, 512])
        ptab_d = P.din("ptab", [4, NPG], I32)
        ccmp_d = P.din("ccmp", [NPHYS * 128, 512])
        csel_d = P.din("csel", [NPHYS * 128, 512])
        ts_f32 = P.din("ts_f32", [128, TSF_N])
        ts_bf = P.din("ts_bf", [128, TSB_N], BF16)
        ys_d = P.dout("ys", [16, D])
        oskv = [P.dout("oskv%d" % i, [16, 512]) for i in range(2)]
        owin_s = P.dout("owin_s", [4, 512, 512])
        oconv_s = P.dout("oconv_s", [4, 2, D])
        if cfg.get("dbg"):
            dbg_d = P.dout("dbg", [4, 3, 4, D])
    def scr(name, shape):
        return nc.dram_tensor(name, list(shape), BF16).ap()
    s_inx = scr("s_inx", [D, NCX])
    s_d2 = scr("s_d2", [D, ND2])
    s_ada = scr("s_ada", [D, 6 * D])
    s_out = scr("s_out", [D, D])
    s_gu = scr("s_gu", [D, 2 * DFF])
    s_down = scr("s_down", [DFF, D])
    s_phi1 = scr("s_phi1", [2 * 2048, 128])

    def sb(name, shape, dt):
        return P.sb(es, name, shape, dt)
    ident = sb("ident", [128, 128], BF16)
    TF = sb("TF", [128, TF_N], F32)
    TB = sb("TB", [128, TB_N], BF16)
    MOD = sb("MOD", [128, 6 * D], F32)
    KE = VS = KW = VW = KcT = VcT = Vc = CONVO = None
    R2 = sb("R2", [128, 8, 562], BF16)
    W2 = sb("W2", [128, 2, 64], BF16)
    PET = sb("PET", [128, 2, 16], BF16)
    PEB = sb("PEB", [128, 2], F32)
    WCV = sb("WCV", [128, 8, 4], F32)
    NSL = 2
    P.slabs = [sb("slab%d" % i, [128, 4096], BF16) for i in range(NSL)]
    P.slab_res = [Res() for _ in range(NSL)]
    xk = Ring([sb("xk%d" % i, [128, D], F32) for i in range(2)])
    hb = Ring([sb("hb%d" % i, [128, D], BF16) for i in range(2)])
    sm = Ring([sb("sm%d" % i, [128, 8], F32) for i in range(6)])
    PD = Ring([P.ps(es, "PD%d" % i, [128, 512], F32)[:] for i in range(3)])
    PTRt = P.ps(es, "PTR", [128, 1024], BF16)
    PTR = Ring([PTRt[:]])
    PSt = [P.ps(es, "PS%d" % i, [128, 512], F32) for i in range(2)]
    PS = Ring([PSt[0][:, 0:256], PSt[1][:, 0:256], PD.aps[0][:, 0:256], PD.aps[1][:, 0:256]])
    PS.res[2], PS.res[3] = PD.res[0], PD.res[1]
    POt = [P.ps(es, "PO%d" % i, [128, 512], F32) for i in range(2)]
    PO = Ring([POt[0][:, 0:130], POt[1][:, 0:130]])
    r_ident, r_TF, r_TB, r_MOD, r_NFR = Res(), Res(), Res(), Res(), Res()
    r_KE = [Res() for _ in range(4)]
    r_VS, r_KW, r_VW, r_Kc, r_VcT, r_Vc, r_R2, r_W2, r_PEB, r_WCV, r_CONVO = (Res() for _ in range(11))

    def tf(name):
        o, n = TF_OFF[name]
        return TF[:, o:o + n]

    def tb(name):
        o, n = TB_OFF[name]
        return TB[:, o:o + n]
    QPOS = tf("QPOS").rearrange("p (m a) -> p m a", a=2)
    CI31 = tf("CI31")
    CPOS = tf("CPOS")
    QSB = tf("QSB").rearrange("p (m a h) -> p m a h", a=2, h=16)
    FBT = tb("FBT").rearrange("p (u j) -> p u j", j=64)
    POSQ = tf("POSQ")
    ALB = tf("ALB").rearrange("p (h r) -> p h r", r=36)
    HF = tf("HF").rearrange("p (m k) -> p m k", k=4)
    DT = tb("DT").rearrange("p (i q) -> p i q", q=128)
    WT = tb("WT").rearrange("p (i q) -> p i q", q=128)

    MSEC = dict(SH1=0, M1=1, G1=2, SH2=3, M2=4, G2=5)

    def msec(name):
        k = MSEC[name]
        return MOD[:, k * D:(k + 1) * D]

    def precast(src, dst, rows, cols):
        for r0 in range(0, rows, 128):
            for c0 in range(0, cols, 4096):
                cw = min(4096, cols - c0)
                k = P.slab_i
                P.slab_i += 1
                buf, rr = P.slabs[k % NSL], P.slab_res[k % NSL]
                P.dma("pool", buf[:, 0:cw], src[r0:r0 + 128, c0:c0 + cw], writes=(rr,), max_dma_last_dim=4096)
                P.dma("sp", dst[r0:r0 + 128, c0:c0 + cw], buf[:, 0:cw], reads=(rr,))

    def rstd_of(xt, xr, n):
        junk, jr = hb.next()
        s1, s1r = sm.next()
        P.act(junk[0:n, :], xt, AF.Square, reads=(xr,), writes=(jr, s1r), accum_out=s1[0:n, 0:1])
        P.act(s1[0:n, 1:2], s1[0:n, 0:1], AF.Ln, reads=(s1r,), writes=(s1r,), scale=1.0 / D, bias=TF[0:n, TF_OFF["EPS"][0]:TF_OFF["EPS"][0] + 1])
        P.act(s1[0:n, 2:3], s1[0:n, 1:2], AF.Exp, reads=(s1r,), writes=(s1r,), scale=-0.5)
        return s1[0:n, 2:3], s1r

    def norm_mod(xt, xr, n, Mn, SHn, dst, col):
        rs, rsr = rstd_of(xt, xr, n)
        tmp, tr_ = xk.next()
        P.stt(tmp[0:n, :], xt, rs, msec(Mn)[0:n, :], ALU.mult, ALU.mult, reads=(xr, rsr, r_MOD), writes=(tr_,))
        h, hr = hb.next()
        P.tt("pool", h[0:n, :], tmp[0:n, :], msec(SHn)[0:n, :], ALU.add, reads=(tr_, r_MOD), writes=(hr,))
        dstap, dstres = dst
        pt, ptr_ = PTR.next()
        for kc in range(8):
            P.tr(pt[:, kc * 128:kc * 128 + n], h[0:n, kc * 128:(kc + 1) * 128], ident[0:n, 0:n],
                 reads=(hr, r_ident), writes=(ptr_,), signal=(kc == 7))
        ptv = pt.rearrange("p (j t) -> p j t", t=128)
        P.nm_i = getattr(P, "nm_i", 0) + 1
        P.cp("act" if P.nm_i % 2 == 0 else "dve", dstap[:, 0:8, col:col + n], ptv[:, 0:8, 0:n], reads=(ptr_,), writes=(dstres,))

    def proj(out_ps, out_res, lhs_list, rhs_list, reads):
        n = len(lhs_list)
        for k in range(n):
            P.mm(out_ps, lhs_list[k], rhs_list[k], start=(k == 0), stop=(k == n - 1), reads=reads, writes=(out_res,))

    def sigmoid_from(out, src, shape_cols, reads, writes, tmpring):
        P.act(out, src, AF.Exp, reads=reads, writes=writes, scale=-1.0)
        P.ts("dve", out, out, 1.0, ALU.add, reads=writes, writes=writes)
        P.op("dve", lambda e, out=out: e.reciprocal(out=out, in_=out), reads=writes, writes=writes)

    P.dma("pool", ident[:], t_ident, writes=(r_ident,))
    P.dma("sp", TF[:], t_f32, writes=(r_TF,))
    P.dma("sp", TB[:], t_bf, writes=(r_TB,))
    P.dma("sp", WCV[:], wcv, writes=(r_WCV,))
    P.dma("pool", W2[:], w_phi2.rearrange("k p d -> p k d"), writes=(r_W2,))
    with nc.allow_non_contiguous_dma(reason="tiny pe table"):
        P.dma("pool", PET[:], pe_cmp.rearrange("k (j s) d -> (s d) k j", s=2), writes=(r_TB,), allow_slow_non_contiguous=True)
    P.memset("pool", R2[:], 0.0, writes=(r_R2,))
    if not S.dry:
        precast(w_phi1, s_phi1, 4096, 128)
        precast(w_ada, s_ada, D, 6 * D)
        precast(w_inx, s_inx, D, NCX)
        precast(w_d2, s_d2, D, ND2)
        precast(w_out, s_out, D, D)
        precast(w_gu, s_gu, D, 2 * DFF)
        precast(w_down, s_down, DFF, D)
        S.barrier()
        P.slab_i = 0
        P.slab_res = [Res() for _ in range(NSL)]

    def compute_mod(kind):
        with ExitStack() as ph:
            cT = P.sb(ph, "cT", [128, 8, 128], BF16)
            bt = Ring([P.sb(ph, "bt%d" % i, [128, 512], F32) for i in range(2)])
            r_cT = Res()
            ct, cr = xk.next()
            P.dma("pool", ct[:], ctok[kind], writes=(cr,))
            e1, e1r = xk.next()
            sigmoid_from(e1[:], ct[:], None, (cr,), (e1r,), None)
            sl, slr = hb.next()
            P.tt("dve", sl[:], ct[:], e1[:], ALU.mult, reads=(cr, e1r), writes=(slr,))
            mstop = cfg.get("mstop", 9)
            if mstop < 2:
                S.barrier()
                return
            pt, ptr_ = PTR.next()
            for kc in range(8):
                P.tr(pt[:, kc * 128:(kc + 1) * 128], sl[:, kc * 128:(kc + 1) * 128], ident[:], reads=(slr, r_ident), writes=(ptr_,), signal=(kc == 7))
            P.cp("act", cT[:], pt.rearrange("p (j t) -> p j t", t=128), reads=(ptr_,), writes=(r_cT,))
            if mstop < 3:
                S.barrier()
                return
            for n in range(12 if mstop >= 4 else 1):
                w, wr = P.slab(s_ada[:, n * 512:(n + 1) * 512], 8, 512)
                b_, br = bt.next()
                P.dma("pool", b_[:], b_ada[0:1, n * 512:(n + 1) * 512].to_broadcast((128, 512)), writes=(br,))
                if S.dry:
                    continue
                pd, pdr = PD.next()
                proj(pd, pdr, [cT[:, k, :] for k in range(8)], [w[:, k, :] for k in range(8)], (r_cT, wr))
                P.tt("dve", MOD[:, n * 512:(n + 1) * 512], pd, b_[:], ALU.add, reads=(pdr, br), writes=(r_MOD,))
            if mstop < 5:
                S.barrier()
                return
            for sec, nrow in ((1, 0), (4, 1)):
                nr, nrr = xk.next()
                P.dma("pool", nr[:], norms[nrow:nrow + 1, :].to_broadcast((128, D)), writes=(nrr,))
                P.stt(MOD[:, sec * D:(sec + 1) * D], MOD[:, sec * D:(sec + 1) * D], 1.0, nr[:], ALU.add, ALU.mult,
                      reads=(r_MOD, nrr), writes=(r_MOD,))
            S.barrier()

    def compress_block(sb_, KcTd, rKc, VcTd, rVc, XG, TG, GG, r_XG, r_TG, r_GG, R2x=None, rR2=None, nb=34, ntok=512):
        if R2x is None:
            R2x, rR2 = R2, r_R2
        pc, pcr = PD.next()
        for kv in range(2):
            w1, w1r = P.slab(s_phi1[kv * 2048:(kv + 1) * 2048, :], 16, 128)
            if S.dry:
                continue
            for g in range(4):
                gi = kv * 4 + g
                for j in range(16):
                    P.mm(pc[:, gi * nb:(gi + 1) * nb], w1[:, j, :], R2x[:, gi, 2 * j:2 * j + 16 * (nb - 1) + 1:16],
                         start=(j == 0), stop=(j == 15), reads=(w1r, rR2), writes=(pcr,))
            if not hasattr(P, "peb_done"):
                pb_, pbr = PD.next()
                for j in range(16):
                    P.mm(pb_[:, 0:1], w1[:, j, :], PET[:, kv, j:j + 1], start=(j == 0), stop=(j == 15), reads=(w1r, r_TB), writes=(pbr,))
                P.cp("dve", PEB[:, kv:kv + 1], pb_[:, 0:1], reads=(pbr,), writes=(r_PEB,))
        if S.dry:
            return
        P.peb_done = True
        for kv in range(2):
            P.ts("dve", XG[:, kv * 4 * nb:(kv + 1) * 4 * nb], pc[:, kv * 4 * nb:(kv + 1) * 4 * nb], PEB[:, kv:kv + 1], ALU.add,
                 reads=(pcr, r_PEB), writes=(r_XG,))
        P.tt("dve", TG[:], XG[:], XG[:], ALU.mult, reads=(r_XG,), writes=(r_TG,))
        P.ts("dve", TG[:], TG[:], 0.044715, ALU.mult, 1.0, ALU.add, reads=(r_TG,), writes=(r_TG,))
        P.tt("dve", TG[:], TG[:], XG[:], ALU.mult, reads=(r_TG, r_XG), writes=(r_TG,))
        P.act(TG[:], TG[:], AF.Exp, reads=(r_TG,), writes=(r_TG,), scale=-1.5957691216057308)
        P.ts("dve", TG[:], TG[:], 1.0, ALU.add, reads=(r_TG,), writes=(r_TG,))
        P.op("dve", lambda e: e.reciprocal(out=TG[:], in_=TG[:]), reads=(r_TG,), writes=(r_TG,))
        P.tt("dve", GG[:], XG[:], TG[:], ALU.mult, reads=(r_TG, r_XG), writes=(r_GG,))
        p2, p2r = PD.next()
        for gi in range(8):
            P.mm(p2[0:64, gi * nb:(gi + 1) * nb], W2[:, gi // 4, :], GG[:, gi * nb:(gi + 1) * nb], start=True, stop=True,
                 reads=(r_W2, r_GG), writes=(p2r,), signal=(gi == 7))
        base = (ntok // 16) * sb_
        P.cp("act", KcTd[0:64, :, base + 1:base + 1 + nb], p2[0:64, 0:4 * nb].rearrange("p (g i) -> p g i", i=nb), reads=(p2r,), writes=(rKc,))
        P.cp("act", VcTd[0:64, :, base + 1:base + 1 + nb], p2[0:64, 4 * nb:8 * nb].rearrange("p (g i) -> p g i", i=nb), reads=(p2r,), writes=(rVc,))
        P.cp("pool", R2x[:, :, 0:16], R2x[:, :, ntok:ntok + 16], reads=(rR2,), writes=(rR2,))

    def kv_pass(sb_):
        slot = sb_ % 2
        with ExitStack() as ph:
            hT = P.sb(ph, "hTkv", [128, 8, 512], BF16)
            r_hT = Res()
            stage = Ring([P.sb(ph, "stg%d" % i, [128, 512], F32) for i in range(2)])
            XG = P.sb(ph, "XG", [128, 272], F32)
            TG = P.sb(ph, "TG", [128, 272], F32)
            GG = P.sb(ph, "GG", [128, 272], BF16)
            r_XG, r_TG, r_GG = Res(), Res(), Res()
            for tt_ in range(4):
                xt, xr = xk.next()
                P.dma("pool", xt[:], xall[sb_ * 512 + tt_ * 128: sb_ * 512 + tt_ * 128 + 128, :], writes=(xr,))
                norm_mod(xt[:], xr, 128, "M1", "SH1", (hT, r_hT), tt_ * 128)
            kstop = cfg.get("kstop", 9)
            for i in range(8 if kstop >= 2 else 0):
                w, wr = P.slab(s_inx[:, KD_0 + 256 * i: KD_0 + 256 * i + 256], 8, 256)
                if S.dry:
                    continue
                for j in range(2):
                    ti = 2 * i + j
                    kind, g = ti // 4, ti % 4
                    pd, pdr = PD.next()
                    proj(pd, pdr, [w[:, k, j * 128:(j + 1) * 128] for k in range(8)], [hT[:, k, :] for k in range(8)], (r_hT, wr))
                    if kind == 0:
                        P.cp("act", KE[0:64, g, sb_ * 512:(sb_ + 1) * 512], pd[0:64, :], reads=(pdr,), writes=(r_KE[g],))
                    elif kind == 1:
                        P.cp("act", KW[0:64, g, slot * 512:(slot + 1) * 512], pd[0:64, :], reads=(pdr,), writes=(r_KW,))
                    else:
                        gi = (kind - 2) * 4 + g
                        P.cp("dve", R2[0:64, gi, 16:528], pd[0:64, :], reads=(pdr,), writes=(r_R2,))
                        P.cp("act", R2[64:128, gi, 15:527], pd[64:128, :], reads=(pdr,), writes=(r_R2,))
            compress_block(sb_, KcT, r_Kc, VcT, r_VcT, XG, TG, GG, r_XG, r_TG, r_GG)
            if not S.dry:
                for cc in range((32 * sb_ + 31) // 128 + 1):
                    for g in range(4 if kstop >= 3.4 else 0):
                        pt, ptr_ = PTR.next()
                        P.tr(pt[:, 0:64], VcT[0:64, g, 128 * cc + 2:128 * cc + 130], ident[0:64, 0:64], reads=(r_VcT, r_ident), writes=(ptr_,))
                        P.cp("dve", Vc[:, cc, g, :], pt[:, 0:64], reads=(ptr_,), writes=(r_Vc,))
            for i in range(3 if kstop >= 5 else 0):
                w, wr = P.slab(s_inx[:, KV_0 + 512 * i: KV_0 + 512 * i + 512], 8, 512)
                if S.dry:
                    continue
                for tt_ in range(4):
                    pd, pdr = PD.next()
                    proj(pd, pdr, [hT[:, k, tt_ * 128:(tt_ + 1) * 128] for k in range(8)], [w[:, k, :] for k in range(8)], (r_hT, wr))
                    st, sr = stage.next()
                    P.cp("act" if tt_ % 2 == 0 else "dve", st[:], pd, reads=(pdr,), writes=(sr,))
                    t0 = sb_ * 512 + tt_ * 128
                    if i < 2:
                        P.dma("pool", okv[i][t0:t0 + 128, :], st[:], reads=(sr,))
                    elif sb_ == 7:
                        P.dma("pool", owin[tt_ * 128:(tt_ + 1) * 128, :], st[:], reads=(sr,))
                    if i == 1:
                        P.cp("pool", VS[:, sb_ * 4 + tt_, :, 0:64], st[:, 256:512].rearrange("p (g d) -> p g d", d=64), reads=(sr,), writes=(r_VS,))
                    elif i == 2:
                        P.cp("pool", VW[:, slot * 4 + tt_, :, 0:64], st[:, 256:512].rearrange("p (g d) -> p g d", d=64), reads=(sr,), writes=(r_VW,))
            S.barrier()

    def attention(m, QB, r_QB, GATES, r_G, OB, r_OB, ph):
        N = 32 * m + 32
        NS = N // 4
        ncc = (N + 127) // 128
        CVB = P.sb(ph, "CVB", [128, 256], BF16)
        SC = Ring([P.sb(ph, "SC%d" % i, [128, 256], F32) for i in range(2)])
        EC = Ring([P.sb(ph, "EC%d" % i, [128, 256], F32) for i in range(2)])
        PB = Ring([P.sb(ph, "PB%d" % i, [128, 256], BF16) for i in range(2)])
        PcT = Ring([P.sb(ph, "PcT%d" % i, [128, 2, 128], BF16) for i in range(2)])
        IMP = Ring([P.sb(ph, "IMP%d" % i, [128, 256], F32) for i in range(2)])
        I64 = P.sb(ph, "I64", [128, 64], F32)
        SCO = P.sb(ph, "SCO", [128, 64], F32)
        WRK = P.sb(ph, "WRK", [128, 64], F32)
        M8 = P.sb(ph, "M8", [128, 16], F32)
        SELB = Ring([P.sb(ph, "SELB%d" % i, [128, 128], BF16) for i in range(2)])
        PT = Ring([P.sb(ph, "PT%d" % i, [128, 256], BF16) for i in range(4)])
        OH = Ring([P.sb(ph, "OH%d" % i, [128, 2, 64], F32) for i in range(2)])
        RS = Ring([P.sb(ph, "RS%d" % i, [128, 4], F32) for i in range(4)])
        r_CVB, r_ZER, r_I64, r_SCO, r_WRK, r_M8 = (Res() for _ in range(6))
        for sbuf_, sres in zip(SELB.aps, SELB.res):
            P.memset("pool", sbuf_[:, 0:64], 0.0, writes=(sres,))
            P.memset("pool", sbuf_[:, 64:128], -BIG, writes=(sres,))
        for a in range(2):
            P.ts("dve", CVB[:, 0:N], CI31[:, 0:N], QPOS[:, m, a:a + 1], ALU.is_gt, -BIG, ALU.mult, reads=(r_TF,), writes=(r_CVB,))
            for g in range(4):
                imp, impr = IMP.next()
                for r in range(4):
                    h = 4 * g + r
                    pc, pcr = PS.next()
                    P.mm(pc[:, 0:N], QB[0:64, h, a * 128:(a + 1) * 128], KcT[0:64, g, 2:2 + N], start=True, stop=False, reads=(r_QB[h], r_Kc), writes=(pcr,))
                    P.mm(pc[:, 0:N], ident[:], CVB[:, 0:N], start=False, stop=True, reads=(r_ident, r_CVB), writes=(pcr,))
                    sc, scr_ = SC.next()
                    P.stt(sc[:, 0:N], CPOS[:, 0:N], SLOPES[h], pc[:, 0:N], ALU.mult, ALU.add, reads=(pcr, r_TF), writes=(scr_,))
                    ec, ecr = EC.next()
                    rs, rsr = RS.next()
                    P.act(ec[:, 0:N], sc[:, 0:N], AF.Exp, reads=(scr_, r_TF), writes=(ecr, rsr), bias=QSB[:, m, a, h:h + 1], accum_out=rs[:, 0:1])
                    P.ts("dve", rs[:, 1:2], rs[:, 0:1], 1e-30, ALU.add, reads=(rsr,), writes=(rsr,))
                    P.op("dve", lambda e, rs=rs: e.reciprocal(out=rs[:, 2:3], in_=rs[:, 1:2]), reads=(rsr,), writes=(rsr,))
                    if r == 0:
                        P.ts("dve", imp[:, 0:N], ec[:, 0:N], rs[:, 2:3], ALU.mult, reads=(ecr, rsr), writes=(impr,))
                    else:
                        P.stt(imp[:, 0:N], ec[:, 0:N], rs[:, 2:3], imp[:, 0:N], ALU.mult, ALU.add, reads=(ecr, rsr, impr), writes=(impr,))
                    pb, pbr = PB.next()
                    P.ts("dve", pb[:, 0:N], ec[:, 0:N], rs[:, 2:3], ALU.mult, reads=(ecr, rsr), writes=(pbr,))
                    pct, pctr = PcT.next()
                    for cc in range(ncc):
                        w_ = min(128, N - 128 * cc)
                        pt, ptr_ = PTR.next()
                        P.tr(pt[0:w_, 0:128], pb[:, 128 * cc:128 * cc + w_], ident[:], reads=(pbr, r_ident), writes=(ptr_,))
                        P.cp("act", pct[0:w_, cc, :], pt[0:w_, 0:128], reads=(ptr_,), writes=(pctr,))
                    po_, por = PO.next()
                    po = po_[:, 0:64]
                    for cc in range(ncc):
                        w_ = min(128, N - 128 * cc)
                        P.mm(po, pct[0:w_, cc, :], Vc[0:w_, cc, g, :], start=(cc == 0), stop=(cc == ncc - 1), reads=(pctr, r_Vc), writes=(por,))
                    P.ts("dve", OB[:, a, 64 * h:64 * h + 64], po, GATES[:, a, 3 * h:3 * h + 1], ALU.mult, reads=(por, r_G), writes=(r_OB,))
                P.op("dve", lambda e, imp=imp: e.tensor_reduce(out=I64[:, 0:NS], in_=imp[:, 0:N].rearrange("p (j k) -> p j k", k=4), axis=AX.X, op=ALU.add),
                     reads=(impr,), writes=(r_I64,))
                if NS > 1:
                    P.tt("dve", I64[:, 1:NS], I64[:, 1:NS], imp[:, 3:N - 4:4], ALU.add, reads=(impr, r_I64), writes=(r_I64,))
                P.tt("dve", SCO[:, 0:NS], I64[:, 0:NS], FBT[:, 2 * m + a, 0:NS], ALU.add, reads=(r_I64, r_TB), writes=(r_SCO,))
                sel, selr = SELB.next()
                if NS >= 16:
                    P.op("dve", lambda e: e.max(out=M8[:, 0:8], in_=SCO[:, 0:NS]), reads=(r_SCO,), writes=(r_M8,))
                    P.op("dve", lambda e: e.match_replace(out=WRK[:, 0:NS], in_to_replace=M8[:, 0:8], in_values=SCO[:, 0:NS], imm_value=-3.0e38),
                         reads=(r_SCO, r_M8), writes=(r_WRK,))
                    P.op("dve", lambda e: e.max(out=M8[:, 8:16], in_=WRK[:, 0:NS]), reads=(r_WRK,), writes=(r_M8,))
                    P.ts("dve", M8[:, 15:16], M8[:, 15:16], -1e29, ALU.max, reads=(r_M8,), writes=(r_M8,))
                    P.ts("dve", sel[:, 64:64 + NS], SCO[:, 0:NS], M8[:, 15:16], ALU.is_lt, -BIG, ALU.mult, reads=(r_SCO, r_M8), writes=(selr,))
                else:
                    P.ts("dve", sel[:, 64:64 + NS], SCO[:, 0:NS], -1e29, ALU.is_lt, -BIG, ALU.mult, reads=(r_SCO,), writes=(selr,))
                pt, ptr_ = PTR.next()
                P.tr(pt[:, 0:128], sel[:], ident[:], reads=(selr, r_ident), writes=(ptr_,))
                for r in range(4):
                    h = 4 * g + r
                    P.stt(QB[64:128, h, a * 128:(a + 1) * 128], POSQ[64:128, a * 128:(a + 1) * 128], -SLOPES[h], pt[64:128, 0:128],
                          ALU.mult, ALU.add, reads=(ptr_, r_TF), writes=(r_QB[h],))
        tiles = []
        for h in range(16):
            g = h // 4
            for br in (1, 2):
                lst = []
                if br == 1:
                    for j in range(4 * m + 4):
                        r = j - 4 * m
                        c0, c1 = (0, 256) if r <= 1 else (128, 256)
                        masks = {0: [(0, DT[:, 0, :])], 1: [(0, DT[:, 1, :])], 2: [(1, DT[:, 2, :])], 3: [(1, DT[:, 3, :])]}.get(r, [])
                        pv = [a for a in (0, 1) if (a == 1 or r <= 1)]
                        lst.append(dict(k=KE[:, g, j * 128:(j + 1) * 128], kres=r_KE[g], c0=c0, c1=c1, masks=masks,
                                        bias=ALB[:, h, 32 + r:33 + r], v=VS[:, j, g, 0:65], vres=r_VS, pv=pv))
                else:
                    for rp in range(8):
                        j = 4 * m - 4 + rp
                        if j < 0:
                            continue
                        sl = (j // 4) % 2
                        c0, c1 = (0, 128) if rp <= 1 else ((0, 256) if rp <= 5 else (128, 256))
                        masks = []
                        if rp <= 5:
                            masks.append((0, WT[:, rp, :]))
                        if rp >= 2:
                            masks.append((1, WT[:, 6 + rp - 2, :]))
                        pv = [a for a in (0, 1) if (a == 0 and rp <= 5) or (a == 1 and rp >= 2)]
                        lst.append(dict(k=KW[:, g, sl * 512 + (j % 4) * 128: sl * 512 + (j % 4) * 128 + 128], kres=r_KW, c0=c0, c1=c1, masks=masks,
                                        bias=ALB[:, h, 32 + rp - 4:33 + rp - 4], v=VW[:, sl * 4 + j % 4, g, 0:65], vres=r_VW, pv=pv))
                for a in (0, 1):
                    idx = [i for i, t_ in enumerate(lst) if a in t_["pv"]]
                    for i, t_ in enumerate(lst):
                        t_.setdefault("first", {})[a] = (i == idx[0])
                        t_.setdefault("last", {})[a] = (i == idx[-1])
                for i, t_ in enumerate(lst):
                    t_["h"] = h
                    t_["br"] = br
                    t_["end"] = (i == len(lst) - 1)
                    t_["begin"] = (i == 0)
                tiles.extend(lst)
        nt = len(tiles)
        state = {}

        def stageA(t_):
            ps, psr = PS.next()
            t_["ps"], t_["psr"] = ps, psr
            c0, c1 = t_["c0"], t_["c1"]
            nm = len(t_["masks"])
            P.mm(ps[:, c0:c1], t_["k"], QB[:, t_["h"], c0:c1], start=True, stop=(nm == 0), reads=(t_["kres"], r_QB[t_["h"]]), writes=(psr,))
            for i, (a, tab) in enumerate(t_["masks"]):
                P.mm(ps[:, a * 128:(a + 1) * 128], ident[:], tab, start=False, stop=(i == nm - 1), reads=(r_ident, r_TB), writes=(psr,))

        def stageB(t_):
            pt, ptr_ = PT.next()
            t_["pt"], t_["ptr"] = pt, ptr_
            c0, c1 = t_["c0"], t_["c1"]
            P.act(pt[:, c0:c1], t_["ps"][:, c0:c1], AF.Exp, reads=(t_["psr"], r_TF), writes=(ptr_,), bias=t_["bias"])

        def stageC(t_):
            if t_["begin"]:
                state["po"] = PO.next()
            po, por = state["po"]
            pov = po.rearrange("p (a d) -> p a d", d=65)
            for a in t_["pv"]:
                P.mm(pov[:, a, :], t_["pt"][:, a * 128:(a + 1) * 128], t_["v"], start=(t_["begin"] and a == t_["pv"][0]), stop=t_["last"][a],
                     reads=(t_["ptr"], t_["vres"]), writes=(por,), signal=True, sgc=True)
            if t_["end"]:
                h, br = t_["h"], t_["br"]
                rs, rsr = RS.next()
                P.op("dve", lambda e, rs=rs, pov=pov: e.reciprocal(out=rs[:, 0:2], in_=pov[:, :, 64]), reads=(por,), writes=(rsr,))
                P.tt("dve", rs[:, 2:4], rs[:, 0:2], GATES[:, :, 3 * h + br], ALU.mult, reads=(rsr, r_G), writes=(rsr,))
                if br == 1:
                    state["oh"] = OH.next()
                oh, ohr = state["oh"]
                for a in (0, 1):
                    if br == 1:
                        P.ts("dve", oh[:, a, :], pov[:, a, 0:64], rs[:, 2 + a:3 + a], ALU.mult, reads=(por, rsr), writes=(ohr,))
                    else:
                        P.stt(oh[:, a, :], pov[:, a, 0:64], rs[:, 2 + a:3 + a], oh[:, a, :], ALU.mult, ALU.add, reads=(por, rsr, ohr), writes=(ohr,))
                if br == 2:
                    P.tt("pool", OB[:, :, 64 * h:64 * h + 64], oh[:], OB[:, :, 64 * h:64 * h + 64], ALU.add, reads=(ohr, r_OB), writes=(r_OB,))

        for t in range(nt + 2):
            if t < nt:
                stageA(tiles[t])
            if 1 <= t <= nt:
                stageB(tiles[t - 1])
            if t >= 2:
                stageC(tiles[t - 2])

    def own_unit(m):
        with ExitStack() as ph:
            hT = P.sb(ph, "hTo", [128, 8, 260], BF16)
            REG = P.sb(ph, "REG", [128, 6144], BF16)
            YA = REG[:, 0:2048].rearrange("p (c t) -> p c t", t=256)
            oT = REG[:, 2048:4096].rearrange("p (c t) -> p c t", t=256)
            zT = REG[:, 4096:6144].rearrange("p (c t) -> p c t", t=256)
            aT = REG[:, 0:5632].rearrange("p (c t) -> p c t", t=256)
            XO = P.sb(ph, "XO", [128, 2, D], F32)
            QB = P.sb(ph, "QB", [128, 16, 256], BF16)
            OB = P.sb(ph, "OB", [128, 2, D], BF16)
            GATES = P.sb(ph, "GATES", [128, 2, 48], F32)
            CG = P.sb(ph, "CG", [128, 260], F32)
            UP = P.sb(ph, "UP", [128, 2, 130], F32)
            VV = P.sb(ph, "VV", [128, 2, 128], F32)
            SG = P.sb(ph, "SG", [128, 512], F32)
            TM = Ring([P.sb(ph, "TM%d" % i, [128, 512], F32) for i in range(2)])
            r_hT, r_YA, r_oT, r_zT, r_aT, r_G, r_OB, r_CG, r_UP, r_VV, r_SG, r_Z1, r_Z2 = (Res() for _ in range(13))
            r_XO = [Res(), Res()]
            r_QB = [Res() for _ in range(16)]
            TMN = (REG[:, 0:2048].bitcast(F32), r_aT)
            for a in range(2):
                P.dma("pool", XO[:, a, :], xown[m, a * 128:(a + 1) * 128, :], writes=(r_XO[a],))
                norm_mod(XO[:, a, :], r_XO[a], 128, "M1", "SH1", (hT, r_hT), a * 128)
            xh, xhr = xk.next()
            P.dma("pool", xh[0:4, :], xown[m, 256:260, :], writes=(xhr,))
            norm_mod(xh[0:4, :], xhr, 4, "M1", "SH1", (hT, r_hT), 256)
            for c in range(8):
                w, wr = P.slab(s_inx[:, FM1_0 + 384 * c: FM1_0 + 384 * c + 384], 8, 384)
                if S.dry:
                    continue
                pa, par = PD.next()
                pb_, pbr = PD.next()
                rh = [hT[:, k, 0:256] for k in range(8)]
                rhh = [hT[:, k, 256:260] for k in range(8)]
                proj(pa[:, 0:256], par, [w[:, k, 0:128] for k in range(8)], rh, (r_hT, wr))
                proj(pa[:, 256:512], par, [w[:, k, 128:256] for k in range(8)], rh, (r_hT, wr))
                proj(pb_[:, 0:256], pbr, [w[:, k, 256:384] for k in range(8)], rh, (r_hT, wr))
                proj(pb_[:, 256:260], pbr, [w[:, k, 128:256] for k in range(8)], rhh, (r_hT, wr))
                proj(pb_[:, 260:264], pbr, [w[:, k, 256:384] for k in range(8)], rhh, (r_hT, wr))
                P.cp("act", CG[:, 0:256], pa[:, 256:512], reads=(par,), writes=(r_CG,))
                P.cp("act", CG[:, 256:260], pb_[:, 256:260], reads=(pbr,), writes=(r_CG,))
                P.tt("dve", UP[:, :, 2:130], pb_[:, 0:256].rearrange("p (a t) -> p a t", t=128), CG[:, 0:256].rearrange("p (a t) -> p a t", t=128),
                     ALU.mult, reads=(pbr, r_CG), writes=(r_UP,))
                P.tt("dve", UP[:, :, 0:2], pb_[:, 260:264].rearrange("p (a t) -> p a t", t=2), CG[:, 256:260].rearrange("p (a t) -> p a t", t=2),
                     ALU.mult, reads=(pbr, r_CG), writes=(r_UP,))
                P.tt("dve", UP[:, :, 0:2], UP[:, :, 0:2], HF[:, m, :].rearrange("p (a t) -> p a t", t=2), ALU.mult, reads=(r_UP, r_TF), writes=(r_UP,))
                P.ts("dve", VV[:], UP[:, :, 0:128], WCV[:, c, 0:1], ALU.mult, WCV[:, c, 3:4], ALU.add, reads=(r_UP, r_WCV), writes=(r_VV,))
                P.stt(VV[:], UP[:, :, 1:129], WCV[:, c, 1:2], VV[:], ALU.mult, ALU.add, reads=(r_UP, r_WCV, r_VV), writes=(r_VV,))
                P.stt(VV[:], UP[:, :, 2:130], WCV[:, c, 2:3], VV[:], ALU.mult, ALU.add, reads=(r_UP, r_WCV, r_VV), writes=(r_VV,))
                P.tt("dve", YA[:, c, :].rearrange("p (a t) -> p a t", t=128), VV[:], pa[:, 0:256].rearrange("p (a t) -> p a t", t=128), ALU.mult,
                     reads=(r_VV, par), writes=(r_YA,))
                if m == 7:
                    P.cp("dve", CONVO[:, c, :], UP[:, 1, 128:130], reads=(r_UP,), writes=(r_CONVO,))
            w, wr = P.slab(s_inx[:, G_0:G_0 + 64], 8, 64)
            if not S.dry:
                for a in range(2):
                    pd, pdr = PD.next()
                    proj(pd[:, 0:48], pdr, [hT[:, k, a * 128:(a + 1) * 128] for k in range(8)], [w[:, k, 0:48] for k in range(8)], (r_hT, wr))
                    sigmoid_from(GATES[:, a, :], pd[:, 0:48], None, (pdr,), (r_G,), None)
            for i in range(4):
                w, wr = P.slab(s_inx[:, Q_0 + 256 * i: Q_0 + 256 * i + 256], 8, 256)
                if S.dry:
                    continue
                for j in range(2):
                    mt = 2 * i + j
                    pd, pdr = PD.next()
                    proj(pd[:, 0:256], pdr, [w[:, k, j * 128:(j + 1) * 128] for k in range(8)], [hT[:, k, 0:256] for k in range(8)], (r_hT, wr))
                    P.op("act", lambda e, pd=pd, mt=mt: e.mul(out=QB[0:64, 2 * mt, :], in_=pd[0:64, 0:256], mul=0.125), reads=(pdr,), writes=(r_QB[2 * mt],))
                    P.op("act", lambda e, pd=pd, mt=mt: e.mul(out=QB[0:64, 2 * mt + 1, :], in_=pd[64:128, 0:256], mul=0.125), reads=(pdr,), writes=(r_QB[2 * mt + 1],))
            if not S.dry:
                if cfg["attn"]:
                    attention(m, QB, r_QB, GATES, r_G, OB, r_OB, ph)
                else:
                    P.memset("pool", OB[:], 0.0, writes=(r_OB,))
                for a in range(2):
                    pt, ptr_ = PTR.next()
                    for kc in range(8):
                        P.tr(pt[:, kc * 128:(kc + 1) * 128], OB[:, a, kc * 128:(kc + 1) * 128], ident[:], reads=(r_OB, r_ident), writes=(ptr_,), signal=(kc == 7))
                    ptv = pt.rearrange("p (j t) -> p j t", t=128)
                    P.cp("act" if a == 0 else "dve", oT[:, 0:8, a * 128:(a + 1) * 128], ptv[:, 0:8, :], reads=(ptr_,), writes=(r_oT,))
            for c in range(8):
                w, wr = P.slab(s_d2[:, 512 * c:512 * c + 512], 8, 512)
                if S.dry:
                    continue
                pa, par = PD.next()
                pb_, pbr = PD.next()
                rh = [hT[:, k, 0:256] for k in range(8)]
                proj(pa[:, 0:256], par, [w[:, k, 0:128] for k in range(8)], rh, (r_hT, wr))
                proj(pa[:, 256:512], par, [w[:, k, 128:256] for k in range(8)], rh, (r_hT, wr))
                proj(pb_[:, 0:256], pbr, [w[:, k, 256:384] for k in range(8)], [YA[:, k, :] for k in range(8)], (r_YA, wr))
                proj(pb_[:, 256:512], pbr, [w[:, k, 384:512] for k in range(8)], [oT[:, k, :] for k in range(8)], (r_oT, wr))
                sigmoid_from(SG[:], pa, None, (par,), (r_SG,), None)
                P.tt("dve", SG[:], SG[:], pb_, ALU.mult, reads=(r_SG, pbr), writes=(r_SG,))
                P.tt("pool", zT[:, c, :], SG[:, 0:256], SG[:, 256:512], ALU.add, reads=(r_SG,), writes=(r_zT,))
            for i in range(2):
                w, wr = P.slab(s_out[:, 512 * i:512 * i + 512], 8, 512)
                if S.dry:
                    continue
                for a in range(2):
                    pd, pdr = PD.next()
                    proj(pd, pdr, [zT[:, k, a * 128:(a + 1) * 128] for k in range(8)], [w[:, k, :] for k in range(8)], (r_zT, wr))
                    tm, tmr = TM.next()
                    P.tt("dve", tm[:], pd, msec("G1")[:, 512 * i:512 * i + 512], ALU.mult, reads=(pdr, r_MOD), writes=(tmr,))
                    P.tt("pool", XO[:, a, 512 * i:512 * i + 512], tm[:], XO[:, a, 512 * i:512 * i + 512], ALU.add, reads=(tmr, r_XO[a]), writes=(r_XO[a],))
            if not S.dry:
                for a in range(2):
                    norm_mod(XO[:, a, :], r_XO[a], 128, "M2", "SH2", (hT, r_hT), a * 128)
                S.barrier()
            for fc in range(NFC):
                w, wr = P.slab(s_gu[:, 256 * fc:256 * fc + 256], 8, 256)
                if S.dry:
                    continue
                pd, pdr = PD.next()
                rh = [hT[:, k, 0:256] for k in range(8)]
                proj(pd[:, 0:256], pdr, [w[:, k, 0:128] for k in range(8)], rh, (r_hT, wr))
                proj(pd[:, 256:512], pdr, [w[:, k, 128:256] for k in range(8)], rh, (r_hT, wr))
                tm, tmr = TM.next()
                sigmoid_from(tm[:, 0:256], pd[:, 0:256], None, (pdr,), (tmr,), None)
                P.tt("dve", tm[:, 0:256], tm[:, 0:256], pd[:, 0:256], ALU.mult, reads=(tmr, pdr), writes=(tmr,))
                P.tt("dve", aT[:, fc, :], tm[:, 0:256], pd[:, 256:512], ALU.mult, reads=(tmr, pdr), writes=(r_aT,))
            for i in range(2):
                pds = None
                for s_ in range(3):
                    nk = 8 if s_ < 2 else 6
                    w, wr = P.slab(s_down[1024 * s_:1024 * s_ + 128 * nk, 512 * i:512 * i + 512], nk, 512)
                    if S.dry:
                        continue
                    if pds is None:
                        pds = [PD.next(), PD.next()]
                    for a in range(2):
                        pd, pdr = pds[a]
                        for k in range(nk):
                            fc = 8 * s_ + k
                            P.mm(pd, aT[:, fc, a * 128:(a + 1) * 128], w[:, k, :], start=(fc == 0), stop=(fc == NFC - 1), reads=(r_aT, wr), writes=(pdr,),
                                 signal=(k == nk - 1))
                if S.dry:
                    continue
                for a in range(2):
                    pd, pdr = pds[a]
                    tm, tmr = TM.next()
                    P.tt("dve", tm[:], pd, msec("G2")[:, 512 * i:512 * i + 512], ALU.mult, reads=(pdr, r_MOD), writes=(tmr,))
                    P.tt("pool", XO[:, a, 512 * i:512 * i + 512], tm[:], XO[:, a, 512 * i:512 * i + 512], ALU.add, reads=(tmr, r_XO[a]), writes=(r_XO[a],))
            if not S.dry:
                nf, nfr = TMN
                P.dma("pool", nf[:], norms[2:3, :].to_broadcast((128, D)), writes=(nfr,))
                for a in range(2):
                    rs, rsr = rstd_of(XO[:, a, :], r_XO[a], 128)
                    yt, ytr = xk.next()
                    P.stt(yt[:], XO[:, a, :], rs, nf[:], ALU.mult, ALU.mult, reads=(r_XO[a], rsr, nfr), writes=(ytr,))
                    P.dma("pool", yown[m, a * 128:(a + 1) * 128, :], yt[:], reads=(ytr,))
                S.barrier()

    def sample_phase():
        compute_mod(1)
        with ExitStack() as ph:
            def A(name, shape, dt):
                return P.sb(ph, name, shape, dt)
            TSF = A("TSF", [128, TSF_N], F32)
            TSB = A("TSB", [128, TSB_N], BF16)
            r_TS = Res()
            P.dma("sp", TSF[:], ts_f32, writes=(r_TS,))
            P.dma("sp", TSB[:], ts_bf, writes=(r_TS,))

            def tsf(name, rows=128):
                o, n = TSF_OFF[name]
                return TSF[0:rows, o:o + n]
            SLC = tsf("SLC", 64)
            NQB = tsf("NQB", 64)
            CPS = tsf("CPS", 64)
            FBS = tsf("FBS", 64)
            SELG = tsf("SELG", 64)
            L2 = tsf("L2", 2)
            B2S = tsf("B2S", 2).rearrange("p (c r) -> p c r", r=64)
            B2W = tsf("B2W", 2).rearrange("p (c r) -> p c r", r=64)
            NEWB = tsf("NEWB", 4)
            WM0 = TSB[:, TSB_OFF["WM0"][0]:TSB_OFF["WM0"][0] + 64]
            EFULL = TSB[:, TSB_OFF["EFULL"][0]:TSB_OFF["EFULL"][0] + PAST]
            PTB = A("PTB", [128, 4 * NPG], I32)
            IDX = A("IDX", [128, 4 * NPG], I32)
            r_IDX = Res()
            P.dma("pool", PTB[:], ptab_d.rearrange("b j -> (b j)").rearrange("(o n) -> o n", o=1).to_broadcast((128, 4 * NPG)), writes=(r_IDX,))
            P.ts("dve", IDX[:], PTB[:], 128.0, ALU.mult, tsf("PIO")[:, 0:1], ALU.add, reads=(r_IDX, r_TS), writes=(r_IDX,))
            XS = A("XS", [128, D], F32)
            hT = A("hTs", [128, 8, 16], BF16)
            YA = A("YAs", [128, 8, 16], BF16)
            oT = A("oTs", [128, 8, 16], BF16)
            zT = A("zTs", [128, 8, 16], BF16)
            aT = A("aTs", [128, NFC, 16], BF16)
            QBs = A("QBs", [64, 16, 16], BF16)
            KNS = A("KNS", [64, 4, 16], BF16)
            KNW = A("KNW", [64, 4, 16], BF16)
            VN = A("VN", [4, 4, 2, 4, 66], BF16)
            SCV = A("SCV", [128, 8, 4, 2], F32)
            CONVS = A("CONVS", [128, 8, 4, 2], F32)
            UPs = A("UPs", [128, 4, 6], F32)
            VVs = A("VVs", [128, 4, 4], F32)
            CGs = A("CGs", [128, 16], F32)
            GATS = A("GATS", [4, 4, 48], F32)
            STG = A("STG", [128, 3, 512], F32)
            SG = A("SGs", [128, 64], F32)
            TM = Ring([A("TMs%d" % i, [128, 512], F32) for i in range(2)])
            XG = A("XGs", [128, 512], F32)
            TG = A("TGs", [128, 512], F32)
            GG = A("GGs", [128, 512], BF16)
            R2s = A("R2s", [128, 8, 1042], BF16)
            r_R2s = Res()
            PG = Ring([A("PG%d" % i, [128, 4, 512], BF16) for i in range(2)])
            KT = Ring([A("KT%d" % i, [64, 4, 512], BF16) for i in range(2)])
            VP = Ring([A("VP%d" % i, [128, 4, 4, 66], BF16) for i in range(2)])
            KcTs = A("KcTs", [64, 4, 520], BF16)
            VcTs = A("VcTs", [64, 4, 520], BF16)
            Vcs = A("Vcs", [128, 4, 4, 64], BF16)
            QZ = A("QZ", [64, 4, 64], BF16)
            SCs = A("SCs", [64, 512], F32)
            ECs = A("ECs", [64, 512], F32)
            P32 = A("P32", [64, 512], F32)
            PBs = A("PBs", [64, 512], BF16)
            I64s = A("I64s", [64, 128], F32)
            SCOs = A("SCOs", [64, 128], F32)
            WRKs = A("WRKs", [64, 128], F32)
            M8s = A("M8s", [64, 16], F32)
            SELs = A("SELs", [64, 128], BF16)
            MBT = A("MBT", [128, 64], BF16)
            PcTs = A("PcTs", [128, 4, 64], BF16)
            PTs = Ring([A("PTs%d" % i, [128, 256], BF16) for i in range(2)])
            SN = A("SN", [4, 64], F32)
            PTn = A("PTn", [4, 64], BF16)
            OSUM = A("OSUM", [4, 16, 64], F32)
            OTMP = A("OTMP", [4, 6, 64], F32)
            OBs = A("OBs", [4, D], BF16)
            RSs = Ring([A("RSs%d" % i, [64, 16], F32) for i in range(4)])
            (r_XS, r_hT, r_YA, r_oT, r_zT, r_aT, r_QBs, r_KN, r_VN, r_SCV, r_CONVS, r_UP, r_VV, r_CG, r_GAT, r_STG, r_SG, r_XG, r_TG, r_GG,
             r_KcTs, r_VcTs, r_Vcs, r_QZ, r_SCs, r_ECs, r_P32, r_PBs, r_I64, r_SCO, r_WRK, r_M8, r_SEL, r_MBT, r_PcT, r_SN, r_PTn, r_OSUM,
             r_OTMP, r_OBs) = (Res() for _ in range(40))
            for vtile, rr in zip(VP.aps, VP.res):
                P.memset("pool", vtile[:, :, :, 64:66], 1.0, writes=(rr,))
            P.memset("pool", VN[:, :, :, :, 64:66], 1.0, writes=(r_VN,))
            P.memset("pool", QZ[:], 0.0, writes=(r_QZ,))
            P.memset("pool", KcTs[:], 0.0, writes=(r_KcTs,))
            P.memset("pool", VcTs[:], 0.0, writes=(r_VcTs,))
            P.memset("pool", oT[:], 0.0, writes=(r_oT,))
            P.memset("pool", P32[:], 0.0, writes=(r_P32,))
            P.dma("pool", XS[0:16, :], xs_d, writes=(r_XS,))
            norm_mod(XS[0:16, :], r_XS, 16, "M1", "SH1", (hT, r_hT), 0)
            for c in range(8):
                P.dma("pool", SCV[:, c, :, :], sconv_d[:, :, c * 128:(c + 1) * 128].rearrange("b k p -> p b k"), writes=(r_SCV,), allow_slow_non_contiguous=True)
            rh = [hT[:, k, :] for k in range(8)]
            for c in range(8):
                w, wr = P.slab(s_inx[:, FM1_0 + 384 * c: FM1_0 + 384 * c + 384], 8, 384)
                if S.dry:
                    continue
                pa, par = PD.next()
                proj(pa[:, 0:16], par, [w[:, k, 0:128] for k in range(8)], rh, (r_hT, wr))
                proj(pa[:, 16:32], par, [w[:, k, 128:256] for k in range(8)], rh, (r_hT, wr))
                proj(pa[:, 32:48], par, [w[:, k, 256:384] for k in range(8)], rh, (r_hT, wr))
                P.cp("dve", CGs[:], pa[:, 16:32], reads=(par,), writes=(r_CG,))
                P.tt("dve", UPs[:, :, 2:6], pa[:, 32:48].rearrange("p (b q) -> p b q", q=4), CGs[:].rearrange("p (b q) -> p b q", q=4), ALU.mult,
                     reads=(par, r_CG), writes=(r_UP,))
                P.cp("dve", UPs[:, :, 0:2], SCV[:, c, :, :], reads=(r_SCV,), writes=(r_UP,))
                P.ts("dve", VVs[:], UPs[:, :, 0:4], WCV[:, c, 0:1], ALU.mult, WCV[:, c, 3:4], ALU.add, reads=(r_UP, r_WCV), writes=(r_VV,))
                P.stt(VVs[:], UPs[:, :, 1:5], WCV[:, c, 1:2], VVs[:], ALU.mult, ALU.add, reads=(r_UP, r_WCV, r_VV), writes=(r_VV,))
                P.stt(VVs[:], UPs[:, :, 2:6], WCV[:, c, 2:3], VVs[:], ALU.mult, ALU.add, reads=(r_UP, r_WCV, r_VV), writes=(r_VV,))
                P.tt("dve", YA[:, c, :].rearrange("p (b q) -> p b q", q=4), VVs[:], pa[:, 0:16].rearrange("p (b q) -> p b q", q=4), ALU.mult,
                     reads=(r_VV, par), writes=(r_YA,))
                P.cp("dve", CONVS[:, c, :, :], UPs[:, :, 4:6], reads=(r_UP,), writes=(r_CONVS,))
            if not S.dry:
                for c in range(8):
                    P.dma("pool", oconv_s[:, :, c * 128:(c + 1) * 128].rearrange("b k p -> p b k"), CONVS[:, c, :, :], reads=(r_CONVS,), allow_slow_non_contiguous=True)
            w, wr = P.slab(s_inx[:, G_0:G_0 + 64], 8, 64)
            if not S.dry:
                pd, pdr = PD.next()
                for bi in range(4):
                    proj(pd[0:4, 48 * bi:48 * bi + 48], pdr, [hT[:, k, 4 * bi:4 * bi + 4] for k in range(8)], [w[:, k, 0:48] for k in range(8)], (r_hT, wr))
                sigmoid_from(GATS[:].rearrange("p b j -> p (b j)"), pd[0:4, 0:192], None, (pdr,), (r_GAT,), None)
            for i in range(4):
                w, wr = P.slab(s_inx[:, Q_0 + 256 * i: Q_0 + 256 * i + 256], 8, 256)
                if S.dry:
                    continue
                for j in range(2):
                    mt = 2 * i + j
                    pd, pdr = PD.next()
                    proj(pd[:, 0:16], pdr, [w[:, k, j * 128:(j + 1) * 128] for k in range(8)], rh, (r_hT, wr))
                    P.op("act", lambda e, pd=pd, mt=mt: e.mul(out=QBs[0:64, 2 * mt, :], in_=pd[0:64, 0:16], mul=0.125), reads=(pdr,), writes=(r_QBs,))
                    P.op("act", lambda e, pd=pd, mt=mt: e.mul(out=QBs[0:64, 2 * mt + 1, :], in_=pd[64:128, 0:16], mul=0.125), reads=(pdr,), writes=(r_QBs,))
            for i in range(4):
                w, wr = P.slab(s_inx[:, KD_0 + 256 * i: KD_0 + 256 * i + 256], 8, 256)
                if S.dry:
                    continue
                for j in range(2):
                    ti = 2 * i + j
                    kind, g = ti // 4, ti % 4
                    pd, pdr = PD.next()
                    proj(pd[:, 0:16], pdr, [w[:, k, j * 128:(j + 1) * 128] for k in range(8)], rh, (r_hT, wr))
                    P.cp("act", (KNS if kind == 0 else KNW)[0:64, g, :], pd[0:64, 0:16], reads=(pdr,), writes=(r_KN,))
            for i in range(3):
                w, wr = P.slab(s_inx[:, KV_0 + 512 * i: KV_0 + 512 * i + 512], 8, 512)
                if S.dry:
                    continue
                pd, pdr = PD.next()
                proj(pd[0:16, :], pdr, [hT[:, k, :] for k in range(8)], [w[:, k, :] for k in range(8)], (r_hT, wr))
                P.cp("act", STG[0:16, i, :], pd[0:16, :], reads=(pdr,), writes=(r_STG,))
                if i < 2:
                    P.dma("pool", oskv[i], STG[0:16, i, :], reads=(r_STG,))
                else:
                    for bi in range(4):
                        P.dma("pool", owin_s[bi, 508:512, :], STG[4 * bi:4 * bi + 4, i, :], reads=(r_STG,))
                if i >= 1:
                    for bi in range(4):
                        pv, pvr = PD.next()
                        proj(pv[0:4, 0:256], pvr, [hT[:, k, 4 * bi:4 * bi + 4] for k in range(8)], [w[:, k, 256:512] for k in range(8)], (r_hT, wr))
                        P.cp("dve", VN[0:4, bi, i - 1, :, 0:64], pv[0:4, 0:256].rearrange("p (g d) -> p g d", d=64), reads=(pvr,), writes=(r_VN,))
            if not S.dry:
                for bi in range(4):
                    P.dma("sp", owin_s[bi, 0:508, :], cwin_d[bi, 4:512, :])
            OAB = [(PO.aps[0], PO.res[0]), (PO.aps[1], PO.res[1]), (PD.aps[2], PD.res[2])]
            PDs = Ring([PD.aps[0], PD.aps[1]])
            PDs.res = [PD.res[0], PD.res[1]]
            POt_full = [POt[0][:], POt[1][:], PD.aps[2]]

            def oa(h):
                b = h // 6
                sl_ = h - 6 * b
                return POt_full[b][0:4, sl_ * 65:sl_ * 65 + 65], OAB[b][1], b

            regstate = {}

            def gather4(cache, bi, pg, dst, dres):
                for s_ in range(4):
                    j = 4 * pg + s_

                    def fn(e, j=j, s_=s_, dst=dst, cache=cache, bi=bi):
                        return e.indirect_dma_start(out=dst[:, s_, :], out_offset=None, in_=cache,
                                                    in_offset=bass.IndirectOffsetOnAxis(ap=IDX[:, bi * NPG + j:bi * NPG + j + 1], axis=0))
                    P.op("pool", fn, (r_IDX,), (dres,), dma=True)

            def trans_block(src, sres, col0, eng):
                pt, ptr_ = PTR.next()
                for s_ in range(4):
                    P.tr(pt[:, s_ * 128:(s_ + 1) * 128], src[:, s_, col0:col0 + 128], ident[:], reads=(sres, r_ident), writes=(ptr_,), signal=(s_ == 3))
                return pt[:, 0:512], ptr_

            def attn_chunks(bi, ktile, kres, vtile, vres, nchunks, mask_fn, bias_fn, first_flags):
                ps, psr = PS.next()
                for cs in range(nchunks):
                    for g in range(4):
                        P.mm(ps[:, cs * 64 + 16 * g:cs * 64 + 16 * g + 16], ktile[0:64, g, cs * 128:(cs + 1) * 128], QBs[0:64, 4 * g:4 * g + 4, 4 * bi:4 * bi + 4],
                             start=(cs == 0 and g == 0), stop=False, reads=(kres, r_QBs), writes=(psr,), signal=False, sgc=True)
                    mk = mask_fn(cs)
                    if mk is not None:
                        P.mm(ps[:, cs * 64:cs * 64 + 64], mk[0], mk[1], start=False, stop=False, reads=(mk[2],), writes=(psr,), signal=False, sgc=True)
                    P.mm(ps[:, cs * 64:cs * 64 + 64], L2, bias_fn(cs), start=False, stop=True, reads=(r_TS,), writes=(psr,), signal=True, sgc=True)
                pt_, ptr2 = PTs.next()
                P.act(pt_[:, 0:64 * nchunks], ps[:, 0:64 * nchunks], AF.Exp, reads=(psr,), writes=(ptr2,))
                for cs in range(nchunks):
                    for h in range(16):
                        o_, ores, b = oa(h)
                        P.mm(o_, pt_[:, cs * 64 + 4 * h:cs * 64 + 4 * h + 4], vtile[:, cs, h // 4, 0:65], start=first_flags[b], stop=False,
                             reads=(ptr2, vres), writes=(ores,), signal=(h % 6 == 5 or h == 15), sgc=True)
                        first_flags[b] = False

            def new_chunk(bi, KN, vsel, first_flags):
                pn, pnr = PS.next()
                for g in range(4):
                    P.mm(pn[0:4, 16 * g:16 * g + 16], KN[0:64, g, 4 * bi:4 * bi + 4], QBs[0:64, 4 * g:4 * g + 4, 4 * bi:4 * bi + 4],
                         start=(g == 0), stop=(g == 3), reads=(r_KN, r_QBs), writes=(pnr,), signal=(g == 3), sgc=True)
                P.tt("dve", SN[:], pn[0:4, 0:64], NEWB, ALU.add, reads=(pnr, r_TS), writes=(r_SN,))
                P.act(PTn[:], SN[:], AF.Exp, reads=(r_SN,), writes=(r_PTn,))
                for h in range(16):
                    o_, ores, b = oa(h)
                    P.mm(o_, PTn[0:4, 4 * h:4 * h + 4], VN[0:4, bi, vsel, h // 4, 0:65], start=first_flags[b], stop=True,
                         reads=(r_PTn, r_VN), writes=(ores,), signal=(h % 6 == 5 or h == 15), sgc=True)
                    first_flags[b] = False

            def combine(bi, br, normalized, first):
                for b in range(3):
                    h0 = 6 * b
                    nh = min(6, 16 - h0)
                    ov = POt_full[b][0:4, 0:nh * 65].rearrange("p (h d) -> p h d", d=65)
                    ores = OAB[b][1]
                    rs, rsr = RSs.next()
                    gsl = GATS[0:4, bi, 3 * h0 + br:3 * (h0 + nh - 1) + br + 1:3]
                    if normalized:
                        P.cp("dve", rs[0:4, 8:8 + nh], gsl, reads=(r_GAT,), writes=(rsr,))
                    else:
                        P.op("dve", lambda e, rs=rs, ov=ov, nh=nh: e.reciprocal(out=rs[0:4, 0:nh], in_=ov[:, :, 64]), reads=(ores,), writes=(rsr,))
                        P.tt("dve", rs[0:4, 8:8 + nh], rs[0:4, 0:nh], gsl, ALU.mult, reads=(rsr, r_GAT), writes=(rsr,))
                    wv = rs[0:4, 8:8 + nh]
                    wb = bass.AP(wv.tensor, wv.offset, [list(x) for x in wv.ap] + [[0, 64]])
                    if first:
                        P.tt("dve", OSUM[0:4, h0:h0 + nh, :], ov[:, :, 0:64], wb, ALU.mult, reads=(ores, rsr), writes=(r_OSUM,))
                    else:
                        P.tt("dve", OTMP[0:4, 0:nh, :], ov[:, :, 0:64], wb, ALU.mult, reads=(ores, rsr), writes=(r_OTMP,))
                        P.tt("dve", OSUM[0:4, h0:h0 + nh, :], OSUM[0:4, h0:h0 + nh, :], OTMP[0:4, 0:nh, :], ALU.add, reads=(r_OTMP, r_OSUM), writes=(r_OSUM,))

            for bi in range(cfg.get("sbatches", 4)):
                P.memset("pool", R2s[:], 0.0, writes=(r_R2s,))
                for G8 in range(8):
                    for hf in range(2):
                        pgt, pgr = PG.next()
                        gather4(ccmp_d, bi, 2 * G8 + hf, pgt, pgr)
                        c0 = 16 + 512 * hf
                        for blk in range(4):
                            pt, ptr_ = trans_block(pgt, pgr, 128 * blk, None)
                            gA = (blk // 2) * 4 + 2 * (blk % 2)
                            eng = "act" if blk % 2 == 0 else "dve"
                            P.cp(eng, R2s[0:64, gA, c0:c0 + 512], pt[0:64, :], reads=(ptr_,), writes=(r_R2s,))
                            P.cp(eng, R2s[64:128, gA, c0 - 1:c0 + 511], pt[0:64, :], reads=(ptr_,), writes=(r_R2s,))
                            P.cp(eng, R2s[64:128, gA + 1, c0 - 1:c0 + 511], pt[64:128, :], reads=(ptr_,), writes=(r_R2s,))
                            P.cp(eng, R2s[0:64, gA + 1, c0:c0 + 512], pt[64:128, :], reads=(ptr_,), writes=(r_R2s,))
                    compress_block(G8, KcTs, r_KcTs, VcTs, r_VcTs, XG, TG, GG, r_XG, r_TG, r_GG, R2x=R2s, rR2=r_R2s, nb=64, ntok=1024)
                for cc in range(4):
                    for g in range(4):
                        pt, ptr_ = PTR.next()
                        P.tr(pt[:, 0:64], VcTs[0:64, g, 128 * cc + 2:128 * cc + 130], ident[0:64, 0:64], reads=(r_VcTs, r_ident), writes=(ptr_,))
                        P.cp("dve", Vcs[:, cc, g, :], pt[:, 0:64], reads=(ptr_,), writes=(r_Vcs,))
                for g in range(4):
                    P.cp("dve", QZ[0:64, g, 16 * g:16 * g + 16].rearrange("p (r q) -> p r q", q=4), QBs[0:64, 4 * g:4 * g + 4, 4 * bi:4 * bi + 4],
                         reads=(r_QBs,), writes=(r_QZ,))
                pc, pcr = PDs.next()
                for g in range(4):
                    P.mm(pc[0:64, 0:512], QZ[0:64, g, :], KcTs[0:64, g, 2:514], start=(g == 0), stop=(g == 3), reads=(r_QZ, r_KcTs), writes=(pcr,))
                P.stt(SCs[:, 0:511], CPS[:, 0:511], SLC[:, 0:1], pc[0:64, 0:511], ALU.mult, ALU.add, reads=(pcr, r_TS), writes=(r_SCs,))
                rs, rsr = RSs.next()
                P.act(ECs[:, 0:511], SCs[:, 0:511], AF.Exp, reads=(r_SCs, r_TS), writes=(r_ECs, rsr), bias=NQB[:, 0:1], accum_out=rs[:, 0:1])
                P.ts("dve", rs[:, 1:2], rs[:, 0:1], 1e-30, ALU.add, reads=(rsr,), writes=(rsr,))
                P.op("dve", lambda e, rs=rs: e.reciprocal(out=rs[:, 2:3], in_=rs[:, 1:2]), reads=(rsr,), writes=(rsr,))
                P.ts("dve", P32[:, 0:511], ECs[:, 0:511], rs[:, 2:3], ALU.mult, reads=(r_ECs, rsr), writes=(r_P32,))
                P.cp("dve", PBs[:], P32[:], reads=(r_P32,), writes=(r_PBs,))
                pi, pir = PDs.next()
                P.mm(pi[0:64, 0:512], SELG, P32[:], start=True, stop=True, reads=(r_TS, r_P32), writes=(pir,))
                P.op("dve", lambda e, pi=pi: e.tensor_reduce(out=I64s[:], in_=pi[0:64, 0:512].rearrange("p (j k) -> p j k", k=4), axis=AX.X, op=ALU.add),
                     reads=(pir,), writes=(r_I64,))
                P.tt("dve", I64s[:, 1:128], I64s[:, 1:128], pi[0:64, 3:508:4], ALU.add, reads=(pir, r_I64), writes=(r_I64,))
                P.tt("dve", SCOs[:], I64s[:], FBS, ALU.add, reads=(r_I64, r_TS), writes=(r_SCO,))
                P.op("dve", lambda e: e.max(out=M8s[:, 0:8], in_=SCOs[:]), reads=(r_SCO,), writes=(r_M8,))
                P.op("dve", lambda e: e.match_replace(out=WRKs[:], in_to_replace=M8s[:, 0:8], in_values=SCOs[:], imm_value=-3.0e38),
                     reads=(r_SCO, r_M8), writes=(r_WRK,))
                P.op("dve", lambda e: e.max(out=M8s[:, 8:16], in_=WRKs[:]), reads=(r_WRK,), writes=(r_M8,))
                P.ts("dve", SELs[:], SCOs[:], M8s[:, 14:15], ALU.is_lt, -BIG, ALU.mult, reads=(r_SCO, r_M8), writes=(r_SEL,))
                pt, ptr_ = PTR.next()
                P.tr(pt[:, 0:64], SELs[:], ident[0:64, 0:64], reads=(r_SEL, r_ident), writes=(ptr_,))
                P.cp("act", MBT[:], pt[:, 0:64], reads=(ptr_,), writes=(r_MBT,))
                pt, ptr_ = PTR.next()
                for cc in range(4):
                    P.tr(pt[:, cc * 64:(cc + 1) * 64], PBs[:, 128 * cc:128 * cc + 128], ident[0:64, 0:64], reads=(r_PBs, r_ident), writes=(ptr_,), signal=(cc == 3))
                P.cp("act", PcTs[:], pt[:, 0:256].rearrange("p (c r) -> p c r", r=64), reads=(ptr_,), writes=(r_PcT,))
                ff = [True, True, True]
                for cc in range(4):
                    for h in range(16):
                        o_, ores, b = oa(h)
                        P.mm(o_[:, 0:64], PcTs[:, cc, 4 * h:4 * h + 4], Vcs[:, cc, h // 4, :], start=ff[b], stop=(cc == 3), reads=(r_PcT, r_Vcs), writes=(ores,),
                             signal=(h % 6 == 5 or h == 15), sgc=True)
                        ff[b] = False
                combine(bi, 0, True, True)
                if cfg.get("dbg2") and bi == 0:
                    DV = A("DV", [128, 1024], F32)
                    DPT = A("DPT", [128, 256], F32)
                    r_DV = Res()
                    P.dma("pool", dbg_p, P32[:], reads=(r_P32,))
                    P.cp("dve", DV[:], Vcs[:].rearrange("p a b d -> p (a b d)"), reads=(r_Vcs,), writes=(r_DV,))
                    P.dma("pool", dbg_v, DV[:], reads=(r_DV,))
                    P.cp("dve", DPT[:], PcTs[:].rearrange("p a r -> p (a r)"), reads=(r_PcT,), writes=(r_DV,))
                    P.dma("pool", dbg_pt, DPT[:], reads=(r_DV,))
                if cfg.get("dbg"):
                    P.dma("pool", dbg_d[bi, 0].rearrange("q (h d) -> q h d", d=64), OSUM[:], reads=(r_OSUM,))
                ff = [True, True, True]
                for pg in range(16):
                    pgt, pgr = PG.next()
                    gather4(csel_d, bi, pg, pgt, pgr)
                    vp, vpr = VP.next()
                    P.cp("pool", vp[:, :, :, 0:64], pgt[:, :, 256:512].rearrange("p s (g d) -> p s g d", d=64), reads=(pgr,), writes=(vpr,))
                    kt, ktr = KT.next()
                    for blk in range(2):
                        pt, ptr_ = trans_block(pgt, pgr, 128 * blk, None)
                        eng = "act" if blk == 0 else "dve"
                        P.cp(eng, kt[0:64, 2 * blk, :], pt[0:64, :], reads=(ptr_,), writes=(ktr,))
                        P.cp(eng, kt[0:64, 2 * blk + 1, :], pt[64:128, :], reads=(ptr_,), writes=(ktr,))
                    attn_chunks(bi, kt, ktr, vp, vpr, 4,
                                lambda cs, pg=pg: (EFULL[:, (4 * pg + cs) * 128:(4 * pg + cs + 1) * 128], MBT[:], r_MBT),
                                lambda cs, pg=pg: B2S[:, 4 * pg + cs, :], ff)
                new_chunk(bi, KNS, 0, ff)
                combine(bi, 1, False, False)
                if cfg.get("dbg"):
                    P.dma("pool", dbg_d[bi, 1].rearrange("q (h d) -> q h d", d=64), OSUM[:], reads=(r_OSUM,))
                ff = [True, True, True]
                pgt, pgr = PG.next()
                for s_ in range(4):
                    P.dma("pool", pgt[:, s_, :], cwin_d[bi, 128 * s_:128 * s_ + 128, :], writes=(pgr,))
                vp, vpr = VP.next()
                P.cp("pool", vp[:, :, :, 0:64], pgt[:, :, 256:512].rearrange("p s (g d) -> p s g d", d=64), reads=(pgr,), writes=(vpr,))
                kt, ktr = KT.next()
                for blk in range(2):
                    pt, ptr_ = trans_block(pgt, pgr, 128 * blk, None)
                    eng = "act" if blk == 0 else "dve"
                    P.cp(eng, kt[0:64, 2 * blk, :], pt[0:64, :], reads=(ptr_,), writes=(ktr,))
                    P.cp(eng, kt[0:64, 2 * blk + 1, :], pt[64:128, :], reads=(ptr_,), writes=(ktr,))
                attn_chunks(bi, kt, ktr, vp, vpr, 4,
                            lambda cs: (ident[:], WM0, r_TS) if cs == 0 else None,
                            lambda cs: B2W[:, cs, :], ff)
                new_chunk(bi, KNW, 1, ff)
                combine(bi, 2, False, False)
                if cfg.get("dbg"):
                    P.dma("pool", dbg_d[bi, 2].rearrange("q (h d) -> q h d", d=64), OSUM[:], reads=(r_OSUM,))
                P.cp("dve", OBs[:], OSUM[:].rearrange("p h d -> p (h d)"), reads=(r_OSUM,), writes=(r_OBs,))
                pt, ptr_ = PTR.next()
                for kc in range(8):
                    P.tr(pt[:, kc * 4:kc * 4 + 4], OBs[0:4, kc * 128:(kc + 1) * 128], ident[0:4, 0:4], reads=(r_OBs, r_ident), writes=(ptr_,), signal=(kc == 7))
                P.cp("act", oT[:, :, 4 * bi:4 * bi + 4], pt[:, 0:32].rearrange("p (k t) -> p k t", t=4), reads=(ptr_,), writes=(r_oT,))
            for c in range(8):
                w, wr = P.slab(s_d2[:, 512 * c:512 * c + 512], 8, 512)
                if S.dry:
                    continue
                pa, par = PD.next()
                proj(pa[:, 0:16], par, [w[:, k, 0:128] for k in range(8)], rh, (r_hT, wr))
                proj(pa[:, 16:32], par, [w[:, k, 128:256] for k in range(8)], rh, (r_hT, wr))
                proj(pa[:, 32:48], par, [w[:, k, 256:384] for k in range(8)], [YA[:, k, :] for k in range(8)], (r_YA, wr))
                proj(pa[:, 48:64], par, [w[:, k, 384:512] for k in range(8)], [oT[:, k, :] for k in range(8)], (r_oT, wr))
                sigmoid_from(SG[:, 0:32], pa[:, 0:32], None, (par,), (r_SG,), None)
                P.tt("dve", SG[:, 0:32], SG[:, 0:32], pa[:, 32:64], ALU.mult, reads=(r_SG, par), writes=(r_SG,))
                P.tt("pool", zT[:, c, :], SG[:, 0:16], SG[:, 16:32], ALU.add, reads=(r_SG,), writes=(r_zT,))
            for i in range(2):
                w, wr = P.slab(s_out[:, 512 * i:512 * i + 512], 8, 512)
                if S.dry:
                    continue
                pd, pdr = PD.next()
                proj(pd[0:16, :], pdr, [zT[:, k, :] for k in range(8)], [w[:, k, :] for k in range(8)], (r_zT, wr))
                tm, tmr = TM.next()
                P.tt("dve", tm[0:16, :], pd[0:16, :], msec("G1")[0:16, 512 * i:512 * i + 512], ALU.mult, reads=(pdr, r_MOD), writes=(tmr,))
                P.tt("pool", XS[0:16, 512 * i:512 * i + 512], tm[0:16, :], XS[0:16, 512 * i:512 * i + 512], ALU.add, reads=(tmr, r_XS), writes=(r_XS,))
            if not S.dry:
                norm_mod(XS[0:16, :], r_XS, 16, "M2", "SH2", (hT, r_hT), 0)
            for fc in range(NFC):
                w, wr = P.slab(s_gu[:, 256 * fc:256 * fc + 256], 8, 256)
                if S.dry:
                    continue
                pd, pdr = PD.next()
                proj(pd[:, 0:16], pdr, [w[:, k, 0:128] for k in range(8)], rh, (r_hT, wr))
                proj(pd[:, 16:32], pdr, [w[:, k, 128:256] for k in range(8)], rh, (r_hT, wr))
                tm, tmr = TM.next()
                sigmoid_from(tm[:, 0:16], pd[:, 0:16], None, (pdr,), (tmr,), None)
                P.tt("dve", tm[:, 0:16], tm[:, 0:16], pd[:, 0:16], ALU.mult, reads=(tmr, pdr), writes=(tmr,))
                P.tt("dve", aT[:, fc, :], tm[:, 0:16], pd[:, 16:32], ALU.mult, reads=(tmr, pdr), writes=(r_aT,))
            for i in range(2):
                pd, pdr = None, None
                for s_ in range(3):
                    nk = 8 if s_ < 2 else 6
                    w, wr = P.slab(s_down[1024 * s_:1024 * s_ + 128 * nk, 512 * i:512 * i + 512], nk, 512)
                    if S.dry:
                        continue
                    if pd is None:
                        pd, pdr = PD.next()
                    for k in range(nk):
                        fc = 8 * s_ + k
                        P.mm(pd[0:16, :], aT[:, fc, :], w[:, k, :], start=(fc == 0), stop=(fc == NFC - 1), reads=(r_aT, wr), writes=(pdr,), signal=(k == nk - 1))
                if S.dry:
                    continue
                tm, tmr = TM.next()
                P.tt("dve", tm[0:16, :], pd[0:16, :], msec("G2")[0:16, 512 * i:512 * i + 512], ALU.mult, reads=(pdr, r_MOD), writes=(tmr,))
                P.tt("pool", XS[0:16, 512 * i:512 * i + 512], tm[0:16, :], XS[0:16, 512 * i:512 * i + 512], ALU.add, reads=(tmr, r_XS), writes=(r_XS,))
            if not S.dry:
                nf, nfr = xk.next()
                P.dma("pool", nf[:], norms[2:3, :].to_broadcast((128, D)), writes=(nfr,))
                rs, rsr = rstd_of(XS[0:16, :], r_XS, 16)
                yt, ytr = xk.next()
                P.stt(yt[0:16, :], XS[0:16, :], rs, nf[0:16, :], ALU.mult, ALU.mult, reads=(r_XS, rsr, nfr), writes=(ytr,))
                P.dma("pool", ys_d, yt[0:16, :], reads=(ytr,))
                S.barrier()

    def body():
        nonlocal KE, VS, KW, VW, KcT, VcT, Vc, CONVO
        stop = cfg.get("stop", 9)
        pes = ExitStack()
        KE = P.sb(pes, "KE", [128, 4, T], BF16)
        VS = P.sb(pes, "VS", [128, 32, 4, 66], BF16)
        KW = P.sb(pes, "KW", [128, 4, 1024], BF16)
        VW = P.sb(pes, "VW", [128, 8, 4, 66], BF16)
        KcT = P.sb(pes, "KcT", [64, 4, 260], BF16)
        VcT = P.sb(pes, "VcT", [64, 4, 260], BF16)
        Vc = P.sb(pes, "Vc", [128, 2, 4, 64], BF16)
        CONVO = P.sb(pes, "CONVO", [128, 8, 2], F32)
        for g in range(4):
            P.dma("sp", KE[64:128, g, :], t_E, writes=(r_KE[g],))
        P.memset("pool", VS[:, :, :, 64:66], 1.0, writes=(r_VS,))
        P.memset("pool", VW[:, :, :, 64:66], 1.0, writes=(r_VW,))
        P.memset("pool", KW[64:128, :, :], 0.0, writes=(r_KW,))
        P.memset("pool", KW[64:65, :, :], 1.0, writes=(r_KW,))
        P.memset("pool", KcT[:], 0.0, writes=(r_Kc,))
        P.memset("pool", VcT[:], 0.0, writes=(r_VcT,))
        P.memset("pool", Vc[:], 0.0, writes=(r_Vc,))
        P.memset("pool", CONVO[:], 0.0, writes=(r_CONVO,))
        if stop >= 2:
            compute_mod(0)
        for m in range(NU):
            if stop >= 3:
                kv_pass(m)
            if stop >= 4:
                own_unit(m)
        if not S.dry:
            with nc.allow_non_contiguous_dma(reason="tiny conv state"):
                for c in range(8):
                    P.dma("pool", oconv[:, c * 128:(c + 1) * 128].rearrange("k p -> p k"), CONVO[:, c, :], reads=(r_CONVO,), allow_slow_non_contiguous=True)
        S.barrier()
        pes.close()
        if cfg.get("sample", True):
            sample_phase()

    return P, body


def _lay(items):
    d, off = {}, 0
    for n, sz in items:
        d[n] = (off, sz)
        off += sz
    return d, off


TF_OFF, TF_N = _lay([("QPOS", 16), ("CI31", 256), ("CPOS", 256), ("QSB", 256), ("POSQ", 256), ("ALB", 576), ("HF", 32), ("EPS", 8)])
TB_OFF, TB_N = _lay([("DT", 512), ("WT", 1536), ("FBT", 1024)])


TSF_OFF, TSF_N = _lay([("SLC", 8), ("NQB", 8), ("CPS", 512), ("FBS", 128), ("SELG", 64), ("L2", 128), ("B2S", 4096), ("B2W", 256), ("NEWB", 64), ("PIO", 8)])
TSB_OFF, TSB_N = _lay([("WM0", 64), ("EFULL", PAST)])


def make_sample_tables():
    tf_ = np.zeros((128, TSF_N), np.float64)
    tb_ = np.zeros((128, TSB_N), np.float64)
    rho = np.arange(64)
    hh, qq = rho // 4, rho % 4
    sl = np.array(SLOPES)[hh]

    def put(tab, off, name, arr, rows):
        o, n = off[name]
        tab[0:rows, o:o + n] = np.asarray(arr).reshape(rows, n)
    put(tf_, TSF_OFF, "SLC", np.repeat(sl[:, None], 8, 1), 64)
    put(tf_, TSF_OFF, "NQB", np.repeat((-sl * qq)[:, None], 8, 1), 64)
    c = np.arange(512)
    put(tf_, TSF_OFF, "CPS", np.broadcast_to(16.0 * c + 15.5 - PAST, (64, 512)), 64)
    fbs = np.zeros((64, 128))
    fbs[:, 0] = 1000.0
    fbs[:, 127] = 1000.0
    put(tf_, TSF_OFF, "FBS", fbs, 64)
    selg = ((hh[:, None] // 4 == hh[None, :] // 4) & (qq[:, None] == qq[None, :])).astype(np.float64)
    put(tf_, TSF_OFF, "SELG", selg, 64)
    put(tf_, TSF_OFF, "L2", np.stack([np.arange(128.0), np.ones(128)]), 2)
    ck = np.arange(64)
    b2s = np.stack([np.broadcast_to(sl[None, :], (64, 64)), sl[None, :] * (128.0 * ck[:, None] - PAST - qq[None, :])])
    put(tf_, TSF_OFF, "B2S", b2s, 2)
    cw = np.arange(4)
    b2w = np.stack([np.broadcast_to(sl[None, :], (4, 64)), sl[None, :] * (PAST - 512 + 128.0 * cw[:, None] - PAST - qq[None, :])])
    put(tf_, TSF_OFF, "B2W", b2w, 2)
    j = np.arange(4)[:, None]
    put(tf_, TSF_OFF, "NEWB", np.where(j <= qq[None, :], sl[None, :] * (j - qq[None, :]), -BIG), 4)
    put(tf_, TSF_OFF, "PIO", np.repeat(np.arange(128.0)[:, None], 8, 1), 128)
    pk = np.arange(128)[:, None]
    put(tb_, TSB_OFF, "WM0", np.where(pk <= qq[None, :], -BIG, 0.0), 128)
    put(tb_, TSB_OFF, "EFULL", (np.arange(PAST)[None, :] // 64 == np.arange(128)[:, None]).astype(np.float64), 128)
    return tf_.astype(np.float32), tb_.astype(NPBF)


def make_tables(p):
    Sp = SUBS[p]
    f = np.arange(128)
    tfv = np.zeros((128, TF_N), np.float32)
    tbv = np.zeros((128, TB_N), np.float32)

    def put(tab, off, name, arr):
        o, n = off[name]
        tab[:, o:o + n] = arr.reshape(128, n)
    qpos = np.zeros((128, 8, 2), np.float64)
    for m in range(8):
        for a in range(2):
            qpos[:, m, a] = 512 * m + 128 * Sp[a] + f
    put(tfv, TF_OFF, "QPOS", qpos)
    ci = np.arange(256)
    put(tfv, TF_OFF, "CI31", np.broadcast_to(16.0 * ci + 31, (128, 256)))
    put(tfv, TF_OFF, "CPOS", np.broadcast_to(16.0 * ci + 15.5, (128, 256)))
    sl = np.array(SLOPES, np.float64)
    put(tfv, TF_OFF, "QSB", -qpos[:, :, :, None] * sl[None, None, None, :])
    posq = np.concatenate([128 * Sp[a] + f for a in range(2)]).astype(np.float64)
    put(tfv, TF_OFF, "POSQ", np.broadcast_to(posq, (128, 256)))
    rel = np.arange(36) - 32
    put(tfv, TF_OFF, "ALB", sl[None, :, None] * (128.0 * rel[None, None, :] + f[:, None, None]))
    hf = np.ones((128, 8, 4))
    for a in range(2):
        if Sp[a] == 0:
            hf[:, 0, 2 * a:2 * a + 2] = 0.0
    put(tfv, TF_OFF, "HF", hf)
    put(tfv, TF_OFF, "EPS", np.full((128, 8), EPS))
    pk = f[:, None]
    qf = f[None, :]
    tri_l = np.where(pk <= qf, 0.0, -BIG)
    tri_u = np.where(pk > qf, 0.0, -BIG)
    neg = np.full((128, 128), -BIG)
    zero = np.zeros((128, 128))
    dt = np.zeros((128, 4, 128))
    for i, (a, r) in enumerate(((0, 0), (0, 1), (1, 2), (1, 3))):
        dt[:, i, :] = tri_l if r == Sp[a] else zero
    put(tbv, TB_OFF, "DT", dt)
    wt = np.zeros((128, 12, 128))
    for i in range(12):
        a, rp = (0, i) if i < 6 else (1, i - 6 + 2)
        d = Sp[a] - (rp - 4)
        wt[:, i, :] = neg if (d < 0 or d > 4) else (tri_l if d == 0 else (tri_u if d == 4 else zero))
    put(tbv, TB_OFF, "WT", wt)
    fbt = np.zeros((128, 16, 64))
    j = np.arange(64)[None, :]
    for m in range(8):
        for a in range(2):
            cur = (qpos[:, m, a] // 64)[:, None]
            v = np.where(j > cur, -1e30, np.where((j == 0) | (j == cur) | (j == cur - 1), 1000.0, 0.0))
            fbt[:, 2 * m + a, :] = v
    put(tbv, TB_OFF, "FBT", fbt)
    return tfv, tbv.astype(NPBF)


_CACHE = {}


def get_program():
    key = tuple(sorted(CFG.items()))
    if key not in _CACHE:
        P, body = build_program(CFG)
        P.S.dry = True
        body()
        P.S.dry = False
        body()
        P.S.finish()
        _CACHE[key] = P
    return _CACHE[key]


def kernel(x_prompt, x_sample, c_prompt, c_sample, cache_cmp, cache_sel, cache_win, state_conv, page_table,
           w_ada, b_ada, norm1, w_in, w_conv, b_conv, w_out_conv, pe_cmp, w_phi1, w_phi2, w_o_nsa, w_out,
           norm2, w_gate, w_up, w_down, norm_f):
    A = lambda v: np.ascontiguousarray(np.asarray(v))
    x_prompt, x_sample, c_prompt, c_sample = A(x_prompt), A(x_sample), A(c_prompt), A(c_sample)
    win = A(w_in)[0]
    cols = lambda o, n: win[:, o:o + n]
    BG, CGo, XI, Qo, KC, VCo, KSo, VSo, KWo, VWo, NG, MG = 0, 1024, 2048, 3072, 4096, 4352, 4608, 4864, 5120, 5376, 5632, 5680
    parts = []
    for c in range(8):
        parts += [cols(BG + 128 * c, 128), cols(CGo + 128 * c, 128), cols(XI + 128 * c, 128)]
    parts.append(cols(Qo, 1024))
    for base in (KSo, KWo, KC, VCo):
        for g in range(4):
            parts += [cols(base + 64 * g, 64), cols(base + 64 * g, 64)]
    parts.append(cols(KC, 1536))
    parts += [cols(NG, 48), np.zeros((D, 16), np.float32)]
    w_inx = np.ascontiguousarray(np.concatenate(parts, axis=1))
    assert w_inx.shape == (D, NCX)
    woc, won = A(w_out_conv)[0], A(w_o_nsa)[0]
    parts = []
    for c in range(8):
        parts += [cols(MG + 128 * c, 128), cols(MG + 1024 + 128 * c, 128), woc[:, 128 * c:128 * c + 128], won[:, 128 * c:128 * c + 128]]
    w_d2 = np.ascontiguousarray(np.concatenate(parts, axis=1))
    wg, wu = A(w_gate)[0], A(w_up)[0]
    parts = []
    for fc in range(NFC):
        parts += [wg[:, 128 * fc:128 * fc + 128], wu[:, 128 * fc:128 * fc + 128]]
    w_gu = np.ascontiguousarray(np.concatenate(parts, axis=1))
    wc4 = np.concatenate([A(w_conv)[0], A(b_conv)], axis=0)
    wcv = np.ascontiguousarray(wc4.reshape(4, 8, 128).transpose(2, 1, 0))
    norms = np.ascontiguousarray(np.stack([A(norm1)[0], A(norm2)[0], A(norm_f)], axis=0))
    common = dict(w_inx=w_inx, w_d2=w_d2, w_ada=A(w_ada)[0], b_ada=A(b_ada), norms=norms, w_out=A(w_out)[0], w_gu=w_gu,
                  w_down=A(w_down)[0], w_phi1=A(w_phi1)[0].reshape(4096, 128), w_phi2=A(w_phi2)[0], pe_cmp=A(pe_cmp)[0], wcv=wcv,
                  t_ident=np.eye(128, dtype=np.float32))
    tE = (np.arange(T)[None, :] // 64 == np.arange(64)[:, None]).astype(np.float32).astype(NPBF)
    tabs = [make_tables(p) for p in range(2)]
    if CFG.get("sample", True):
        stab = make_sample_tables()
        ccmp_full = A(cache_cmp)[0].reshape(NPHYS * 128, 512)
        csel_full = A(cache_sel)[0].reshape(NPHYS * 128, 512)
    in_maps = []
    for c in range(8):
        b, p = c // 2, c % 2
        Sp = SUBS[p]
        xo = np.zeros((8, 260, D), np.float32)
        for m in range(8):
            for a in range(2):
                st = 512 * m + 128 * Sp[a]
                xo[m, a * 128:(a + 1) * 128] = x_prompt[b, st:st + 128]
                if st > 0:
                    xo[m, 256 + 2 * a:258 + 2 * a] = x_prompt[b, st - 2:st]
        ct = np.zeros((2, 128, D), np.float32)
        ct[0] = c_prompt[b][None, :]
        for t_ in range(16):
            ct[1, t_] = c_sample[4 * c + t_ // 4]
        mp = dict(common)
        mp.update(xall=x_prompt[b], xown=xo, ctok=ct, t_E=tE, t_f32=tabs[p][0], t_bf=tabs[p][1])
        if CFG.get("sample", True):
            mp.update(xs=np.ascontiguousarray(x_sample[4 * c:4 * c + 4].reshape(16, D)),
                      sconv=np.ascontiguousarray(A(state_conv)[0, 4 * c:4 * c + 4]),
                      cwin=np.ascontiguousarray(A(cache_win)[0, 4 * c:4 * c + 4].reshape(4, 512, 512)),
                      ptab=np.ascontiguousarray(A(page_table)[4 * c:4 * c + 4].astype(np.int32)),
                      ccmp=ccmp_full, csel=csel_full, ts_f32=stab[0], ts_bf=stab[1])
        in_maps.append(mp)
    P = get_program()
    ncores = CFG.get("ncores", 8)
    res = run_bass_kernel_spmd(P.nc, in_maps[:ncores], core_ids=list(range(ncores)))
    R = list(res.results)
    while len(R) < 8:
        R.append(R[0])
    y_prompt = np.zeros((4, T, D), np.float32)
    for c in range(8):
        b, p = c // 2, c % 2
        Sp = SUBS[p]
        for m in range(8):
            for a in range(2):
                st = 512 * m + 128 * Sp[a]
                y_prompt[b, st:st + 128] = R[c]["yown"][m, a * 128:(a + 1) * 128]
    new_cmp_p = np.stack([R[2 * b]["okv0"].reshape(T, 2, 4, 64) for b in range(4)])[None]
    new_sel_p = np.stack([R[2 * b]["okv1"].reshape(T, 2, 4, 64) for b in range(4)])[None]
    new_win_p = np.stack([R[2 * b]["owin"].reshape(512, 2, 4, 64) for b in range(4)])[None]
    new_conv_p = np.stack([R[2 * b]["oconv"] for b in range(4)])[None]
    z = lambda *s: np.zeros(s, np.float32)
    if not CFG.get("sample", True):
        return (y_prompt, z(32, 4, D), new_cmp_p, new_sel_p, new_win_p, new_conv_p,
                z(1, 32, 4, 2, 4, 64), z(1, 32, 4, 2, 4, 64), z(1, 32, 512, 2, 4, 64), z(1, 32, 2, 1024))
    y_sample = np.concatenate([R[c]["ys"].reshape(4, 4, D) for c in range(8)], axis=0)
    new_cmp_s = np.concatenate([R[c]["oskv0"].reshape(4, 4, 2, 4, 64) for c in range(8)], axis=0)[None]
    new_sel_s = np.concatenate([R[c]["oskv1"].reshape(4, 4, 2, 4, 64) for c in range(8)], axis=0)[None]
    new_win_s = np.concatenate([R[c]["owin_s"].reshape(4, 512, 2, 4, 64) for c in range(8)], axis=0)[None]
    new_conv_s = np.concatenate([R[c]["oconv_s"] for c in range(8)], axis=0)[None]
    return (y_prompt, y_sample, new_cmp_p, new_sel_p, new_win_p, new_conv_p, new_cmp_s, new_sel_s, new_win_s, new_conv_s)
```
